# Optimizing a Trainium2 kernel written in Bass

```python
import math
import jax, jax.numpy as jnp
from jax import lax
import numpy as np

D_MODEL = 1024
BATCH = 8
SEQ = 2048
DEPTH = 2
DEC_BATCH = 128
DEC_SEQ = 4
PAST_LEN = 16384
PAGE_SIZE = 128

D_MIX = D_MODEL
SSD_WIDTH = D_MIX // 2
SSD_HEAD_DIM = 64
SSD_HEADS = SSD_WIDTH // SSD_HEAD_DIM
SSD_GROUPS = 2
SSD_STATE = 128
SSD_CONV = 4
SSD_CHUNK = 64
SSD_CONV_DIM = SSD_WIDTH + 2 * SSD_GROUPS * SSD_STATE
RWKV_WIDTH = D_MIX // 4
RWKV_HEAD_DIM = 64
RWKV_HEADS = RWKV_WIDTH // RWKV_HEAD_DIM
RWKV_DECAY_LORA = 64
RWKV_A_LORA = 64
RWKV_GATE_LORA = 128
RWKV_PROJ = 3 * RWKV_WIDTH + RWKV_DECAY_LORA + RWKV_A_LORA + RWKV_GATE_LORA
RWKV_LN_EPS = 64e-5
S5_WIDTH = D_MIX - SSD_WIDTH - RWKV_WIDTH
S5_GROUP_CH = 16
S5_GROUPS = S5_WIDTH // S5_GROUP_CH
S5_STATE = 64
D_FF = 4 * D_MODEL
NORM_EPS = 1e-6
IN_SPLITS = (SSD_WIDTH, SSD_WIDTH + SSD_CONV_DIM, SSD_WIDTH + SSD_CONV_DIM + SSD_HEADS, SSD_WIDTH + SSD_CONV_DIM + SSD_HEADS + RWKV_PROJ)
IN_COLS = IN_SPLITS[-1] + S5_WIDTH

kernel_name = 'hybrid_ssd_rwkv7_s5_adaln_step'


def _rmsnorm(x, g):
    xf = x.astype(jnp.float32)
    y = xf * lax.rsqrt(jnp.mean(jnp.square(xf), axis=-1, keepdims=True) + NORM_EPS)
    return (y * g.astype(jnp.float32)).astype(x.dtype)


def _causal_conv(u, buf, w, b):
    seqlen = u.shape[1]
    full = jnp.concatenate([buf.astype(u.dtype), u], axis=1)
    out = b + sum(full[:, j:j + seqlen] * w[j] for j in range(SSD_CONV))
    return out, full[:, seqlen:]


def _segsum(x):
    T = x.shape[-1]
    xr = jnp.broadcast_to(x[..., :, None], x.shape + (T,))
    strict = jnp.tril(jnp.ones((T, T), bool), -1)
    ss = jnp.cumsum(jnp.where(strict, xr, 0.0), axis=-2)
    return jnp.where(jnp.tril(jnp.ones((T, T), bool)), ss, -jnp.inf)


def _ssd_chunked(xdt, a, bm, cm, h0):
    bsz, seqlen, nh, hp = xdt.shape
    ng, ns = bm.shape[2], bm.shape[3]
    rg = nh // ng
    T = math.gcd(seqlen, SSD_CHUNK)
    nc = seqlen // T
    x = xdt.reshape(bsz, nc, T, ng, rg, hp)
    a = a.reshape(bsz, nc, T, ng, rg).transpose(0, 3, 4, 1, 2)
    bm = bm.reshape(bsz, nc, T, ng, ns)
    cm = cm.reshape(bsz, nc, T, ng, ns)
    a_cs = jnp.cumsum(a, axis=-1)
    decay_in = jnp.exp(_segsum(a))
    cb = jnp.einsum('bclgn,bcsgn->bgcls', cm, bm)
    y_diag = jnp.einsum('bgcls,bgrcls,bcsgrp->bclgrp', cb, decay_in, x)
    decay_to_end = jnp.exp(a_cs[..., -1:] - a_cs)
    chunk_states = jnp.einsum('bcsgn,bgrcs,bcsgrp->bcgrpn', bm, decay_to_end, x)
    states = jnp.concatenate([h0.reshape(bsz, 1, ng, rg, hp, ns), chunk_states], axis=1)
    chunk_tot = jnp.pad(a_cs[..., -1], ((0, 0), (0, 0), (0, 0), (1, 0)))
    decay_chunk = jnp.exp(_segsum(chunk_tot))
    states = jnp.einsum('bgrzc,bcgrpn->bzgrpn', decay_chunk, states)
    enter, final = states[:, :-1], states[:, -1]
    y_off = jnp.einsum('bclgn,bcgrpn,bgrcl->bclgrp', cm, enter, jnp.exp(a_cs))
    y = (y_diag + y_off).reshape(bsz, seqlen, nh, hp)
    return y, final.reshape(bsz, nh, hp, ns)


def _ssd_mixer(z, xbc, dt, st_ssd, st_conv, l, P):
    f32 = jnp.float32
    bsz, seqlen = z.shape[:2]
    xbc, new_conv = _causal_conv(xbc.astype(f32), st_conv, P['ssd_conv_w'][l], P['ssd_conv_b'][l])
    xbc = jax.nn.silu(xbc)
    xs, bm, cm = jnp.split(xbc, [SSD_WIDTH, SSD_WIDTH + SSD_GROUPS * SSD_STATE], axis=-1)
    dt = jax.nn.softplus(dt.astype(f32) + P['ssd_dt_bias'][l])
    a = -jnp.exp(P['ssd_a_log'][l].astype(f32))
    xh = xs.reshape(bsz, seqlen, SSD_HEADS, SSD_HEAD_DIM)
    y, new_state = _ssd_chunked(xh * dt[..., None], dt * a,
                                bm.reshape(bsz, seqlen, SSD_GROUPS, SSD_STATE),
                                cm.reshape(bsz, seqlen, SSD_GROUPS, SSD_STATE),
                                st_ssd.astype(f32))
    y = y + xh * P['ssd_d'][l][:, None]
    y = y.reshape(bsz, seqlen, SSD_WIDTH) * jax.nn.silu(z.astype(f32))
    y = _rmsnorm(y, P['ssd_norm_g'][l])
    return y, new_state.astype(st_ssd.dtype), new_conv.astype(st_conv.dtype)


def _rwkv_step(S, inp):
    r_t, w_t, k_t, v_t, a_t, b_t = inp
    sa = jnp.einsum('bhvk,bhk->bhv', S, a_t)
    S = S * w_t[:, :, None, :] + sa[..., None] * b_t[:, :, None, :] + v_t[..., None] * k_t[:, :, None, :]
    return S, jnp.einsum('bhvk,bhk->bhv', S, r_t)


def _rwkv_mixer(f, st_rwkv, st_shift, l, P):
    f32 = jnp.float32
    f = f.astype(f32)
    bsz, seqlen = f.shape[:2]
    prev = jnp.concatenate([st_shift.astype(f32)[:, None], f[:, :-1]], axis=1)
    fm = f + (prev - f) * P['rwkv_mu'][l]
    r, k, v, wl, al, gl = jnp.split(fm, [RWKV_WIDTH, 2 * RWKV_WIDTH, 3 * RWKV_WIDTH,
                                         3 * RWKV_WIDTH + RWKV_DECAY_LORA,
                                         3 * RWKV_WIDTH + RWKV_DECAY_LORA + RWKV_A_LORA], axis=-1)
    w_log = -jax.nn.softplus(-(P['rwkv_w0'][l] + jnp.tanh(wl) @ P['rwkv_w2'][l])) - 0.5
    decay = jnp.exp(-jnp.exp(w_log))
    a = jax.nn.sigmoid(P['rwkv_a0'][l] + al @ P['rwkv_a2'][l])
    g = jax.nn.sigmoid(gl) @ P['rwkv_g2'][l]
    hs = lambda t: t.reshape(bsz, seqlen, RWKV_HEADS, RWKV_HEAD_DIM)
    r, k, v, a, decay = hs(r), hs(k), hs(v), hs(a), hs(decay)
    k_k = P['rwkv_k_k'][l].reshape(RWKV_HEADS, RWKV_HEAD_DIM)
    k_a = P['rwkv_k_a'][l].reshape(RWKV_HEADS, RWKV_HEAD_DIM)
    kk = k * k_k
    kk = kk * lax.rsqrt(jnp.maximum(jnp.sum(jnp.square(kk), axis=-1, keepdims=True), 1e-24))
    k = k * (1.0 + (a - 1.0) * k_a)
    seq_first = lambda t: jnp.moveaxis(t, 1, 0)
    s_last, o = lax.scan(_rwkv_step, st_rwkv.astype(f32),
                         (seq_first(r), seq_first(decay), seq_first(k), seq_first(v),
                          seq_first(-kk), seq_first(kk * a)))
    o = jnp.moveaxis(o, 0, 1)
    mu = jnp.mean(o, axis=-1, keepdims=True)
    var = jnp.mean(jnp.square(o - mu), axis=-1, keepdims=True)
    o = ((o - mu) * lax.rsqrt(var + RWKV_LN_EPS)).reshape(bsz, seqlen, RWKV_WIDTH)
    o = o * P['rwkv_ln_g'][l] + P['rwkv_ln_b'][l]
    bonus = jnp.sum(r * k * P['rwkv_r_k'][l], axis=-1, keepdims=True) * v
    o = o + bonus.reshape(bsz, seqlen, RWKV_WIDTH)
    return o * g, s_last.astype(st_rwkv.dtype), f[:, -1].astype(st_shift.dtype)


def _complex_affine_combine(e1, e2):
    a1r, a1i, b1r, b1i = e1
    a2r, a2i, b2r, b2i = e2
    return (a2r * a1r - a2i * a1i, a2r * a1i + a2i * a1r,
            a2r * b1r - a2i * b1i + b2r, a2r * b1i + a2i * b1r + b2i)


def _s5_mixer(u, st_re, st_im, l, P):
    f32 = jnp.float32
    u = u.astype(f32)
    bsz, seqlen = u.shape[:2]
    ug = u.reshape(bsz, seqlen, S5_GROUPS, S5_GROUP_CH)
    lre = P['s5_a_re'][l].astype(f32)
    lim = P['s5_a_im'][l].astype(f32)
    dt = jnp.exp(P['s5_log_dt'][l].astype(f32))[:, None]
    mag = jnp.exp(lre * dt)
    ang = lim * dt
    ab_re, ab_im = mag * jnp.cos(ang), mag * jnp.sin(ang)
    den = jnp.square(lre) + jnp.square(lim)
    q_re = ((ab_re - 1.0) * lre + ab_im * lim) / den
    q_im = (ab_im * lre - (ab_re - 1.0) * lim) / den
    b_re, b_im = P['s5_b_re'][l], P['s5_b_im'][l]
    bb_re = q_re[..., None] * b_re - q_im[..., None] * b_im
    bb_im = q_re[..., None] * b_im + q_im[..., None] * b_re
    bu_re = jnp.einsum('blgc,gpc->blgp', ug, bb_re)
    bu_im = jnp.einsum('blgc,gpc->blgp', ug, bb_im)
    s_re, s_im = st_re.astype(f32), st_im.astype(f32)
    bu_re = bu_re.at[:, 0].add(ab_re * s_re - ab_im * s_im)
    bu_im = bu_im.at[:, 0].add(ab_re * s_im + ab_im * s_re)
    _, _, h_re, h_im = lax.associative_scan(
        _complex_affine_combine,
        (jnp.broadcast_to(ab_re, bu_re.shape), jnp.broadcast_to(ab_im, bu_re.shape), bu_re, bu_im), axis=1)
    y = (jnp.einsum('gcp,blgp->blgc', P['s5_c_re'][l], h_re)
         - jnp.einsum('gcp,blgp->blgc', P['s5_c_im'][l], h_im))
    y = y.reshape(bsz, seqlen, S5_WIDTH) + P['s5_d'][l] * u
    y = jax.nn.gelu(y)
    y = y * jax.nn.sigmoid(y @ P['s5_glu_w'][l] + P['s5_glu_b'][l])
    return y, h_re[:, -1].astype(st_re.dtype), h_im[:, -1].astype(st_im.dtype)


def _layer(x, c, l, st_ssd, st_conv, st_rwkv, st_shift, st_re, st_im, P):
    mod = jax.nn.silu(c) @ P['ada_w'][l] + P['ada_b'][l]
    sh1, sc1, g1, sh2, sc2, g2 = jnp.split(mod[:, None, :], 6, axis=-1)
    h = _rmsnorm(x, P['norm1_g'][l]) * (1.0 + sc1) + sh1
    proj = h @ P['w_in'][l]
    z, xbc, dt, frw, us5 = jnp.split(proj, list(IN_SPLITS), axis=-1)
    y_ssd, n_ssd, n_conv = _ssd_mixer(z, xbc, dt, st_ssd, st_conv, l, P)
    y_rw, n_rwkv, n_shift = _rwkv_mixer(frw, st_rwkv, st_shift, l, P)
    y_s5, n_re, n_im = _s5_mixer(us5, st_re, st_im, l, P)
    mix = jnp.concatenate([y_ssd, y_rw.astype(x.dtype), y_s5.astype(x.dtype)], axis=-1).astype(x.dtype) @ P['w_out'][l]
    x = x + g1 * mix
    h2 = _rmsnorm(x, P['norm2_g'][l]) * (1.0 + sc2) + sh2
    ff = jnp.square(jax.nn.relu(h2 @ P['mlp_w1'][l])) @ P['mlp_w2'][l]
    x = x + g2 * ff
    return x, (n_ssd, n_conv, n_rwkv, n_shift, n_re, n_im)


def _trunk(x, c, st_ssd, st_conv, st_rwkv, st_shift, st_re, st_im, P):
    new = ([], [], [], [], [], [])
    for l in range(DEPTH):
        x, states = _layer(x, c, l, st_ssd[l], st_conv[l], st_rwkv[l], st_shift[l], st_re[l], st_im[l], P)
        for lst, s in zip(new, states):
            lst.append(s)
    y = _rmsnorm(x, P['final_g'])
    return y, [jnp.stack(lst) for lst in new]


def setup_inputs(seed: int = 0) -> dict:
    key = jax.random.key(seed)
    ks = iter(jax.random.split(key, 64))

    def nrm(shape, scale):
        return scale * jax.random.normal(next(ks), shape, jnp.float32)

    def unif(shape, lo, hi):
        return jax.random.uniform(next(ks), shape, jnp.float32, lo, hi)

    L = DEPTH
    dt0 = jnp.exp(unif((L, SSD_HEADS), math.log(1e-3), math.log(1e-1)))
    return {
        'x_prompt': nrm((BATCH, SEQ, D_MODEL), 1.0),
        'x_sample': nrm((DEC_BATCH, DEC_SEQ, D_MODEL), 1.0),
        'c_prompt': nrm((BATCH, D_MODEL), 1.0),
        'c_sample': nrm((DEC_BATCH, D_MODEL), 1.0),
        'state_ssd': nrm((L, DEC_BATCH, SSD_HEADS, SSD_HEAD_DIM, SSD_STATE), 0.1),
        'state_ssd_conv': nrm((L, DEC_BATCH, SSD_CONV - 1, SSD_CONV_DIM), 1.0),
        'state_rwkv': nrm((L, DEC_BATCH, RWKV_HEADS, RWKV_HEAD_DIM, RWKV_HEAD_DIM), 0.1),
        'state_rwkv_shift': nrm((L, DEC_BATCH, RWKV_PROJ), 1.0),
        'state_s5_re': nrm((L, DEC_BATCH, S5_GROUPS, S5_STATE), 0.05),
        'state_s5_im': nrm((L, DEC_BATCH, S5_GROUPS, S5_STATE), 0.05),
        'ada_w': nrm((L, D_MODEL, 6 * D_MODEL), 0.5 * D_MODEL ** -0.5),
        'ada_b': nrm((L, 6 * D_MODEL), 0.01),
        'norm1_g': 1.0 + nrm((L, D_MODEL), 0.1),
        'norm2_g': 1.0 + nrm((L, D_MODEL), 0.1),
        'w_in': nrm((L, D_MODEL, IN_COLS), D_MODEL ** -0.5),
        'ssd_conv_w': nrm((L, SSD_CONV, SSD_CONV_DIM), 0.5),
        'ssd_conv_b': nrm((L, SSD_CONV_DIM), 0.01),
        'ssd_dt_bias': dt0 + jnp.log(-jnp.expm1(-dt0)),
        'ssd_a_log': jnp.log(unif((L, SSD_HEADS), 1.0, 16.0)),
        'ssd_d': 1.0 + nrm((L, SSD_HEADS), 0.1),
        'ssd_norm_g': 1.0 + nrm((L, SSD_WIDTH), 0.1),
        'rwkv_mu': unif((L, RWKV_PROJ), 0.0, 1.0),
        'rwkv_w0': unif((L, RWKV_WIDTH), -6.0, 0.0),
        'rwkv_w2': nrm((L, RWKV_DECAY_LORA, RWKV_WIDTH), 0.1),
        'rwkv_a0': nrm((L, RWKV_WIDTH), 0.1),
        'rwkv_a2': nrm((L, RWKV_A_LORA, RWKV_WIDTH), 0.5 * RWKV_A_LORA ** -0.5),
        'rwkv_g2': nrm((L, RWKV_GATE_LORA, RWKV_WIDTH), RWKV_GATE_LORA ** -0.5),
        'rwkv_k_k': 1.0 + nrm((L, RWKV_WIDTH), 0.1),
        'rwkv_k_a': 1.0 + nrm((L, RWKV_WIDTH), 0.1),
        'rwkv_r_k': nrm((L, RWKV_HEADS, RWKV_HEAD_DIM), 0.1),
        'rwkv_ln_g': 1.0 + nrm((L, RWKV_WIDTH), 0.1),
        'rwkv_ln_b': nrm((L, RWKV_WIDTH), 0.01),
        's5_a_re': -0.5 + nrm((L, S5_GROUPS, S5_STATE), 0.01),
        's5_a_im': jnp.pi * jnp.arange(S5_STATE, dtype=jnp.float32) + nrm((L, S5_GROUPS, S5_STATE), 0.01),
        's5_log_dt': unif((L, S5_GROUPS), math.log(1e-3), math.log(1e-1)),
        's5_b_re': nrm((L, S5_GROUPS, S5_STATE, S5_GROUP_CH), (2 * S5_GROUP_CH) ** -0.5),
        's5_b_im': nrm((L, S5_GROUPS, S5_STATE, S5_GROUP_CH), (2 * S5_GROUP_CH) ** -0.5),
        's5_c_re': nrm((L, S5_GROUPS, S5_GROUP_CH, S5_STATE), 0.25),
        's5_c_im': nrm((L, S5_GROUPS, S5_GROUP_CH, S5_STATE), 0.25),
        's5_d': nrm((L, S5_WIDTH), 1.0),
        's5_glu_w': nrm((L, S5_WIDTH, S5_WIDTH), S5_WIDTH ** -0.5),
        's5_glu_b': nrm((L, S5_WIDTH), 0.01),
        'w_out': nrm((L, D_MIX, D_MODEL), D_MIX ** -0.5),
        'mlp_w1': nrm((L, D_MODEL, D_FF), D_MODEL ** -0.5),
        'mlp_w2': nrm((L, D_FF, D_MODEL), D_FF ** -0.5),
        'final_g': 1.0 + nrm((D_MODEL,), 0.1),
    }


def reference(x_prompt, x_sample, c_prompt, c_sample, state_ssd, state_ssd_conv, state_rwkv, state_rwkv_shift,
              state_s5_re, state_s5_im, ada_w, ada_b, norm1_g, norm2_g, w_in, ssd_conv_w, ssd_conv_b, ssd_dt_bias,
              ssd_a_log, ssd_d, ssd_norm_g, rwkv_mu, rwkv_w0, rwkv_w2, rwkv_a0, rwkv_a2, rwkv_g2, rwkv_k_k, rwkv_k_a,
              rwkv_r_k, rwkv_ln_g, rwkv_ln_b, s5_a_re, s5_a_im, s5_log_dt, s5_b_re, s5_b_im, s5_c_re, s5_c_im, s5_d,
              s5_glu_w, s5_glu_b, w_out, mlp_w1, mlp_w2, final_g):
    P = dict(ada_w=ada_w, ada_b=ada_b, norm1_g=norm1_g, norm2_g=norm2_g, w_in=w_in,
             ssd_conv_w=ssd_conv_w, ssd_conv_b=ssd_conv_b, ssd_dt_bias=ssd_dt_bias, ssd_a_log=ssd_a_log,
             ssd_d=ssd_d, ssd_norm_g=ssd_norm_g, rwkv_mu=rwkv_mu, rwkv_w0=rwkv_w0, rwkv_w2=rwkv_w2,
             rwkv_a0=rwkv_a0, rwkv_a2=rwkv_a2, rwkv_g2=rwkv_g2, rwkv_k_k=rwkv_k_k, rwkv_k_a=rwkv_k_a,
             rwkv_r_k=rwkv_r_k, rwkv_ln_g=rwkv_ln_g, rwkv_ln_b=rwkv_ln_b, s5_a_re=s5_a_re, s5_a_im=s5_a_im,
             s5_log_dt=s5_log_dt, s5_b_re=s5_b_re, s5_b_im=s5_b_im, s5_c_re=s5_c_re, s5_c_im=s5_c_im,
             s5_d=s5_d, s5_glu_w=s5_glu_w, s5_glu_b=s5_glu_b, w_out=w_out, mlp_w1=mlp_w1, mlp_w2=mlp_w2,
             final_g=final_g)
    bp = x_prompt.shape[0]
    zeros = lambda shape: jnp.zeros((DEPTH, bp) + shape, x_prompt.dtype)
    y_prompt, sp = _trunk(x_prompt, c_prompt,
                          zeros((SSD_HEADS, SSD_HEAD_DIM, SSD_STATE)), zeros((SSD_CONV - 1, SSD_CONV_DIM)),
                          zeros((RWKV_HEADS, RWKV_HEAD_DIM, RWKV_HEAD_DIM)), zeros((RWKV_PROJ,)),
                          zeros((S5_GROUPS, S5_STATE)), zeros((S5_GROUPS, S5_STATE)), P)
    y_sample, ss = _trunk(x_sample, c_sample, state_ssd, state_ssd_conv, state_rwkv, state_rwkv_shift,
                          state_s5_re, state_s5_im, P)
    p_ssd, p_conv, p_rwkv, p_shift, p_s5_re, p_s5_im = sp
    s_ssd, s_conv, s_rwkv, s_shift, s_s5_re, s_s5_im = ss
    return (y_prompt, y_sample, p_ssd, p_conv, p_rwkv, p_shift, p_s5_re, p_s5_im,
            s_ssd, s_conv, s_rwkv, s_shift, s_s5_re, s_s5_im)
```

```python
import numpy as np
from contextlib import ExitStack
import concourse.bass as bass
import concourse.mybir as mybir
from concourse.bass_utils import run_bass_kernel_spmd

F32 = mybir.dt.float32
BF16 = mybir.dt.bfloat16
F32R = mybir.dt.float32r
AF = mybir.ActivationFunctionType
ALU = mybir.AluOpType
AX = mybir.AxisListType
ENGS = ("sync", "gpsimd", "scalar", "vector", "tensor")
BLK = 64


class V:
    def __init__(self, ap, res):
        self.ap = ap
        self.res = res


def _norm(x):
    if isinstance(x, V):
        return x.ap, list(x.res)
    return x, [(x.name, None)]


def _ap(x):
    return x.ap if isinstance(x, V) else x


def R(x):
    if isinstance(x, V):
        return V(x.ap.bitcast(F32R), x.res)
    return V(x.bitcast(F32R), [(x.name, None)])


FOLD_WAITS = True
POOL_TO_DVE = True
ROUNDED_TENSORS = {"scrR", "cbr", "BBt0", "BBt1", "Cbd0", "Cbd1", "gluw", "rwH_P"}


class Prog:
    NDMASEM = 8

    def __init__(self, nc):
        self.nc = nc
        self.es = ExitStack()
        self.ops = []
        self.state = {}
        self.banks = []
        self.bi = 0

    def sb(self, name, shape, dtype=F32):
        return self.es.enter_context(self.nc.sbuf_tensor(name, list(shape), dtype))

    def ps(self, name, shape, dtype=F32):
        return self.es.enter_context(self.nc.psum_tensor(name, list(shape), dtype))

    def bank(self, hold=False):
        if not self.banks:
            self.banks = [self.ps("psb%d" % i, [128, 512]) for i in range(8)]
            self.held = set()
        while (self.bi % 8) in self.held:
            self.bi += 1
        k = self.bi % 8
        self.bi += 1
        if hold:
            self.held.add(k)
        return self.banks[k]

    def unhold(self, b):
        self.held.discard(self.banks.index(b))

    def op(self, eng, fn, reads=(), writes=(), dma=False):
        if eng == "gpsimd" and not dma and POOL_TO_DVE:
            eng = "vector"
        rr = []
        for x in reads:
            rr += _norm(x)[1]
        ww = []
        for x in writes:
            rs_ = _norm(x)[1]
            if rs_ and rs_[0][0] in ROUNDED_TENSORS:
                assert _ap(x).dtype == F32R, ("non-rounded write into fp32r arena", rs_[0])
            ww += rs_
        deps = set()

        def entries(name, key):
            d = self.state.setdefault(name, {})
            if key is None:
                return list(d.values())
            out = []
            if key in d:
                out.append(d[key])
            if None in d:
                out.append(d[None])
            return out

        for (n, k) in rr:
            for ent in entries(n, k):
                if ent[0] is not None:
                    deps.add((ent[0], "raw"))
                if n.startswith("psb"):
                    for r in ent[1]:
                        deps.add((r, "rar"))
        for (n, k) in ww:
            for ent in entries(n, k):
                if ent[0] is not None:
                    deps.add((ent[0], "waw"))
                for r in ent[1]:
                    deps.add((r, "war"))
        idx = len(self.ops)
        self.ops.append(dict(eng=eng, fn=fn, deps=deps, dma=dma))
        for (n, k) in rr:
            d = self.state.setdefault(n, {})
            if k is None:
                for ent in d.values():
                    ent[1].append(idx)
                if None not in d:
                    d[None] = [None, [idx]]
            else:
                d.setdefault(k, [None, []])[1].append(idx)
        for (n, k) in ww:
            d = self.state.setdefault(n, {})
            if k is None:
                d.clear()
                d[None] = [idx, []]
            else:
                if None in d:
                    pass
                d[k] = [idx, []]
        return idx

    def emit(self):
        nc = self.nc
        ops = self.ops
        es = self.es
        esem = {e: es.enter_context(nc.semaphore("s_" + e)) for e in ENGS}
        dsem = {e: [es.enter_context(nc.semaphore("d_%s%d" % (e, i))) for i in range(self.NDMASEM)]
                for e in ("sync", "gpsimd", "scalar")}
        eff = []
        for i, o in enumerate(ops):
            e = o["eng"]
            ds = set()
            for (d, kind) in o["deps"]:
                od = ops[d]
                if od["eng"] == e and not od["dma"] and not o["dma"]:
                    if e == "tensor" or kind == "rar":
                        continue
                ds.add(d)
            latest = {}
            keep = set()
            for d in ds:
                od = ops[d]
                if od["dma"]:
                    keep.add(d)
                elif latest.get(od["eng"], -1) < d:
                    latest[od["eng"]] = d
            keep.update(latest.values())
            eff.append(keep)
        signal = [False] * len(ops)
        for ds in eff:
            for d in ds:
                signal[d] = True
        ecount = {e: 0 for e in ENGS}
        dcount = {e: [0] * self.NDMASEM for e in dsem}
        drr = {e: 0 for e in dsem}
        for i, o in enumerate(ops):
            e = o["eng"]
            if o["dma"]:
                j = drr[e] % self.NDMASEM
                drr[e] += 1
                o["prev_dma_val"] = dcount[e][j]
                dcount[e][j] += 16
                o["sem"] = dsem[e][j]
                o["val"] = dcount[e][j]
                o["semkey"] = ("d", e, j)
            elif signal[i]:
                ecount[e] += 1
                o["sem"] = esem[e]
                o["val"] = ecount[e]
                o["semkey"] = ("e", e)
        semobj = {("e", e): esem[e] for e in ENGS}
        for e in dsem:
            for j in range(self.NDMASEM):
                semobj[("d", e, j)] = dsem[e][j]
        eclock = {e: {} for e in ENGS}
        opclock = {}
        waits = [None] * len(ops)
        self.nwaits = 0
        for i, o in enumerate(ops):
            e = o["eng"]
            ck = eclock[e]
            need = {}
            if o["dma"] and o["prev_dma_val"] > 0:
                need[o["semkey"]] = o["prev_dma_val"]
            for d in eff[i]:
                od = ops[d]
                sk = od["semkey"]
                if need.get(sk, 0) < od["val"]:
                    need[sk] = od["val"]
            todo = []
            for sk, v in need.items():
                if ck.get(sk, 0) < v:
                    todo.append((sk, v))
            for d in eff[i]:
                oc = opclock.get(d)
                if oc:
                    for sk, v in oc.items():
                        if ck.get(sk, 0) < v:
                            ck[sk] = v
            for sk, v in need.items():
                if ck.get(sk, 0) < v:
                    ck[sk] = v
            todo2 = []
            for (sk, v) in todo:
                implied = False
                for d in eff[i]:
                    od = ops[d]
                    if od["semkey"] == sk:
                        continue
                    oc = opclock.get(d)
                    if oc and oc.get(sk, 0) >= v and any(t[0] == od["semkey"] for t in todo):
                        implied = True
                        break
                if not implied:
                    todo2.append((sk, v))
            waits[i] = todo2
            self.nwaits += len(todo2)
            if o["dma"] or signal[i]:
                oc = dict(ck)
                oc[o["semkey"]] = max(oc.get(o["semkey"], 0), o["val"])
                opclock[i] = oc
        per = {e: [i for i, o in enumerate(ops) if o["eng"] == e] for e in ENGS}
        finals = [(dsem[e][j], dcount[e][j]) for e in dsem for j in range(self.NDMASEM) if dcount[e][j] > 0]
        self.ecount = ecount

        def run(e, eng):
            for i in per[e]:
                o = ops[i]
                todo = waits[i]
                fold = FOLD_WAITS and (not o["dma"]) and len(todo) > 0
                for (sk, v) in (todo[:-1] if fold else todo):
                    eng.wait_ge(semobj[sk], v)
                ins = o["fn"](eng)
                if fold:
                    sk, v = todo[-1]
                    ins._wait_ge(semobj[sk], v)
                if o["dma"]:
                    ins.then_inc(o["sem"], 16)
                elif signal[i]:
                    ins.then_inc(o["sem"], 1)
            if e == "sync":
                for (s_, v) in finals:
                    eng.wait_ge(s_, v)

        with nc.allow_non_contiguous_dma(reason="tiny constant/state gathers"), nc.Block() as block:
            @block.sync
            def _(eng):
                run("sync", eng)

            @block.gpsimd
            def _(eng):
                run("gpsimd", eng)

            @block.scalar
            def _(eng):
                run("scalar", eng)

            @block.vector
            def _(eng):
                run("vector", eng)

            @block.tensor
            def _(eng):
                run("tensor", eng)
        es.close()

    def dma(self, out, in_, q="sync"):
        o, i = _ap(out), _ap(in_)
        return self.op(q, lambda e: e.dma_start(out=o, in_=i), reads=[in_], writes=[out], dma=True)

    def mm(self, out, lhsT, rhs, start=True, stop=True, r=False):
        if r:
            lhsT, rhs = R(lhsT), R(rhs)
        o, a, b = _ap(out), _ap(lhsT), _ap(rhs)
        return self.op("tensor", lambda e: e.matmul(o, a, b, start=start, stop=stop),
                       reads=[lhsT, rhs], writes=[out])

    def tr(self, out, in_, ident):
        o, a, b = _ap(out), _ap(in_), _ap(ident)
        return self.op("tensor", lambda e: e.transpose(o, a, b), reads=[in_, ident], writes=[out])

    def act(self, out, in_, func, bias=None, scale=None):
        o, a = _ap(out), _ap(in_)
        kw = {}
        rd = [in_]
        if bias is not None:
            if isinstance(bias, (int, float)):
                kw["bias"] = float(bias)
            else:
                kw["bias"] = _ap(bias)
                rd.append(bias)
        if scale is not None:
            if isinstance(scale, (int, float)):
                kw["scale"] = float(scale)
            else:
                kw["scale"] = _ap(scale)
                rd.append(scale)
        return self.op("scalar", lambda e: e.activation(out=o, in_=a, func=func, **kw), reads=rd, writes=[out])

    def tt(self, out, in0, in1, op, eng="vector"):
        o, a, b = _ap(out), _ap(in0), _ap(in1)
        return self.op(eng, lambda e: e.tensor_tensor(out=o, in0=a, in1=b, op=op), reads=[in0, in1], writes=[out])

    def ts(self, out, in0, s1, op0, s2=None, op1=None, eng="vector"):
        o, a = _ap(out), _ap(in0)
        rd = [in0]
        x1 = s1
        if not isinstance(s1, (int, float)):
            rd.append(s1)
            x1 = _ap(s1)
        x2 = s2
        if s2 is not None and not isinstance(s2, (int, float)):
            rd.append(s2)
            x2 = _ap(s2)
        if op1 is None:
            fn = lambda e: e.tensor_scalar(out=o, in0=a, scalar1=x1, scalar2=None, op0=op0)
        else:
            fn = lambda e: e.tensor_scalar(out=o, in0=a, scalar1=x1, scalar2=x2, op0=op0, op1=op1)
        return self.op(eng, fn, reads=rd, writes=[out])

    def stt(self, out, in0, scalar, in1, op0, op1):
        o, a, b = _ap(out), _ap(in0), _ap(in1)
        rd = [in0, in1]
        s = scalar
        if not isinstance(scalar, (int, float)):
            rd.append(scalar)
            s = _ap(scalar)
        return self.op("vector", lambda e: e.scalar_tensor_tensor(out=o, in0=a, scalar=s, in1=b, op0=op0, op1=op1),
                       reads=rd, writes=[out])

    def copy(self, out, in_, eng="vector"):
        if eng == "scalar":
            return self.act(out, in_, AF.Copy)
        o, a = _ap(out), _ap(in_)
        return self.op(eng, lambda e: e.tensor_copy(out=o, in_=a), reads=[in_], writes=[out])

    def memset(self, out, val, eng="vector"):
        o = _ap(out)
        return self.op(eng, lambda e: e.memset(o, val), reads=[], writes=[out])

    def rsqrt(self, out, in_, scale=1.0, bias=0.0):
        self.act(out, in_, AF.Ln, bias=bias, scale=scale)
        self.act(out, out, AF.Exp, scale=-0.5)

    def recip(self, out, in_):
        o, a = _ap(out), _ap(in_)
        return self.op("vector", lambda e: e.reciprocal(out=o, in_=a), reads=[in_], writes=[out])

    def reduce(self, out, in_, op=ALU.add):
        o, a = _ap(out), _ap(in_)
        return self.op("vector", lambda e: e.tensor_reduce(out=o, in_=a, axis=AX.X, op=op), reads=[in_], writes=[out])

    def scan(self, out, d0, d1, initial, op0, op1):
        o, a, b = _ap(out), _ap(d0), _ap(d1)
        return self.op("vector", lambda e: e.tensor_tensor_scan(out=o, data0=a, data1=b, initial=initial, op0=op0, op1=op1),
                       reads=[d0, d1], writes=[out])


class Buf:
    def __init__(self, t, name, off, n):
        self.t = t
        self.off = off
        self.n = n
        self.res = [(name, k) for k in range(off // BLK, (off + n - 1) // BLK + 1)]

    def a(self, p0=0, p1=128, lo=0, hi=None):
        hi = self.n if hi is None else hi
        return self.t[p0:p1, self.off + lo:self.off + hi]

    def __call__(self, ap):
        return V(ap, self.res)

    def f(self, p0=0, p1=128, lo=0, hi=None):
        return V(self.a(p0, p1, lo, hi), self.res)


class Buf16(Buf):
    def __init__(self, t16, name, off32, n16):
        self.t16 = t16
        self.off = off32
        self.n = n16
        n32 = (n16 + 1) // 2
        self.res = [(name, k) for k in range(off32 // BLK, (off32 + n32 - 1) // BLK + 1)]

    def a(self, p0=0, p1=128, lo=0, hi=None):
        hi = self.n if hi is None else hi
        return self.t16[p0:p1, 2 * self.off + lo:2 * self.off + hi]


class Scr:
    def __init__(self, p, name, ncols):
        self.name = name
        self.t = p.sb(name, [128, ncols])
        self.t16 = self.t[:, :].bitcast(BF16)
        self.n = ncols
        self.top = 0
        self.peak = 0

    def alloc(self, ncols):
        off = self.top
        self.top += ((ncols + BLK - 1) // BLK) * BLK
        assert self.top <= self.n, ("scratch overflow", self.top, self.n)
        self.peak = max(self.peak, self.top)
        return Buf(self.t, self.name, off, ncols)

    def alloc16(self, n16):
        off = self.top
        n32 = (n16 + 1) // 2
        self.top += ((n32 + BLK - 1) // BLK) * BLK
        assert self.top <= self.n, ("scratch overflow", self.top, self.n)
        self.peak = max(self.peak, self.top)
        return Buf16(self.t16, self.name, off, n16)

    def mark(self):
        return self.top

    def release(self, m):
        self.top = m


def _const_blob():
    c = {}
    cols = []

    def add(name, arr):
        arr = np.asarray(arr, np.float32)
        full = np.zeros((128, arr.shape[1]), np.float32)
        full[:arr.shape[0]] = arr
        c[name] = (sum(a.shape[1] for a in cols), arr.shape[1])
        cols.append(full)

    i128 = np.arange(128)
    add("ident", np.eye(128))
    add("zeros", np.zeros((128, 512)))
    add("ones", np.ones((128, 128)))
    inclP = (i128[:, None] <= i128[None, :]).astype(np.float32)
    add("inclP", inclP)
    add("strictP", (i128[:, None] < i128[None, :]))
    add("strictPT", (i128[:, None] > i128[None, :]))
    add("mbP", np.where(inclP > 0, 0.0, -30000.0))
    add("sameP", np.ones((128, 128)))
    i64 = np.arange(64)
    same = (i64[:, None] // 4) == (i64[None, :] // 4)
    inclS = same & (i64[:, None] <= i64[None, :])
    add("inclS", inclS)
    add("strictS", same & (i64[:, None] < i64[None, :]))
    add("strictST", same & (i64[:, None] > i64[None, :]))
    add("mbS", np.where(inclS, 0.0, -30000.0))
    add("sameS", same)
    ind = (i64[:, None] // 4 == np.arange(16)[None, :]).astype(np.float32)
    add("indS", ind)
    add("indST", ind.T)
    add("indBC", np.broadcast_to(ind.T.reshape(1, 1024), (128, 1024)))
    add("restartS", np.broadcast_to((i64 % 4 != 0).astype(np.float32)[None, :], (128, 64)))
    add("onesrow", np.ones((128, 128)))
    part = np.arange(128)
    gl = part // 64
    msk = np.zeros((128, 8, 8), np.float32)
    for j in range(8):
        for g8 in range(8):
            msk[:, j, g8] = ((2 * j + gl) % 8 == g8)
    add("s5msk", msk.reshape(128, 64))
    mz = (part[:, None] // 16 == np.arange(8)[None, :]).astype(np.float32)
    add("s5mz", mz)
    blob = np.concatenate(cols, axis=1)
    return blob, c


_BLOB, _CO = _const_blob()
NCB = _BLOB.shape[1]

D = 1024
NORM_EPS = 1e-6
RWKV_LN_EPS = 64e-5
TWO_PI = 2.0 * np.pi

IN_NAMES = ["ada_w", "ada_b", "norm1_g", "norm2_g", "w_in", "ssd_conv_w", "ssd_conv_b", "ssd_dt_bias", "ssd_a_log",
            "ssd_d", "ssd_norm_g", "rwkv_mu", "rwkv_w0", "rwkv_w2", "rwkv_a0", "rwkv_a2", "rwkv_g2", "rwkv_k_k",
            "rwkv_k_a", "rwkv_r_k", "rwkv_ln_g", "rwkv_ln_b", "s5_a_re", "s5_a_im", "s5_log_dt", "s5_b_re", "s5_b_im",
            "s5_c_re", "s5_c_im", "s5_d", "s5_glu_w", "s5_glu_b", "w_out", "mlp_w1", "mlp_w2", "final_g"]
W_SHAPES = {
    "ada_w": (2, 1024, 6144), "ada_b": (2, 6144), "norm1_g": (2, 1024), "norm2_g": (2, 1024), "w_in": (2, 1024, 2824),
    "ssd_conv_w": (2, 4, 1024), "ssd_conv_b": (2, 1024), "ssd_dt_bias": (2, 8), "ssd_a_log": (2, 8), "ssd_d": (2, 8),
    "ssd_norm_g": (2, 512), "rwkv_mu": (2, 1024), "rwkv_w0": (2, 256), "rwkv_w2": (2, 64, 256), "rwkv_a0": (2, 256),
    "rwkv_a2": (2, 64, 256), "rwkv_g2": (2, 128, 256), "rwkv_k_k": (2, 256), "rwkv_k_a": (2, 256),
    "rwkv_r_k": (2, 4, 64), "rwkv_ln_g": (2, 256), "rwkv_ln_b": (2, 256), "s5_a_re": (2, 16, 64), "s5_a_im": (2, 16, 64),
    "s5_log_dt": (2, 16), "s5_b_re": (2, 16, 64, 16), "s5_b_im": (2, 16, 64, 16), "s5_c_re": (2, 16, 16, 64),
    "s5_c_im": (2, 16, 16, 64), "s5_d": (2, 256), "s5_glu_w": (2, 256, 256), "s5_glu_b": (2, 256),
    "w_out": (2, 1024, 1024), "mlp_w1": (2, 1024, 4096), "mlp_w2": (2, 4096, 1024), "final_g": (1024,),
}


def v3(ap, a):
    return ap.rearrange("p (a b) -> p a b", a=a)


def v4(ap, a, b):
    return ap.rearrange("p (a b c) -> p a b c", a=a, b=b)


def bc_last(ap, n):
    sh = list(ap.shape)
    return ap.unsqueeze(len(sh)).to_broadcast(sh + [n])


def bc_mid(ap, pos, n):
    sh = list(ap.shape)
    return ap.unsqueeze(pos).to_broadcast(sh[:pos] + [n] + sh[pos:])


def build(npt=8, depth=2, dbg=None):
    nc = bass.Bass("TRN2", target_bir_lowering=False)
    p = Prog(nc)
    SP = 256 * npt
    H = {}

    def din(name, shape):
        h = nc.dram_tensor(name, list(shape), F32, kind="ExternalInput")
        H[name] = h
        return h.ap()

    def dout(name, shape):
        h = nc.dram_tensor(name, list(shape), F32, kind="ExternalOutput")
        H[name] = h
        return h.ap()

    xp = din("xp", [SP, D])
    xs = din("xs", [64, D])
    cc = din("cc", [17, D])
    st_ssd = din("st_ssd", [2, 16, 512, 128])
    st_conv = din("st_conv", [2, 48, 1024])
    st_rwkv = din("st_rwkv", [2, 16, 256, 64])
    st_shift = din("st_shift", [2, 16, 1024])
    st_s5re = din("st_s5re", [2, 16, 1024])
    st_s5im = din("st_s5im", [2, 16, 1024])
    cblob = din("cblob", [128, NCB])
    Wt = {n: din(n, W_SHAPES[n]) for n in IN_NAMES}
    yp = dout("yp", [SP, D])
    ys = dout("ys", [64, D])
    o_ssd = {"P": dout("p_ssd", [2, 1, 512, 128]), "S": dout("s_ssd", [2, 16, 512, 128])}
    o_conv = {"P": dout("p_conv", [2, 3, 1024]), "S": dout("s_conv", [2, 48, 1024])}
    o_rwkv = {"P": dout("p_rwkv", [2, 1, 256, 64]), "S": dout("s_rwkv", [2, 16, 256, 64])}
    o_shift = {"P": dout("p_shift", [2, 1, 1024]), "S": dout("s_shift", [2, 16, 1024])}
    o_s5re = {"P": dout("p_s5re", [2, 1, 1024]), "S": dout("s_s5re", [2, 16, 1024])}
    o_s5im = {"P": dout("p_s5im", [2, 1, 1024]), "S": dout("s_s5im", [2, 16, 1024])}
    xmid = nc.dram_tensor("xmid", [128, 8, SP + 64], F32).ap()
    WB = {}
    for nm_ in ["ada_w", "w_in", "w_out", "mlp_w1", "mlp_w2"]:
        for l_ in range(depth):
            WB[(nm_, l_)] = nc.dram_tensor("wb_%s_%d" % (nm_, l_), list(W_SHAPES[nm_][1:]), BF16).ap()
    CAST_ROWS = {"ada_w": 128, "w_in": 256, "w_out": 512, "mlp_w1": 128, "mlp_w2": 512}
    cast_plan = {l_: [(nm_, r0) for nm_ in (["ada_w"] if l_ > 0 else []) + ["w_in", "w_out", "mlp_w1", "mlp_w2"]
                      for r0 in range(0, W_SHAPES[nm_][1], CAST_ROWS[nm_])] for l_ in range(depth)}

    def emit_casts(l_, n=None):
        k = 0
        while cast_plan[l_] and (n is None or k < n):
            nm_, r0 = cast_plan[l_].pop(0)
            r1 = r0 + CAST_ROWS[nm_]
            p.dma(WB[(nm_, l_)][r0:r1, :], Wt[nm_][l_, r0:r1, :], q="gpsimd")
            k += 1
    dbg_outs = {}

    cb = p.sb("cblob_sb", [128, NCB])
    p.dma(cb[:], cblob[:, :], q="gpsimd")

    def K(name, p0=0, p1=128, lo=0, hi=None):
        off, n = _CO[name]
        hi = n if hi is None else hi
        return cb[p0:p1, off + lo:off + hi]

    ident = K("ident")
    cbr = p.sb("cbr", [128, 448])
    for nm_, lo_, n_ in [("ones", 0, 128), ("inclP", 128, 128), ("inclS", 256, 64), ("sameS", 320, 64), ("ones", 384, 64)]:
        p.copy(R(cbr[:, lo_:lo_ + n_]), K(nm_, 0, 128, 0, n_))
    KR = {"ones": cbr[:, 0:128], "inclP": cbr[:, 128:256], "inclS": cbr[:, 256:320], "sameS": cbr[:, 320:384]}
    NWB = 4
    wbufs = [p.sb("wbuf%d" % i, [128, 8 * 256], BF16) for i in range(NWB)]
    wros = [p.sb("wro%d" % i, [64, 4 * 256], BF16) for i in range(NWB)]
    scr = Scr(p, "scr", 20000)
    scrR = Scr(p, "scrR", 9600)
    cT = p.sb("cT", [128, 8 * 17], BF16)
    cT32 = p.sb("cT32", [128, 8 * 17])
    modT = p.sb("modT", [128, 48 * 17])
    gsT = p.sb("gsT", [128, 2 * 8 * 17])
    FC = p.sb("FC", [128, 128])
    FCb = p.sb("FCb", [128, 16])
    F64 = p.sb("F64", [64, 64])
    TB8 = p.sb("TB8", [128, 3 * 8])
    Dcol = p.sb("Dcol", [128, 4])
    w2s = p.sb("w2s", [64, 256])
    a2s = p.sb("a2s", [64, 256])
    g2s = p.sb("g2s", [128, 256])
    gluw = p.sb("gluw", [128, 2 * 256])
    Ptab = [p.sb("Ptab%d" % i, [128, 8 * 128]) for i in range(2)]
    Qtab = [p.sb("Qtab%d" % i, [128, 1024]) for i in range(2)]
    BBt = [p.sb("BBt%d" % i, [128, 2 * 512]) for i in range(2)]
    Cbd = [p.sb("Cbd%d" % i, [128, 8 * 128]) for i in range(2)]
    s5sm = p.sb("s5sm", [128, 8 * 24])
    ssdT_P = p.sb("ssdT_P", [128, 512])
    rwH_P = p.sb("rwH_P", [64, 256])
    s5h_P = [p.sb("s5h_P%d" % i, [128, 8]) for i in range(2)]
    ccar = p.sb("ccar", [128, 24])
    scar = p.sb("scar", [64, 14])
    gcar = p.sb("gcar", [128, 1])
    fincol = p.sb("fincol", [128, 8])

    plan = []

    def plan_blocks():
        for l in range(depth):
            tiles = list(range(npt)) + ["S"]
            for t in tiles:
                for nm in ["z0", "z1", "x0", "x1", "x2", "x3", "dt", "r", "k", "v", "wag", "s5"]:
                    plan.append(("in", l, t, nm))
                for ob in range(4):
                    plan.append(("out", l, t, ob))
                for c1 in range(16):
                    plan.append(("w1", l, t, c1))
                for ob in range(4):
                    for kg in range(4):
                        plan.append(("w2", l, t, ob, kg))

    plan_blocks()
    INCOL = {"z0": (0, 256), "z1": (256, 256), "x0": (512, 256), "x1": (768, 256), "x2": (1024, 256),
             "x3": (1280, 256), "dt": (1536, 8), "r": (1544, 256), "k": (1800, 256), "v": (2056, 256),
             "wag": (2312, 256), "s5": (2568, 256)}
    wstate = {"issued": 0, "next": 0}

    def w_issue(i):
        d = plan[i]
        wb = wbufs[i % NWB]
        wv = v3(wb[:], 8)
        if d[0] == "in":
            c0, n = INCOL[d[3]]
            src = WB[("w_in", d[1])][:, c0:c0 + n].rearrange("(k p) n -> p k n", p=128)
            p.dma(wv[:, :, 0:n], src)
        elif d[0] == "out":
            l, ob = d[1], d[3]
            src = WB[("w_out", l)][0:512, ob * 256:(ob + 1) * 256].rearrange("(k p) n -> p k n", p=128)
            p.dma(wv[:, 0:4, 0:256], src)
            src = WB[("w_out", l)][768:1024, ob * 256:(ob + 1) * 256].rearrange("(k p) n -> p k n", p=128)
            p.dma(wv[:, 4:6, 0:256], src)
            src = WB[("w_out", l)][512:768, ob * 256:(ob + 1) * 256].rearrange("(k p) n -> p k n", p=64)
            p.dma(v3(wros[i % NWB][:], 4)[:, :, 0:256], src)
        elif d[0] == "w1":
            src = WB[("mlp_w1", d[1])][:, d[3] * 256:(d[3] + 1) * 256].rearrange("(k p) n -> p k n", p=128)
            p.dma(wv[:, :, 0:256], src)
        elif d[0] == "w2":
            l, ob, kg = d[1], d[3], d[4]
            src = WB[("mlp_w2", l)][kg * 1024:(kg + 1) * 1024, ob * 256:(ob + 1) * 256].rearrange("(k p) n -> p k n", p=128)
            p.dma(wv[:, :, 0:256], src)

    def wget(tag):
        i = wstate["next"]
        assert plan[i] == tag, (plan[i], tag)
        while wstate["issued"] < min(len(plan), i + NWB):
            w_issue(wstate["issued"])
            wstate["issued"] += 1
        wstate["next"] += 1
        return v3(wbufs[i % NWB][:], 8), v3(wros[i % NWB][:], 4)

    def dbgout(name, src, shape):
        if dbg is None or name not in dbg:
            return
        if name in dbg_outs:
            return
        o = dout("dbg_" + name, shape)
        dbg_outs[name] = o
        p.dma(o, src, q="gpsimd")

    def rows_to_cols(dst, buf, nrows, width):
        ps = p.bank()
        p.tr(ps[0:width, 0:nrows], buf.f(0, nrows, 0, width), ident[0:nrows, 0:nrows])
        p.copy(dst, ps[0:width, 0:nrows])

    m0 = scr.mark()
    cin = scr.alloc(1024)
    p.dma(cin.f(0, 17), cc[:, :], q="sync")
    ps = p.bank()
    for i in range(8):
        p.tr(ps[:, i * 17:(i + 1) * 17], cin(cin.a(0, 17, i * 128, (i + 1) * 128)), ident[0:17, 0:17])
    p.act(cT[:], ps[:, 0:136], AF.Silu)
    p.act(cT32[:], ps[:, 0:136], AF.Silu)
    scr.release(m0)
    cTv = v3(cT[:], 8)
    cTv32 = v3(cT32[:], 8)
    modv = v3(modT[:], 48)
    gsv = v4(gsT[:], 2, 8)

    def layer_consts(l):
        m0 = scr.mark()
        Rw = scr.alloc(128)
        Rb = scr.alloc(128)
        R64 = scr.alloc(64)
        q = "sync"
        p.dma(Rw.f(0, 48), Wt["ada_b"][l].rearrange("(r c) -> r c", c=128), q=q)
        p.dma(Rw.f(48, 56), Wt["norm1_g"][l].rearrange("(r c) -> r c", c=128), q=q)
        p.dma(Rw.f(56, 64), Wt["norm2_g"][l].rearrange("(r c) -> r c", c=128), q=q)
        p.dma(Rw.f(64, 96), Wt["ssd_conv_w"][l].rearrange("j (i c) -> (j i) c", c=128), q=q)
        p.dma(Rw.f(96, 104), Wt["ssd_conv_b"][l].rearrange("(r c) -> r c", c=128), q=q)
        p.dma(Rw.f(104, 108), Wt["ssd_norm_g"][l].rearrange("(r c) -> r c", c=128), q=q)
        p.dma(Rw.f(108, 110), Wt["s5_d"][l].rearrange("(r c) -> r c", c=128), q=q)
        p.dma(Rw.f(110, 112), Wt["s5_glu_b"][l].rearrange("(r c) -> r c", c=128), q=q)
        p.dma(Rw.f(112, 120), Wt["s5_a_re"][l].rearrange("(j a) c -> j (a c)", a=2), q=q)
        p.dma(Rw.f(120, 128), Wt["s5_a_im"][l].rearrange("(j a) c -> j (a c)", a=2), q=q)
        rows_to_cols(FC[:], Rw, 128, 128)
        p.dma(Rb.f(0, 1), Wt["rwkv_mu"][l, 896:1024].rearrange("(r c) -> r c", c=128), q=q)
        p.dma(Rb.f(1, 9), Wt["final_g"].rearrange("(r c) -> r c", c=128), q=q)
        rows_to_cols(FCb[:, 0:9], Rb, 9, 128)
        p.copy(fincol[:], FCb[:, 1:9])
        p.dma(R64.f(0, 14), Wt["rwkv_mu"][l, 0:896].rearrange("(r c) -> r c", c=64), q=q)
        for i, nm in enumerate(["rwkv_w0", "rwkv_a0", "rwkv_k_k", "rwkv_k_a", "rwkv_ln_g", "rwkv_ln_b"]):
            p.dma(R64.f(14 + 4 * i, 18 + 4 * i), Wt[nm][l].rearrange("(r c) -> r c", c=64), q=q)
        p.dma(R64.f(38, 42), Wt["rwkv_r_k"][l], q=q)
        rows_to_cols(F64[:, 0:42], R64, 42, 64)
        p.dma(TB8[:, 0:8], bass.AP(H["ssd_dt_bias"], l * 8, [[0, 128], [1, 8]]), q=q)
        p.dma(TB8[:, 8:16], bass.AP(H["ssd_a_log"], l * 8, [[0, 128], [1, 8]]), q=q)
        p.act(TB8[:, 8:16], TB8[:, 8:16], AF.Exp)
        p.ts(TB8[:, 8:16], TB8[:, 8:16], -1.0, ALU.mult)
        for pr in range(4):
            for hl in range(2):
                p.dma(Dcol[hl * 64:(hl + 1) * 64, pr:pr + 1],
                      bass.AP(H["ssd_d"], l * 8 + 2 * pr + hl, [[0, 64], [1, 1]]), q=q)
        p.dma(w2s[:], Wt["rwkv_w2"][l], q=q)
        p.dma(a2s[:], Wt["rwkv_a2"][l], q=q)
        p.dma(g2s[:], Wt["rwkv_g2"][l], q=q)
        gtmp_ = scr.alloc(512)
        p.dma(gtmp_(v3(gtmp_.a(), 2)), Wt["s5_glu_w"][l].rearrange("(k p) n -> p k n", p=128), q=q)
        for k_ in range(2):
            p.act(R(gluw[:, k_ * 256:(k_ + 1) * 256]), gtmp_.f(0, 128, k_ * 256, (k_ + 1) * 256), AF.Copy)
        ada16 = l > 0
        awb = [scr.alloc16(8 * 256), scr.alloc16(8 * 256)] if ada16 else [scr.alloc(8 * 256), scr.alloc(8 * 256)]
        cTa = cTv if ada16 else cTv32

        def ada_load(cbk):
            asrc = WB[("ada_w", l)] if ada16 else Wt["ada_w"][l]
            src = asrc[:, cbk * 256:(cbk + 1) * 256].rearrange("(k p) n -> p k n", p=128)
            p.dma(awb[cbk % 2](v3(awb[cbk % 2].a(), 8)), src)

        ada_load(0)
        for cbk in range(24):
            if cbk + 1 < 24:
                ada_load(cbk + 1)
            ab = awb[cbk % 2]
            wv = v3(ab.a(), 8)
            for m in range(2):
                ps = p.bank()
                for k in range(8):
                    p.mm(ps[:, 0:17], ab(wv[:, k, m * 128:(m + 1) * 128]), cTa[:, k, :], start=(k == 0), stop=(k == 7))
                j = cbk * 2 + m
                p.ts(modv[:, j, :], ps[:, 0:17], FC[:, j:j + 1], ALU.add)
        for which, (gq, scq) in enumerate([(48, 8), (56, 32)]):
            tmp = scr.alloc(8 * 17)
            p.ts(tmp.f(), modT[:, scq * 17:(scq + 8) * 17], 1.0, ALU.add)
            p.tt(gsv[:, which], tmp(v3(tmp.a(), 8)), bc_last(FC[:, gq:gq + 8], 17), ALU.mult)
        s5_consts(l)
        scr.release(m0)

    def s5_consts(l):
        m0 = scr.mark()
        S = v3(s5sm[:], 24)
        lre, lim = FC[:, 112:120], FC[:, 120:128]
        q = "sync"
        for glh in range(2):
            p.dma(s5sm[glh * 64:(glh + 1) * 64, 0:8], bass.AP(H["s5_log_dt"], l * 16 + glh, [[0, 64], [2, 8]]), q=q)
        sl = lambda i: s5sm[:, i * 8:(i + 1) * 8]
        p.act(sl(0), sl(0), AF.Exp)
        p.tt(sl(1), lre, sl(0), ALU.mult)
        p.act(sl(1), sl(1), AF.Exp)
        p.tt(sl(2), lim, sl(0), ALU.mult)

        def sin_of(dst, src, shift):
            t = scr.alloc(8)
            ti = scr.alloc(8)
            p.ts(t.f(), src, float(shift), ALU.add)
            p.ts(dst, t.f(), 1.0 / TWO_PI, ALU.mult)
            tii = ti(ti.a().bitcast(mybir.dt.int32))
            p.copy(tii, dst)
            p.copy(dst, tii)
            p.stt(dst, dst, -TWO_PI, t.f(), ALU.mult, ALU.add)
            p.ts(t.f(), dst, float(np.pi), ALU.is_gt)
            p.stt(dst, t.f(), -TWO_PI, dst, ALU.mult, ALU.add)
            p.ts(t.f(), dst, float(-np.pi), ALU.is_lt)
            p.stt(dst, t.f(), TWO_PI, dst, ALU.mult, ALU.add)
            p.act(dst, dst, AF.Sin)

        sin_of(sl(3), sl(2), 0.0)
        sin_of(sl(4), sl(2), np.pi / 2)
        p.tt(sl(5), sl(1), sl(4), ALU.mult)
        p.tt(sl(6), sl(1), sl(3), ALU.mult)
        p.tt(sl(7), lre, lre, ALU.mult)
        p.tt(sl(8), lim, lim, ALU.mult)
        p.tt(sl(7), sl(7), sl(8), ALU.add)
        p.recip(sl(7), sl(7))
        p.ts(sl(8), sl(5), -1.0, ALU.add)
        p.tt(sl(9), sl(8), lre, ALU.mult)
        p.tt(sl(10), sl(6), lim, ALU.mult)
        p.tt(sl(9), sl(9), sl(10), ALU.add)
        p.tt(sl(9), sl(9), sl(7), ALU.mult)
        p.tt(sl(10), sl(6), lre, ALU.mult)
        p.tt(sl(11), sl(8), lim, ALU.mult)
        p.tt(sl(10), sl(10), sl(11), ALU.subtract)
        p.tt(sl(10), sl(10), sl(7), ALU.mult)
        p.tt(sl(11), sl(1), sl(1), ALU.mult)
        p.recip(sl(11), sl(11))
        p.tt(sl(12), sl(5), sl(11), ALU.mult)
        p.tt(sl(13), sl(6), sl(11), ALU.mult)
        p.ts(sl(13), sl(13), -1.0, ALU.mult)
        bre = scr.alloc(128)
        bim = scr.alloc(128)
        p.dma(bre(v3(bre.a(), 8)), Wt["s5_b_re"][l].rearrange("(j a) q c -> (a q) j c", a=2), q=q)
        p.dma(bim(v3(bim.a(), 8)), Wt["s5_b_im"][l].rearrange("(j a) q c -> (a q) j c", a=2), q=q)
        t1 = scr.alloc(128)
        t2 = scr.alloc(128)
        bbr = scr.alloc(128)
        bbi = scr.alloc(128)
        qre3, qim3 = bc_last(sl(9), 16), bc_last(sl(10), 16)
        p.tt(t1(v3(t1.a(), 8)), bre(v3(bre.a(), 8)), qre3, ALU.mult)
        p.tt(t2(v3(t2.a(), 8)), bim(v3(bim.a(), 8)), qim3, ALU.mult)
        p.tt(bbr.f(), t1.f(), t2.f(), ALU.subtract)
        p.tt(t1(v3(t1.a(), 8)), bim(v3(bim.a(), 8)), qre3, ALU.mult)
        p.tt(t2(v3(t2.a(), 8)), bre(v3(bre.a(), 8)), qim3, ALU.mult)
        p.tt(bbi.f(), t1.f(), t2.f(), ALU.add)
        msk = v3(K("s5msk"), 8)
        for ri, bb in enumerate([bbr, bbi]):
            X = scr.alloc(1024)
            Xv = v4(X.a(), 8, 8)
            p.tt(X(Xv), bb(bc_mid(v3(bb.a(), 8), 2, 8)), bc_last(msk, 16), ALU.mult)
            for h in range(2):
                ps = p.bank()
                for jj in range(4):
                    j = 4 * h + jj
                    p.tr(ps[:, jj * 128:(jj + 1) * 128], X.f(0, 128, j * 128, (j + 1) * 128), ident)
                p.copy(R(BBt[ri][:, h * 512:(h + 1) * 512]), ps[:, :])
        mz = K("s5mz")
        for ri, nm in enumerate(["s5_c_re", "s5_c_im"]):
            for h in range(2):
                cn = scr.alloc(64)
                p.dma(cn.f(), Wt[nm][l, 8 * h:8 * h + 8].rearrange("g c q -> (g c) q"), q=q)
                Z = scr.alloc(512)
                p.tt(Z(v3(Z.a(), 8)), cn(bc_mid(cn.a(), 1, 8)), bc_last(mz, 64), ALU.mult)
                ps = p.bank()
                for jj in range(4):
                    p.tr(ps[:, jj * 128:(jj + 1) * 128], Z.f(0, 128, jj * 128, (jj + 1) * 128), ident)
                if ri == 0:
                    p.copy(R(Cbd[ri][:, h * 512:(h + 1) * 512]), ps[:, :])
                else:
                    p.ts(R(Cbd[ri][:, h * 512:(h + 1) * 512]), ps[:, :], -1.0, ALU.mult)
        Qf = [scr.alloc(1024), scr.alloc(1024)]
        tA = scr.alloc(512)
        tB = scr.alloc(512)

        def powers(dre, dim, bre_, bim_, wr):
            p.copy(wr[0](dre[:, :, 0:1]), bre_.unsqueeze(2))
            p.copy(wr[1](dim[:, :, 0:1]), bim_.unsqueeze(2))
            n = 1
            while n < 128:
                cr = dre[:, :, n - 1:n].to_broadcast([128, 8, n])
                ci = dim[:, :, n - 1:n].to_broadcast([128, 8, n])
                a_ = v3(tA.a(0, 128, 0, 8 * n), 8)
                b_ = v3(tB.a(0, 128, 0, 8 * n), 8)
                p.tt(tA(a_), wr[0](dre[:, :, 0:n]), wr[0](cr), ALU.mult)
                p.tt(tB(b_), wr[1](dim[:, :, 0:n]), wr[1](ci), ALU.mult, eng="gpsimd")
                p.tt(wr[0](dre[:, :, n:2 * n]), tA(a_), tB(b_), ALU.subtract)
                p.tt(tA(a_), wr[0](dre[:, :, 0:n]), wr[1](ci), ALU.mult)
                p.tt(tB(b_), wr[1](dim[:, :, 0:n]), wr[0](cr), ALU.mult, eng="gpsimd")
                p.tt(wr[1](dim[:, :, n:2 * n]), tA(a_), tB(b_), ALU.add)
                n *= 2

        ident_w = [lambda x: x, lambda x: x]
        powers(v3(Ptab[0][:], 8), v3(Ptab[1][:], 8), sl(5), sl(6), ident_w)
        powers(v3(Qf[0].a(), 8), v3(Qf[1].a(), 8), sl(12), sl(13), [Qf[0], Qf[1]])
        for ri in range(2):
            for h in range(2):
                ps = p.bank()
                for jj in range(4):
                    j = 4 * h + jj
                    p.tr(ps[:, jj * 128:(jj + 1) * 128], Qf[ri].f(0, 128, j * 128, (j + 1) * 128), ident)
                p.copy(Qtab[ri][:, h * 512:(h + 1) * 512], ps[:, :], eng="scalar")
        scr.release(m0)

    def s5_sample_qtab():
        m0 = scr.mark()
        sl = lambda i: s5sm[:, i * 8:(i + 1) * 8]
        q4 = [scr.alloc(32), scr.alloc(32)]
        tA = scr.alloc(32)
        tB = scr.alloc(32)
        qv = [v3(q4[0].a(), 8), v3(q4[1].a(), 8)]
        p.copy(q4[0](qv[0][:, :, 0:1]), sl(12).unsqueeze(2))
        p.copy(q4[1](qv[1][:, :, 0:1]), sl(13).unsqueeze(2))
        n = 1
        while n < 4:
            cr = qv[0][:, :, n - 1:n].to_broadcast([128, 8, n])
            ci = qv[1][:, :, n - 1:n].to_broadcast([128, 8, n])
            a_ = v3(tA.a(0, 128, 0, 8 * n), 8)
            b_ = v3(tB.a(0, 128, 0, 8 * n), 8)
            p.tt(tA(a_), q4[0](qv[0][:, :, 0:n]), q4[0](cr), ALU.mult)
            p.tt(tB(b_), q4[1](qv[1][:, :, 0:n]), q4[1](ci), ALU.mult)
            p.tt(q4[0](qv[0][:, :, n:2 * n]), tA(a_), tB(b_), ALU.subtract)
            p.tt(tA(a_), q4[0](qv[0][:, :, 0:n]), q4[1](ci), ALU.mult)
            p.tt(tB(b_), q4[1](qv[1][:, :, 0:n]), q4[0](cr), ALU.mult)
            p.tt(q4[1](qv[1][:, :, n:2 * n]), tA(a_), tB(b_), ALU.add)
            n *= 2
        for ri in range(2):
            Qs = scr.alloc(512)
            p.copy(Qs(v4(Qs.a(), 8, 16)), q4[ri](bc_mid(qv[ri], 2, 16)))
            for h in range(2):
                ps = p.bank()
                for jj in range(4):
                    j = 4 * h + jj
                    p.tr(ps[0:64, jj * 128:(jj + 1) * 128], Qs.f(0, 128, j * 64, (j + 1) * 64), ident)
                p.copy(Qtab[ri][0:64, h * 512:(h + 1) * 512], ps[0:64, :])
        scr.release(m0)

    def tile_fwd(l, T):
        kind, TT, nseq, L, chunks = T["kind"], T["TT"], T["nseq"], T["L"], T["chunks"]
        tag = T["tag"]
        last_tile = T["last"]
        sq0 = 0 if kind == "P" else 1
        tok0 = T["tok0"]
        sfx = "S" if kind == "S" else "P"
        incl, strict, strictT, mb, same = (K("incl" + sfx), K("strict" + sfx), K("strict" + sfx + "T"),
                                           K("mb" + sfx), K("same" + sfx) if kind == "S" else K("ones"))
        ones = K("ones")
        m_tile = scr.mark()
        xT = scr.alloc(8 * TT)
        xv = v3(xT.a(), 8)
        mixA = scr.alloc16(6 * TT)
        mixv = v3(mixA.a(), 6)
        mixR = scr.alloc16(4 * TT)
        mixRv = v3(mixR.a(0, 64), 4)
        E1 = 1 + L
        E3 = 3 + L
        zs = scr.alloc(4 * TT)
        zsv = v3(zs.a(), 4)
        xext = scr.alloc(8 * nseq * E3)
        xev = v4(xext.a(), 8, nseq)
        rkvx = scr.alloc(12 * nseq * E1)
        rkv = v4(rkvx.a(0, 64), 12, nseq)
        wax = scr.alloc(2 * nseq * E1)
        wav = v4(wax.a(0, 64), 2, nseq)
        glx = scr.alloc(nseq * E1)
        glv = v3(glx.a(), nseq)
        mR_tile = scrR.mark()
        us5 = scrR.alloc(2 * TT)
        usv = v3(us5.a(), 2)
        dtraw = scr.alloc(8 * len(chunks))
        m_h = scr.mark()

        if l == 0:
            for (c0, Tc) in chunks:
                mm_ = scr.mark()
                xin = scr.alloc(1024)
                src = (xp if kind == "P" else xs)[tok0 + c0 - (0 if kind == "P" else SP):tok0 + c0 - (0 if kind == "P" else SP) + Tc, :]
                p.dma(xin.f(0, Tc), src)
                for h2 in range(2):
                    ps = p.bank()
                    for i in range(4):
                        p.tr(ps[:, i * Tc:(i + 1) * Tc], xin.f(0, Tc, (4 * h2 + i) * 128, (4 * h2 + i + 1) * 128), ident[0:Tc, 0:Tc])
                    p.copy(xT(xv[:, 4 * h2:4 * h2 + 4, c0:c0 + Tc]), v3(ps[:, 0:4 * Tc], 4), eng=("vector" if h2 == 0 else "scalar"))
                scr.release(mm_)
        else:
            p.dma(xT(xv), xmid[:, :, tok0:tok0 + TT])

        def rmsnorm_mod(dst, which):
            mm_ = scr.mark()
            mr_ = scrR.mark()
            sq = scrR.alloc(8 * TT)
            p.act(R(sq.f()), xT.f(), AF.Square)
            ps = p.bank()
            for i in range(8):
                p.mm(ps[:, 0:TT], KR["ones"], sq.f(0, 128, i * TT, (i + 1) * TT), start=(i == 0), stop=(i == 7), r=True)
            scrR.release(mr_)
            rstd = scr.alloc(TT)
            p.rsqrt(rstd.f(), ps[:, 0:TT], scale=1.0 / 1024, bias=NORM_EPS)
            t = scr.alloc(8 * TT)
            p.tt(t(v3(t.a(), 8)), xT(xv), rstd(bc_mid(rstd.a(), 1, 8)), ALU.mult)
            gs = gsv[:, which, :, sq0:sq0 + nseq]
            shq = 0 if which == 0 else 24
            sh = modv[:, shq:shq + 8, sq0:sq0 + nseq]
            t4 = v4(t.a(), 8, nseq)
            p.tt(t(t4), t(t4), bc_last(gs, L), ALU.mult, eng="gpsimd")
            p.tt(dst(v4(dst.a(), 8, nseq)), t(t4), bc_last(sh, L), ALU.add)
            scr.release(mm_)

        hT = scr.alloc16(8 * TT)
        hv = v3(hT.a(), 8)
        rmsnorm_mod(hT, 0)

        if kind == "P":
            p.copy(xext(xev[:, :, 0, 0:3]), v3(ccar[:], 8))
            p.copy(rkvx(rkv[:, :, 0, 0:1]), scar[:, 0:12].unsqueeze(2))
            p.copy(wax(wav[:, :, 0, 0:1]), scar[:, 12:14].unsqueeze(2))
            p.copy(glx(glv[:, 0, 0:1]), gcar[:, 0:1])
        else:
            mm_ = scr.mark()
            cst = scr.alloc(1024)
            p.dma(cst.f(0, 48), st_conv[l], q="gpsimd")
            for i in range(8):
                ps = p.bank()
                p.tr(ps[:, 0:48], cst.f(0, 48, i * 128, (i + 1) * 128), ident[0:48, 0:48])
                p.copy(xext(xev[:, i, :, 0:3]), v3(ps[:, 0:48], 16))
            sst = scr.alloc(1024)
            p.dma(sst.f(0, 16), st_shift[l], q="gpsimd")
            ps = p.bank()
            for idx in range(14):
                p.tr(ps[0:64, idx * 16:(idx + 1) * 16], sst.f(0, 16, idx * 64, (idx + 1) * 64), ident[0:16, 0:16])
            p.copy(rkvx(rkv[:, :, :, 0]), v3(ps[0:64, 0:192], 12))
            p.copy(wax(wav[:, :, :, 0]), v3(ps[0:64, 192:224], 2))
            ps = p.bank()
            p.tr(ps[:, 0:16], sst.f(0, 16, 896, 1024), ident[0:16, 0:16])
            p.copy(glx(glv[:, :, 0]), ps[:, 0:16])
            scr.release(mm_)

        def proj_tile(wv, lo, M, dst, func=AF.Copy, parts=128, rnd=False):
            ps = p.bank()
            for k in range(8):
                p.mm(ps[0:M, 0:TT], wv[:, k, lo:lo + M], hT(hv[:, k, :]), start=(k == 0), stop=(k == 7))
            p.act(R(dst) if rnd else dst, v3(ps[0:M, 0:TT], nseq) if dst_is3(dst) else ps[0:M, 0:TT], func)

        def dst_is3(dst):
            return len(_ap(dst).shape) == 3

        for bi, nm in enumerate(["z0", "z1"]):
            wv, _ = wget(("in", l, tag, nm))
            for m in range(2):
                proj_tile(wv, m * 128, 128, zs(zsv[:, 2 * bi + m, :]), AF.Silu)
        for bi in range(4):
            wv, _ = wget(("in", l, tag, "x%d" % bi))
            for m in range(2):
                proj_tile(wv, m * 128, 128, xext(xev[:, 2 * bi + m, :, 3:]))
        wv, _ = wget(("in", l, tag, "dt"))
        for ci, (c0, Tc) in enumerate(chunks):
            ps = p.bank()
            for k in range(8):
                p.mm(ps[0:Tc, 0:8], hT(hv[:, k, c0:c0 + Tc]), wv[:, k, 0:8], start=(k == 0), stop=(k == 7))
            p.tt(dtraw.f(0, Tc, ci * 8, ci * 8 + 8), ps[0:Tc, 0:8], TB8[0:Tc, 0:8], ALU.add)
        for wi, nm in enumerate(["r", "k", "v"]):
            wv, _ = wget(("in", l, tag, nm))
            for h in range(4):
                proj_tile(wv, h * 64, 64, rkvx(rkv[:, 4 * wi + h, :, 1:]))
        wv, _ = wget(("in", l, tag, "wag"))
        proj_tile(wv, 0, 64, wax(wav[:, 0, :, 1:]))
        proj_tile(wv, 64, 64, wax(wav[:, 1, :, 1:]))
        proj_tile(wv, 128, 128, glx(glv[:, :, 1:]))
        wv, _ = wget(("in", l, tag, "s5"))
        for m in range(2):
            proj_tile(wv, m * 128, 128, us5(usv[:, m, :]), rnd=True)
        scr.release(m_h)

        if kind == "P":
            p.copy(v3(ccar[:], 8), xext(xev[:, :, 0, L:L + 3]))
            p.copy(scar[:, 0:12].unsqueeze(2), rkvx(rkv[:, :, 0, L:L + 1]))
            p.copy(scar[:, 12:14].unsqueeze(2), wax(wav[:, :, 0, L:L + 1]))
            p.copy(gcar[:, 0:1], glx(glv[:, 0, L:L + 1]))
        if last_tile:
            mm_ = scr.mark()
            n3 = nseq * 3
            ctmp = scr.alloc(8 * n3)
            p.copy(ctmp(v4(ctmp.a(), 8, nseq)), xext(xev[:, :, :, L:L + 3]))
            crow = scr.alloc(1024)
            for h2 in range(2):
                ps = p.bank()
                for i in range(4):
                    p.tr(ps[0:n3, i * 128:(i + 1) * 128], ctmp.f(0, 128, (4 * h2 + i) * n3, (4 * h2 + i + 1) * n3), ident)
                p.copy(crow.f(0, n3, h2 * 512, (h2 + 1) * 512), ps[0:n3, :])
            p.dma(o_conv[kind][l], crow.f(0, n3), q="gpsimd")
            stmp = scr.alloc(14 * nseq)
            p.copy(stmp(v3(stmp.a(0, 64, 0, 12 * nseq), 12)), rkvx(rkv[:, :, :, L]))
            p.copy(stmp(v3(stmp.a(0, 64, 12 * nseq, 14 * nseq), 2)), wax(wav[:, :, :, L]))
            gtmp = scr.alloc(nseq)
            p.copy(gtmp.f(), glx(glv[:, :, L]))
            srow = scr.alloc(1024)
            for h2 in range(2):
                ps = p.bank()
                n_idx = 8 if h2 == 0 else 6
                for ii in range(n_idx):
                    idx = 8 * h2 + ii
                    p.tr(ps[0:nseq, ii * 64:(ii + 1) * 64], stmp.f(0, 64, idx * nseq, (idx + 1) * nseq), ident[0:64, 0:64])
                if h2 == 1:
                    p.tr(ps[0:nseq, 384:512], gtmp.f(), ident)
                p.copy(srow.f(0, nseq, h2 * 512, (h2 + 1) * 512), ps[0:nseq, :])
            p.dma(o_shift[kind][l], srow.f(0, nseq), q="gpsimd")
            scr.release(mm_)

        for ci, (c0, Tc) in enumerate(chunks):
            nsc = nseq
            ssd_chunk(l, T, ci, c0, Tc, dict(xev=xev, xext=xext, zs=zs, zsv=zsv, dtraw=dtraw, mixA=mixA, mixv=mixv,
                                             incl=incl, mb=mb, same=same, ones=ones))
            rwkv_chunk(l, T, ci, c0, Tc, dict(rkvx=rkvx, rkv=rkv, wax=wax, wav=wav, glx=glx, glv=glv, mixR=mixR,
                                              mixRv=mixRv, incl=incl, strict=strict, strictT=strictT, ones=ones))
            s5_chunk(l, T, ci, c0, Tc, dict(us5=us5, usv=usv, mixA=mixA, mixv=mixv, incl=incl))

        def resid(ps, i, gq):
            g = modv[:, gq + i, sq0:sq0 + nseq]
            if nseq == 1:
                p.stt(xT(xv[:, i, :]), ps[:, 0:TT], g, xT(xv[:, i, :]), ALU.mult, ALU.add)
            else:
                mm_ = scr.mark()
                t = scr.alloc(TT)
                p.tt(t(v3(t.a(), nseq)), v3(ps[:, 0:TT], nseq), bc_last(g, L), ALU.mult)
                p.tt(xT(xv[:, i, :]), xT(xv[:, i, :]), t.f(), ALU.add, eng="gpsimd")
                scr.release(mm_)

        for ob in range(4):
            wv, wr = wget(("out", l, tag, ob))
            for m in range(2):
                ps = p.bank()
                for k in range(4):
                    p.mm(ps[:, 0:TT], wv[:, k, m * 128:(m + 1) * 128], mixA(mixv[:, k, :]), start=(k == 0), stop=False)
                for h in range(4):
                    p.mm(ps[:, 0:TT], wr[:, h, m * 128:(m + 1) * 128], mixR(mixRv[:, h, :]), start=False, stop=False)
                for k in range(2):
                    p.mm(ps[:, 0:TT], wv[:, 4 + k, m * 128:(m + 1) * 128], mixA(mixv[:, 4 + k, :]), start=False, stop=(k == 1))
                resid(ps, 2 * ob + m, 16)
        dbgout("x1_%d_%s" % (l, tag), xT(xv), [128, 8, TT])

        m_mlp = scr.mark()
        h2T = scr.alloc16(8 * TT)
        h2v = v3(h2T.a(), 8)
        rmsnorm_mod(h2T, 1)
        hid = scr.alloc16(32 * TT)
        hidv = v3(hid.a(), 32)
        rl = [scr.alloc(TT), scr.alloc(TT)]
        for c1 in range(16):
            wv, _ = wget(("w1", l, tag, c1))
            for m in range(2):
                ps = p.bank()
                for k in range(8):
                    p.mm(ps[:, 0:TT], wv[:, k, m * 128:(m + 1) * 128], h2T(h2v[:, k, :]), start=(k == 0), stop=(k == 7))
                j = 2 * c1 + m
                p.act(rl[m].f(), ps[:, 0:TT], AF.Relu)
                p.tt(hid(hidv[:, j, :]), rl[m].f(), rl[m].f(), ALU.mult, eng=("vector" if m == 0 else "gpsimd"))
        for ob in range(4):
            pss = [p.bank(), p.bank()]
            for kg in range(4):
                wv, _ = wget(("w2", l, tag, ob, kg))
                for m in range(2):
                    for k in range(8):
                        p.mm(pss[m][:, 0:TT], wv[:, k, m * 128:(m + 1) * 128], hid(hidv[:, kg * 8 + k, :]),
                             start=(kg == 0 and k == 0), stop=(kg == 3 and k == 7))
            for m in range(2):
                resid(pss[m], 2 * ob + m, 40)
        scr.release(m_mlp)
        dbgout("x2_%d_%s" % (l, tag), xT(xv), [128, 8, TT])

        if l < depth - 1:
            p.dma(xmid[:, :, tok0:tok0 + TT], xT(xv), q="gpsimd")
        else:
            mm_ = scr.mark()
            mr_ = scrR.mark()
            sq = scrR.alloc(8 * TT)
            p.act(R(sq.f()), xT.f(), AF.Square)
            ps = p.bank()
            for i in range(8):
                p.mm(ps[:, 0:TT], KR["ones"], sq.f(0, 128, i * TT, (i + 1) * TT), start=(i == 0), stop=(i == 7), r=True)
            scrR.release(mr_)
            rstd = scr.alloc(TT)
            p.rsqrt(rstd.f(), ps[:, 0:TT], scale=1.0 / 1024, bias=NORM_EPS)
            t = scr.alloc(8 * TT)
            tv = v3(t.a(), 8)
            p.tt(t(tv), xT(xv), rstd(bc_mid(rstd.a(), 1, 8)), ALU.mult)
            p.tt(t(tv), t(tv), bc_last(fincol[:, 0:8], TT), ALU.mult, eng="gpsimd")
            orows = [scr.alloc(1024), scr.alloc(1024)]
            for cix, (c0, Tc) in enumerate(chunks):
                orow = orows[cix % 2]
                for h2 in range(2):
                    ps = p.bank()
                    for i in range(4):
                        p.tr(ps[0:Tc, i * 128:(i + 1) * 128], t(tv[:, 4 * h2 + i, c0:c0 + Tc]), ident)
                    p.copy(orow.f(0, Tc, h2 * 512, (h2 + 1) * 512), ps[0:Tc, :], eng=("vector" if h2 == 0 else "scalar"))
                if kind == "P":
                    p.dma(yp[tok0 + c0:tok0 + c0 + Tc, :], orow.f(0, Tc), q="gpsimd")
                else:
                    p.dma(ys[c0:c0 + Tc, :], orow.f(0, Tc), q="gpsimd")
            scr.release(mm_)
        scr.release(m_tile)
        scrR.release(mR_tile)

    def ssd_chunk(l, T, ci, c0, Tc, C):
        kind, L = T["kind"], T["L"]
        nb = 1 if kind == "P" else 16
        Lc = Tc // nb
        last = T["last"] and ci == len(T["chunks"]) - 1
        xev, xext, zs, zsv, dtraw, mixA, mixv = C["xev"], C["xext"], C["zs"], C["zsv"], C["dtraw"], C["mixA"], C["mixv"]
        incl, mb, same, ones = C["incl"], C["mb"], C["same"], C["ones"]
        m0 = scr.mark()
        mR0 = scrR.mark()
        W8 = 8 * Tc
        xbc = scrR.alloc(W8)
        xb3 = v3(xbc.a(), 8)

        def fview(buf):
            return v3(buf.a(), 8) if kind == "P" else v4(buf.a(), 8, 16)

        def tap(j):
            return xev[:, :, 0, c0 + j:c0 + j + Tc] if kind == "P" else xev[:, :, :, j:j + L]

        def bcw(ap2):
            if kind == "P":
                return bc_last(ap2, Tc)
            return ap2.unsqueeze(2).unsqueeze(3).to_broadcast([128, 8, 16, 4])

        m1 = scr.mark()
        acc = scr.alloc(W8)

        def tap_i(i, j):
            return xev[:, i, 0, c0 + j:c0 + j + Tc] if kind == "P" else xev[:, i, :, j:j + L]

        def acc_i(i):
            a2 = acc.a(0, 128, i * Tc, (i + 1) * Tc)
            return acc(a2 if kind == "P" else v3(a2, 16))

        for i in range(8):
            p.act(acc_i(i), xext(tap_i(i, 0)), AF.Identity, bias=FC[:, 96 + i:97 + i], scale=FC[:, 64 + i:65 + i])
        for j in range(1, 4):
            for i in range(8):
                p.stt(acc_i(i), xext(tap_i(i, j)), FC[:, 64 + 8 * j + i:65 + 8 * j + i], acc_i(i), ALU.mult, ALU.add)
        p.act(R(xbc.f()), acc.f(), AF.Silu)
        scr.release(m1)
        dbgout("xbc_%d_%s_%d" % (l, T["tag"], ci), xbc(xb3), [128, 8, Tc])
        dt = scr.alloc(8)
        aa = scrR.alloc(8)
        acs = scr.alloc(8)
        dte = scr.alloc(8)
        p.act(dt.f(0, Tc), dtraw.f(0, Tc, ci * 8, ci * 8 + 8), AF.Exp)
        p.act(dt.f(0, Tc), dt.f(0, Tc), AF.Ln, bias=1.0)
        p.tt(R(aa.f(0, Tc)), dt.f(0, Tc), TB8[0:Tc, 8:16], ALU.mult)
        xdt = scr.alloc(512)
        xd2 = scrR.alloc(512)
        xdt16 = scr.alloc16(512)
        bmtok = scrR.alloc(256)
        inclr = KR["incl" + ("P" if kind == "P" else "S")]
        samer = KR["ones"] if kind == "P" else KR["sameS"]
        ps = p.bank()
        for i in range(4):
            p.tr(ps[0:Tc, i * 128:(i + 1) * 128], xbc(xb3[:, i, :]), ident)
        p.tt(xdt(v3(xdt.a(0, Tc), 8)), v3(ps[0:Tc, :], 8), dt(bc_last(dt.a(0, Tc), 64)), ALU.mult)
        p.act(xdt16.f(0, Tc), xdt.f(0, Tc), AF.Copy)
        ps = p.bank()
        for g in range(2):
            p.tr(ps[0:Tc, g * 128:(g + 1) * 128], xbc(xb3[:, 4 + g, :]), ident)
        p.act(R(bmtok.f(0, Tc)), ps[0:Tc, 0:256], AF.Copy)
        ps = p.bank()
        p.mm(ps[0:Tc, 0:8], inclr[0:Tc, 0:Tc], aa.f(0, Tc), r=True)
        p.copy(acs.f(0, Tc), ps[0:Tc, 0:8])
        ps = p.bank()
        p.mm(ps[0:Tc, 0:8], samer[0:Tc, 0:Tc], aa.f(0, Tc), r=True)
        p.tt(dte.f(0, Tc), ps[0:Tc, 0:8], acs.f(0, Tc), ALU.subtract)
        p.act(dte.f(0, Tc), dte.f(0, Tc), AF.Exp)
        p.tt(R(xd2(v3(xd2.a(0, Tc), 8))), xdt(v3(xdt.a(0, Tc), 8)), dte(bc_last(dte.a(0, Tc), 64)), ALU.mult, eng="gpsimd")
        R1 = scrR.alloc(W8)
        E2 = scr.alloc(W8)
        Lm = scr.alloc(W8)
        p.tt(R(R1(v3(R1.a(0, Tc), 8))), aa(bc_last(aa.a(0, Tc), Tc)), bc_mid(incl[0:Tc, 0:Tc], 1, 8), ALU.mult)
        nhalf = (W8 + 511) // 512
        hph = 512 // Tc
        for hf in range(nhalf):
            ps = p.bank()
            p.mm(ps[:, :], KR["ones"][0:Tc, 0:128], R1.f(0, Tc, hf * 512, (hf + 1) * 512), r=True)
            p.act(E2.f(0, 128, hf * 512, (hf + 1) * 512), ps[:, :], AF.Exp)
            p.tt(Lm(v3(Lm.a(0, Tc, hf * 512, (hf + 1) * 512), hph)), v3(ps[0:Tc, :], hph),
                 acs(bc_last(acs.a(0, Tc, hf * hph, (hf + 1) * hph), Tc)), ALU.subtract)
        p.tt(Lm(v3(Lm.a(0, Tc), 8)), Lm(v3(Lm.a(0, Tc), 8)), bc_mid(mb[0:Tc, 0:Tc], 1, 8), ALU.add, eng="gpsimd")
        p.act(Lm.f(0, Tc), Lm.f(0, Tc), AF.Exp)
        pscb = p.bank()
        for g in range(2):
            p.mm(pscb[0:Tc, g * Tc:(g + 1) * Tc], xbc(xb3[:, 4 + g, :]), xbc(xb3[:, 6 + g, :]), r=True)
        Mm = scr.alloc16(W8)
        p.tt(Mm(v4(Mm.a(0, Tc), 2, 4)), Lm(v4(Lm.a(0, Tc), 2, 4)), bc_mid(v3(pscb[0:Tc, 0:2 * Tc], 2), 2, 4), ALU.mult)
        cmh = scr.alloc16(W8)
        p.tt(cmh(v4(cmh.a(), 2, 4)), E2(v4(E2.a(), 2, 4)), xbc(bc_mid(xb3[:, 6:8, :], 2, 4)), ALU.mult, eng="gpsimd")
        E23 = v3(E2.a(), 8)
        psY = [p.bank(hold=True) for _ in range(4)]

        def yreg(h):
            return psY[h // 2][(h % 2) * 64:(h % 2 + 1) * 64, 0:Tc]

        def state_update(stT, b):
            mm_ = scr.mark()
            mmr_ = scrR.mark()
            if nb > 1:
                bmb = scrR.alloc(256)
                p.ts(R(bmb.f(0, Tc)), bmtok.f(0, Tc), K("indS", 0, Tc, b, b + 1), ALU.mult)
            else:
                bmb = bmtok
            pS = p.bank()
            for g in range(2):
                p.mm(pS[:, g * 256:(g + 1) * 256], bmb.f(0, Tc, g * 128, (g + 1) * 128), xd2.f(0, Tc, g * 256, (g + 1) * 256), r=True)
            dec = E23[:, :, b * Lc + Lc - 1]
            tmp = scr.alloc(512)
            sv = V(v3(stT.ap, 8), stT.res)
            p.tt(tmp(v3(tmp.a(), 8)), sv, E2(bc_last(dec, 64)), ALU.mult, eng="gpsimd")
            p.tt(stT, tmp.f(), pS[:, :], ALU.add)
            scr.release(mm_)
            scrR.release(mmr_)

        if kind == "P":
            stP = V(ssdT_P[:], _norm(ssdT_P[:])[1])
            st16 = scr.alloc16(512)
            p.act(st16.f(), ssdT_P[:], AF.Copy)
            for h in range(8):
                p.mm(yreg(h), xdt16.f(0, Tc, h * 64, (h + 1) * 64), Mm.f(0, Tc, h * Tc, (h + 1) * Tc), start=True, stop=False)
                p.mm(yreg(h), st16.f(0, 128, h * 64, (h + 1) * 64), cmh.f(0, 128, h * Tc, (h + 1) * Tc), start=False, stop=True)
            state_update(stP, 0)
            if last:
                mm_ = scr.mark()
                orow = scr.alloc(512)
                ps = p.bank()
                for i in range(4):
                    p.tr(ps[:, i * 128:(i + 1) * 128], ssdT_P[:, i * 128:(i + 1) * 128], ident)
                p.copy(orow.f(), ps[:, :])
                p.dma(o_ssd["P"][l, 0].rearrange("(i q) n -> q i n", q=128), orow(v3(orow.a(), 4)), q="gpsimd")
                scr.release(mm_)
        else:
            for h in range(8):
                p.mm(yreg(h), xdt16.f(0, Tc, h * 64, (h + 1) * 64), Mm.f(0, Tc, h * Tc, (h + 1) * Tc), start=True, stop=False)
            sbufs = [(scr.alloc(2048), scr.alloc(2048), scr.alloc16(2048)) for _ in range(2)]

            def sload(bg_):
                nat_ = sbufs[bg_ % 2][0]
                p.dma(nat_(v4(nat_.a(), 4, 4)), st_ssd[l, 4 * bg_:4 * bg_ + 4].rearrange("b (i q) n -> q b i n", q=128), q="sync")

            sload(0)
            for bg in range(4):
                mm_ = scr.mark()
                nat, stT, st16 = sbufs[bg % 2]
                if bg + 1 < 4:
                    sload(bg + 1)
                for bl in range(4):
                    ps = p.bank()
                    for i in range(4):
                        p.tr(ps[:, i * 128:(i + 1) * 128], nat.f(0, 128, (bl * 4 + i) * 128, (bl * 4 + i + 1) * 128), ident)
                    p.copy(stT.f(0, 128, bl * 512, (bl + 1) * 512), ps[:, :], eng=("scalar" if bl % 2 else "vector"))
                p.act(st16.f(), stT.f(), AF.Copy)
                for bl in range(4):
                    b = 4 * bg + bl
                    for h in range(8):
                        p.mm(psY[h // 2][(h % 2) * 64:(h % 2 + 1) * 64, b * Lc:(b + 1) * Lc],
                             st16.f(0, 128, bl * 512 + h * 64, bl * 512 + (h + 1) * 64),
                             cmh.f(0, 128, h * Tc + b * Lc, h * Tc + (b + 1) * Lc), start=False, stop=(b == 15))
                for bl in range(4):
                    state_update(stT.f(0, 128, bl * 512, (bl + 1) * 512), 4 * bg + bl)
                for bl in range(4):
                    ps = p.bank()
                    for i in range(4):
                        p.tr(ps[:, i * 128:(i + 1) * 128], stT.f(0, 128, bl * 512 + i * 128, bl * 512 + (i + 1) * 128), ident)
                    p.copy(nat.f(0, 128, bl * 512, (bl + 1) * 512), ps[:, :], eng=("scalar" if bl % 2 else "vector"))
                p.dma(o_ssd["S"][l, 4 * bg:4 * bg + 4].rearrange("b (i q) n -> q b i n", q=128), nat(v4(nat.a(), 4, 4)), q="gpsimd")
                scr.release(mm_)
        y3 = scr.alloc(4 * Tc)
        sq = scrR.alloc(4 * Tc)
        y33 = v3(y3.a(), 4)
        for pr in range(4):
            p.stt(y3(y33[:, pr, :]), xbc(xb3[:, pr, :]), Dcol[:, pr:pr + 1], psY[pr][:, 0:Tc], ALU.mult, ALU.add)
            p.unhold(psY[pr])
        p.tt(y3(y33), y3(y33), zs(zsv[:, :, c0:c0 + Tc]), ALU.mult, eng="gpsimd")
        p.act(R(sq.f()), y3.f(), AF.Square)
        ps = p.bank()
        for pr in range(4):
            p.mm(ps[:, 0:Tc], KR["ones"], sq.f(0, 128, pr * Tc, (pr + 1) * Tc), start=(pr == 0), stop=(pr == 3), r=True)
        rstd = scr.alloc(Tc)
        p.rsqrt(rstd.f(), ps[:, 0:Tc], scale=1.0 / 512, bias=NORM_EPS)
        for pr in range(4):
            p.stt(mixA(mixv[:, pr, c0:c0 + Tc]), y3(y33[:, pr, :]), FC[:, 104 + pr:105 + pr], rstd.f(), ALU.mult, ALU.mult)
        scr.release(m0)
        scrR.release(mR0)

    def rwkv_chunk(l, T, ci, c0, Tc, C):
        kind, L = T["kind"], T["L"]
        nb = 1 if kind == "P" else 16
        Lc = Tc // nb
        last = T["last"] and ci == len(T["chunks"]) - 1
        rkvx, rkv, wax, wav, glx, glv, mixR, mixRv = (C["rkvx"], C["rkv"], C["wax"], C["wav"], C["glx"], C["glv"],
                                                      C["mixR"], C["mixRv"])
        incl, strict, strictT, ones = C["incl"], C["strict"], C["strictT"], C["ones"]
        m0 = scr.mark()
        mR0 = scrR.mark()
        W4 = 4 * Tc

        def fv(buf):
            return v3(buf.a(0, 64), 4) if kind == "P" else v4(buf.a(0, 64), 4, 16)

        def f2(buf):
            return buf.f(0, 64)

        def cur(i0, n):
            return rkv[:, i0:i0 + n, 0, 1 + c0:1 + c0 + Tc] if kind == "P" else rkv[:, i0:i0 + n, :, 1:1 + L]

        def prv(i0, n):
            return rkv[:, i0:i0 + n, 0, c0:c0 + Tc] if kind == "P" else rkv[:, i0:i0 + n, :, 0:L]

        def bc4(ap2):
            if kind == "P":
                return bc_last(ap2, Tc)
            return ap2.unsqueeze(2).unsqueeze(3).to_broadcast([64, 4, 16, 4])

        gT = scr.alloc(W4)
        bonus = scr.alloc(W4)
        vT = scr.alloc(W4)
        epos = scr.alloc(W4)
        at, rt, bt, kt = [scrR.alloc(W4) for _ in range(4)]
        kb, bb = [scr.alloc(W4) for _ in range(2)]
        m1 = scr.mark()
        mR1 = scrR.mark()
        sqr = scrR.alloc(W4)
        rkr = scrR.alloc(W4)
        rT = scr.alloc(W4)
        kT = scr.alloc(W4)
        lw = scr.alloc(W4)
        aG = scr.alloc(W4)
        t1 = scr.alloc(W4)
        t2 = scr.alloc(W4)
        kkn = scr.alloc(W4)
        for wi, dst in enumerate([rT, kT, vT]):
            p.tt(t1(fv(t1)), rkvx(prv(4 * wi, 4)), rkvx(cur(4 * wi, 4)), ALU.subtract)
            p.tt(t1(fv(t1)), t1(fv(t1)), bc4(F64[:, 4 * wi:4 * wi + 4]), ALU.mult, eng="gpsimd")
            p.tt(dst(fv(dst)), t1(fv(t1)), rkvx(cur(4 * wi, 4)), ALU.add)

        def mix1(xv_, i, mucol, dst, parts):
            if kind == "P":
                c_, p_ = xv_[:, i, 0, 1 + c0:1 + c0 + Tc], xv_[:, i, 0, c0:c0 + Tc]
                d_ = dst.a(0, parts, 0, Tc)
            else:
                c_, p_ = xv_[:, i, :, 1:1 + L], xv_[:, i, :, 0:L]
                d_ = v3(dst.a(0, parts, 0, Tc), 16)
            return c_, p_, d_

        wlm = scr.alloc(Tc)
        alm = scr.alloc(Tc)
        glm = scr.alloc(Tc)
        for i, (dst, mc) in enumerate([(wlm, F64[:, 12:13]), (alm, F64[:, 13:14])]):
            c_, p_, d_ = mix1(wav, i, mc, dst, 64)
            p.tt(dst(d_), wax(p_), wax(c_), ALU.subtract)
            p.stt(dst(d_), dst(d_), mc, wax(c_), ALU.mult, ALU.add)
        if kind == "P":
            c_, p_ = glv[:, 0, 1 + c0:1 + c0 + Tc], glv[:, 0, c0:c0 + Tc]
            d_ = glm.a(0, 128, 0, Tc)
        else:
            c_, p_ = glv[:, :, 1:1 + L], glv[:, :, 0:L]
            d_ = v3(glm.a(0, 128, 0, Tc), 16)
        p.tt(glm(d_), glx(p_), glx(c_), ALU.subtract)
        p.stt(glm(d_), glm(d_), FCb[:, 0:1], glx(c_), ALU.mult, ALU.add)
        p.act(wlm.f(0, 64), wlm.f(0, 64), AF.Tanh)
        ps = p.bank()
        for h in range(4):
            p.mm(ps[0:64, h * Tc:(h + 1) * Tc], w2s[:, h * 64:(h + 1) * 64], wlm.f(0, 64))
        for h in range(4):
            p.act(lw.f(0, 64, h * Tc, (h + 1) * Tc), ps[0:64, h * Tc:(h + 1) * Tc], AF.Sigmoid, bias=F64[:, 14 + h:15 + h])
        CDEC = -float(np.exp(-0.5))
        ps = p.bank()
        for h in range(4):
            p.mm(ps[0:64, h * Tc:(h + 1) * Tc], a2s[:, h * 64:(h + 1) * 64], alm.f(0, 64))
        for h in range(4):
            p.act(aG.f(0, 64, h * Tc, (h + 1) * Tc), ps[0:64, h * Tc:(h + 1) * Tc], AF.Sigmoid, bias=F64[:, 18 + h:19 + h])
        p.act(glm.f(), glm.f(), AF.Sigmoid)
        ps = p.bank()
        for h in range(4):
            p.mm(ps[0:64, h * Tc:(h + 1) * Tc], g2s[:, h * 64:(h + 1) * 64], glm.f())
        p.copy(f2(gT), ps[0:64, 0:W4], eng="scalar")
        p.tt(t1(fv(t1)), kT(fv(kT)), bc4(F64[:, 22:26]), ALU.mult)
        p.tt(R(f2(sqr)), f2(t1), f2(t1), ALU.mult, eng="gpsimd")
        ps = p.bank()
        p.mm(ps[0:64, 0:W4], KR["ones"][0:64, 0:64], f2(sqr), r=True)
        p.ts(f2(t2), ps[0:64, 0:W4], 1e-18, ALU.max)
        p.rsqrt(f2(t2), f2(t2))
        p.tt(f2(kkn), f2(t1), f2(t2), ALU.mult)
        p.ts(f2(t1), f2(aG), -1.0, ALU.add)
        p.tt(t1(fv(t1)), t1(fv(t1)), bc4(F64[:, 26:30]), ALU.mult, eng="gpsimd")
        p.stt(f2(kT), f2(t1), 1.0, f2(kT), ALU.add, ALU.mult)
        p.tt(f2(t1), f2(rT), f2(kT), ALU.mult)
        p.tt(R(rkr(fv(rkr))), t1(fv(t1)), bc4(F64[:, 38:42]), ALU.mult, eng="gpsimd")
        ps = p.bank()
        p.mm(ps[0:64, 0:W4], KR["ones"][0:64, 0:64], f2(rkr), r=True)
        p.tt(f2(bonus), ps[0:64, 0:W4], f2(vT), ALU.mult)
        cs = t2
        for h in range(4):
            d0 = ones[0:64, 0:Tc] if kind == "P" else K("restartS", 0, 64)
            p.scan(cs.f(0, 64, h * Tc, (h + 1) * Tc), d0, lw.f(0, 64, h * Tc, (h + 1) * Tc), 0.0, ALU.mult, ALU.add)
        p.act(f2(epos), f2(cs), AF.Exp, scale=CDEC)
        eneg = scr.alloc(W4)
        eprev = scr.alloc(W4)
        eend = scr.alloc(W4)
        p.act(f2(eneg), f2(cs), AF.Exp, scale=-CDEC)
        p.tt(f2(eprev), f2(cs), f2(lw), ALU.subtract)
        p.act(f2(eprev), f2(eprev), AF.Exp, scale=CDEC)
        if kind == "P":
            cs3 = v3(cs.a(0, 64), 4)
            p.tt(eend(v3(eend.a(0, 64), 4)), cs(bc_last(cs3[:, :, Tc - 1], Tc)), cs(cs3), ALU.subtract)
        else:
            cs4 = v4(cs.a(0, 64), 4, 16)
            p.tt(eend(v4(eend.a(0, 64), 4, 16)), cs(bc_last(cs4[:, :, :, Lc - 1], Lc)), cs(cs4), ALU.subtract)
        p.act(f2(eend), f2(eend), AF.Exp, scale=CDEC)
        p.tt(f2(t1), f2(kkn), f2(aG), ALU.mult, eng="gpsimd")
        p.stt(R(f2(at)), f2(kkn), -1.0, f2(eprev), ALU.mult, ALU.mult)
        p.tt(R(f2(rt)), f2(rT), f2(epos), ALU.mult, eng="gpsimd")
        p.tt(R(f2(bt)), f2(t1), f2(eneg), ALU.mult)
        p.tt(R(f2(kt)), f2(kT), f2(eneg), ALU.mult, eng="gpsimd")
        p.tt(f2(kb), f2(kT), f2(eend), ALU.mult)
        p.tt(f2(bb), f2(t1), f2(eend), ALU.mult, eng="gpsimd")
        dbgout("rw_at_%d_%s_%d" % (l, T["tag"], ci), f2(at), [64, W4])
        dbgout("rw_kt_%d_%s_%d" % (l, T["tag"], ci), f2(kt), [64, W4])
        scr.release(m1)
        scrR.release(mR1)
        Vtok = scrR.alloc(256)
        Kbtok = scrR.alloc(256)
        Bbtok = scrR.alloc(256)
        for src_, dst_ in [(vT, Vtok), (kb, Kbtok), (bb, Bbtok)]:
            ps = p.bank()
            for h in range(4):
                p.tr(ps[0:Tc, h * 64:(h + 1) * 64], src_.f(0, 64, h * Tc, (h + 1) * Tc), ident[0:64, 0:64])
            p.act(R(dst_.f(0, Tc)), ps[0:Tc, 0:256], AF.Copy)

        def amat(lh, rh, mask, dst, rnd=True):
            ps = p.bank()
            for h in range(4):
                p.mm(ps[0:Tc, h * Tc:(h + 1) * Tc], lh.f(0, 64, h * Tc, (h + 1) * Tc), rh.f(0, 64, h * Tc, (h + 1) * Tc), r=True)
            dv = dst(v3(dst.a(0, Tc, 0, W4), 4))
            p.tt(R(dv) if rnd else dv, v3(ps[0:Tc, 0:W4], 4), bc_mid(mask[0:Tc, 0:Tc], 1, 4), ALU.mult)

        A0, B0 = [scr.alloc(W4) for _ in range(2)]
        AakT, ArkT, ArbT = [scrR.alloc(W4) for _ in range(3)]
        amat(bt, at, strict, A0, rnd=False)
        amat(at, bt, strictT, B0, rnd=False)
        amat(kt, at, strict, AakT)
        amat(kt, rt, incl, ArkT)
        amat(bt, rt, incl, ArbT)
        Zs = [scr.alloc(W4), scr.alloc(W4)]
        As = [A0, scr.alloc(W4)]
        Bs = [B0, scr.alloc(W4)]
        p.tt(Zs[0](v3(Zs[0].a(0, Tc, 0, W4), 4)), A0(v3(A0.a(0, Tc, 0, W4), 4)), bc_mid(ident[0:Tc, 0:Tc], 1, 4), ALU.add, eng="gpsimd")
        nst = 7 if kind == "P" else 2
        zi = 0
        for i in range(1, nst):
            Ao, Bo, An, Bn = As[(i - 1) % 2], Bs[(i - 1) % 2], As[i % 2], Bs[i % 2]
            psB = p.bank()
            for h in range(4):
                p.mm(psB[0:Tc, h * Tc:(h + 1) * Tc], Ao.f(0, Tc, h * Tc, (h + 1) * Tc), Bo.f(0, Tc, h * Tc, (h + 1) * Tc))
            if i < nst - 1:
                psA = p.bank()
                for h in range(4):
                    p.mm(psA[0:Tc, h * Tc:(h + 1) * Tc], Bo.f(0, Tc, h * Tc, (h + 1) * Tc), Ao.f(0, Tc, h * Tc, (h + 1) * Tc))
            p.copy(Bn.f(0, Tc, 0, W4), psB[0:Tc, 0:W4])
            if i < nst - 1:
                p.act(An.f(0, Tc, 0, W4), psA[0:Tc, 0:W4], AF.Copy)
            psZ = p.bank()
            for h in range(4):
                p.mm(psZ[0:Tc, h * Tc:(h + 1) * Tc], Bn.f(0, Tc, h * Tc, (h + 1) * Tc), Zs[zi].f(0, Tc, h * Tc, (h + 1) * Tc))
            p.tt(Zs[1 - zi].f(0, Tc, 0, W4), Zs[zi].f(0, Tc, 0, W4), psZ[0:Tc, 0:W4], ALU.add)
            zi = 1 - zi
        Z = Zs[zi]
        if kind == "P":
            Hs = None

            def Hap(b, h):
                return rwH_P[:, h * 64:(h + 1) * 64]
            atm, rtm = "at", "rt"

            def lq(which, b, h):
                return (at if which == "at" else rt).f(0, 64, h * Tc, (h + 1) * Tc)
        else:
            Hs = scrR.alloc(16 * 256)
            nat = scr.alloc(16 * 256)
            p.dma(nat(v3(nat.a(0, 64), 64)), st_rwkv[l].rearrange("b (h v) k -> v (b h) k", h=4), q="gpsimd")
            for g8 in range(8):
                ps = p.bank()
                for j in range(8):
                    bh = g8 * 8 + j
                    p.tr(ps[0:64, j * 64:(j + 1) * 64], nat.f(0, 64, bh * 64, (bh + 1) * 64), ident[0:64, 0:64])
                if g8 % 2:
                    p.act(R(Hs.f(0, 64, g8 * 512, (g8 + 1) * 512)), ps[0:64, :], AF.Copy)
                else:
                    p.copy(R(Hs.f(0, 64, g8 * 512, (g8 + 1) * 512)), ps[0:64, :])

            def Hap(b, h):
                return Hs.f(0, 64, (b * 4 + h) * 64, (b * 4 + h + 1) * 64)
            mbuf = [scrR.alloc(16 * Tc), scrR.alloc(16 * Tc)]
            atm, rtm = "at", "rt"
            indv3 = v3(K("indBC", 0, 64), 16)

            def lq(which, b, h):
                return mbuf[h % 2].f(0, 64, b * Tc, (b + 1) * Tc)

        def prep_q(which, h):
            if kind == "P":
                return
            src_ = at if which == "at" else rt
            p.tt(R(mbuf[h % 2](v3(mbuf[h % 2].a(0, 64), 16))), src_(bc_mid(src_.a(0, 64, h * Tc, (h + 1) * Tc), 1, 16)), indv3, ALU.mult)

        psY = p.bank()
        for h in range(4):
            prep_q("at", h)
            for b in range(nb):
                p.mm(psY[0:Tc, h * 64:(h + 1) * 64], lq(atm, b, h), Hap(b, h), start=(b == 0), stop=False, r=True)
            p.mm(psY[0:Tc, h * 64:(h + 1) * 64], AakT.f(0, Tc, h * Tc, (h + 1) * Tc), Vtok.f(0, Tc, h * 64, (h + 1) * 64),
                 start=False, stop=True, r=True)
        Ysb = scrR.alloc(256)
        Usb = scrR.alloc(256)
        p.act(R(Ysb.f(0, Tc)), psY[0:Tc, 0:256], AF.Copy)
        psU = p.bank()
        for h in range(4):
            p.mm(psU[0:Tc, h * 64:(h + 1) * 64], Z.f(0, Tc, h * Tc, (h + 1) * Tc), Ysb.f(0, Tc, h * 64, (h + 1) * 64))
        p.copy(R(Usb.f(0, Tc)), psU[0:Tc, 0:256])
        psO = p.bank()
        for h in range(4):
            prep_q("rt", h)
            for b in range(nb):
                p.mm(psO[0:Tc, h * 64:(h + 1) * 64], lq(rtm, b, h), Hap(b, h), start=(b == 0), stop=False, r=True)
            p.mm(psO[0:Tc, h * 64:(h + 1) * 64], ArbT.f(0, Tc, h * Tc, (h + 1) * Tc), Usb.f(0, Tc, h * 64, (h + 1) * 64),
                 start=False, stop=False, r=True)
            p.mm(psO[0:Tc, h * 64:(h + 1) * 64], ArkT.f(0, Tc, h * Tc, (h + 1) * Tc), Vtok.f(0, Tc, h * 64, (h + 1) * 64),
                 start=False, stop=True, r=True)
        mu = scr.alloc(4)
        var = scr.alloc(4)
        dd = scr.alloc(256)
        d2 = scr.alloc(256)
        dd3 = v3(dd.a(0, Tc), 4)
        p.reduce(mu.f(0, Tc), v3(psO[0:Tc, 0:256], 4))
        p.ts(mu.f(0, Tc), mu.f(0, Tc), 1.0 / 64, ALU.mult)
        p.tt(dd(dd3), v3(psO[0:Tc, 0:256], 4), mu(bc_last(mu.a(0, Tc), 64)), ALU.subtract)
        p.tt(d2.f(0, Tc), dd.f(0, Tc), dd.f(0, Tc), ALU.mult, eng="gpsimd")
        p.reduce(var.f(0, Tc), d2(v3(d2.a(0, Tc), 4)))
        p.rsqrt(var.f(0, Tc), var.f(0, Tc), scale=1.0 / 64, bias=RWKV_LN_EPS)
        p.tt(dd(dd3), dd(dd3), var(bc_last(var.a(0, Tc), 64)), ALU.mult)
        psT = p.bank()
        for h in range(4):
            p.tr(psT[0:64, h * Tc:(h + 1) * Tc], dd.f(0, Tc, h * 64, (h + 1) * 64), ident[0:Tc, 0:Tc])
        onT = scr.alloc(W4)
        for h in range(4):
            p.ts(onT.f(0, 64, h * Tc, (h + 1) * Tc), psT[0:64, h * Tc:(h + 1) * Tc], F64[:, 30 + h:31 + h], ALU.mult,
                 F64[:, 34 + h:35 + h], ALU.add)
        p.tt(f2(onT), f2(onT), f2(bonus), ALU.add, eng="gpsimd")
        p.tt(mixR(mixRv[:, :, c0:c0 + Tc]), onT(v3(onT.a(0, 64), 4)), gT(v3(gT.a(0, 64), 4)), ALU.mult)
        if kind == "P":
            psH = p.bank()
            for h in range(4):
                p.mm(psH[0:64, h * 64:(h + 1) * 64], Kbtok.f(0, Tc, h * 64, (h + 1) * 64), Vtok.f(0, Tc, h * 64, (h + 1) * 64),
                     start=True, stop=False, r=True)
                p.mm(psH[0:64, h * 64:(h + 1) * 64], Bbtok.f(0, Tc, h * 64, (h + 1) * 64), Usb.f(0, Tc, h * 64, (h + 1) * 64),
                     start=False, stop=True, r=True)
            tmp = scr.alloc(256)
            ep3 = v3(epos.a(0, 64), 4)
            p.tt(tmp(v3(tmp.a(0, 64), 4)), v3(rwH_P[:], 4), epos(bc_last(ep3[:, :, Tc - 1], 64)), ALU.mult, eng="gpsimd")
            p.tt(R(rwH_P[:]), tmp.f(0, 64), psH[0:64, 0:256], ALU.add)
            if last:
                ps = p.bank()
                for h in range(4):
                    p.tr(ps[0:64, h * 64:(h + 1) * 64], rwH_P[:, h * 64:(h + 1) * 64], ident[0:64, 0:64])
                orow = scr.alloc(256)
                p.copy(orow.f(0, 64), ps[0:64, 0:256])
                p.dma(o_rwkv["P"][l, 0].rearrange("(h v) k -> v h k", h=4), orow(v3(orow.a(0, 64), 4)), q="gpsimd")
        else:
            ep4 = v4(epos.a(0, 64), 4, 16)
            Hs4 = v4(Hs.a(0, 64), 16, 4)
            for h in range(4):
                mm_ = scr.mark()
                mmr_ = scrR.mark()
                Km, Bm = mbuf[0], mbuf[1]
                indr = bc_last(K("indS", 0, Tc), 64)
                p.tt(R(Km(v3(Km.a(0, Tc), 16))), Kbtok(bc_mid(Kbtok.a(0, Tc, h * 64, (h + 1) * 64), 1, 16)), indr, ALU.mult)
                p.tt(R(Bm(v3(Bm.a(0, Tc), 16))), Bbtok(bc_mid(Bbtok.a(0, Tc, h * 64, (h + 1) * 64), 1, 16)), indr, ALU.mult, eng="gpsimd")
                tmp = scr.alloc(1024)
                p.tt(tmp(v3(tmp.a(0, 64), 16)), Hs(Hs4[:, :, h, :]), epos(bc_last(ep4[:, h, :, Lc - 1], 64)), ALU.mult, eng="gpsimd")
                for half in range(2):
                    psH = p.bank()
                    for j in range(8):
                        b = half * 8 + j
                        p.mm(psH[0:64, j * 64:(j + 1) * 64], Km.f(0, Tc, b * 64, (b + 1) * 64), Vtok.f(0, Tc, h * 64, (h + 1) * 64),
                             start=True, stop=False, r=True)
                        p.mm(psH[0:64, j * 64:(j + 1) * 64], Bm.f(0, Tc, b * 64, (b + 1) * 64), Usb.f(0, Tc, h * 64, (h + 1) * 64),
                             start=False, stop=True, r=True)
                    p.tt(R(Hs(Hs4[:, half * 8:(half + 1) * 8, h, :])), tmp(v3(tmp.a(0, 64, half * 512, (half + 1) * 512), 8)),
                         v3(psH[0:64, :], 8), ALU.add)
                scr.release(mm_)
                scrR.release(mmr_)
            for g8 in range(8):
                ps = p.bank()
                for j in range(8):
                    bh = g8 * 8 + j
                    p.tr(ps[0:64, j * 64:(j + 1) * 64], Hs.f(0, 64, bh * 64, (bh + 1) * 64), ident[0:64, 0:64])
                p.copy(nat.f(0, 64, g8 * 512, (g8 + 1) * 512), ps[0:64, :], eng=("scalar" if g8 % 2 else "vector"))
            p.dma(o_rwkv["S"][l].rearrange("b (h v) k -> v (b h) k", h=4), nat(v3(nat.a(0, 64), 64)), q="gpsimd")
        scr.release(m0)
        scrR.release(mR0)

    s5hS = [None, None]

    def s5_chunk(l, T, ci, c0, Tc, C):
        kind, L = T["kind"], T["L"]
        nb = 1 if kind == "P" else 16
        Lc = Tc // nb
        last = T["last"] and ci == len(T["chunks"]) - 1
        us5, usv, mixA, mixv, incl = C["us5"], C["usv"], C["mixA"], C["mixv"], C["incl"]
        inclr = KR["incl" + ("P" if kind == "P" else "S")]
        m0 = scr.mark()
        mR0 = scrR.mark()
        W8 = 8 * Tc
        A_, B_ = [scrR.alloc(1024) for _ in range(2)]
        C_, D_, E_, F_ = [scr.alloc(1024) for _ in range(4)]
        if kind == "S":
            hin = [scr.alloc(128), scr.alloc(128)]
            for ri, st in enumerate([st_s5re, st_s5im]):
                row = scr.alloc(1024)
                p.dma(row.f(0, 16), st[l], q="gpsimd")
                ps = p.bank()
                for j in range(8):
                    p.tr(ps[:, j * 16:(j + 1) * 16], row.f(0, 16, j * 128, (j + 1) * 128), ident[0:16, 0:16])
                p.copy(hin[ri].f(), ps[:, 0:128])
            hinv = [hin[ri](v3(hin[ri].a(), 8)) for ri in range(2)]
        else:
            hinv = [s5h_P[0][:], s5h_P[1][:]]
        psr = [p.bank(), p.bank()]
        psi = [p.bank(), p.bank()]
        for h in range(2):
            p.mm(psr[h][0:Tc, :], us5(usv[:, h, c0:c0 + Tc]), BBt[0][:, h * 512:(h + 1) * 512], r=True)
            p.mm(psi[h][0:Tc, :], us5(usv[:, h, c0:c0 + Tc]), BBt[1][:, h * 512:(h + 1) * 512], r=True)
        for h in range(2):
            cs_ = slice(h * 512, (h + 1) * 512)
            qre, qim = Qtab[0][0:Tc, cs_], Qtab[1][0:Tc, cs_]
            a_, b_ = A_.f(0, Tc, h * 512, (h + 1) * 512), B_.f(0, Tc, h * 512, (h + 1) * 512)
            c_, d_ = C_.f(0, Tc, h * 512, (h + 1) * 512), D_.f(0, Tc, h * 512, (h + 1) * 512)
            e_, f_ = E_.f(0, Tc, h * 512, (h + 1) * 512), F_.f(0, Tc, h * 512, (h + 1) * 512)
            p.tt(c_, qre, psr[h][0:Tc, :], ALU.mult)
            p.tt(d_, qim, psi[h][0:Tc, :], ALU.mult)
            p.tt(R(a_), c_, d_, ALU.subtract, eng="gpsimd")
            p.tt(e_, qre, psi[h][0:Tc, :], ALU.mult)
            p.tt(f_, qim, psr[h][0:Tc, :], ALU.mult)
            p.tt(R(b_), e_, f_, ALU.add, eng="gpsimd")
        nbank = (W8 + 511) // 512
        jpb = 512 // Tc
        psG = [[p.bank() for _ in range(nbank)] for _ in range(2)]
        for ri, Wb in enumerate([A_, B_]):
            for j in range(8):
                col = j * Tc
                p.mm(psG[ri][col // 512][:, col % 512:col % 512 + Tc], Wb.f(0, Tc, j * 128, (j + 1) * 128), inclr[0:Tc, 0:Tc], r=True)

        def fview(ap2, nj):
            return v3(ap2, nj) if kind == "P" else v4(ap2, nj, 16)

        for ri, Gb in enumerate([C_, D_]):
            for bk in range(nbank):
                j0 = bk * jpb
                hslice = hinv[ri].ap[:, j0:j0 + jpb] if isinstance(hinv[ri], V) else hinv[ri][:, j0:j0 + jpb]
                hres = hinv[ri].res if isinstance(hinv[ri], V) else _norm(hinv[ri])[1]
                p.tt(Gb(fview(Gb.a(0, 128, bk * 512, bk * 512 + jpb * Tc), jpb)), fview(psG[ri][bk][:, 0:jpb * Tc], jpb),
                     V(bc_last(hslice, Lc), hres), ALU.add)
        P3 = [v3(Ptab[ri][:], 8) for ri in range(2)]
        if kind == "P":
            pw = [P3[ri][:, :, 0:Tc] for ri in range(2)]
        else:
            pw = [bc_mid(P3[ri][:, :, 0:Lc], 2, 16) for ri in range(2)]
        fa = lambda buf: buf(fview(buf.a(0, 128, 0, W8), 8))
        p.tt(fa(E_), fa(C_), pw[0], ALU.mult)
        p.tt(fa(F_), fa(D_), pw[1], ALU.mult, eng="gpsimd")
        p.tt(R(fa(A_)), fa(E_), fa(F_), ALU.subtract)
        p.tt(fa(E_), fa(C_), pw[1], ALU.mult, eng="gpsimd")
        p.tt(fa(F_), fa(D_), pw[0], ALU.mult)
        p.tt(R(fa(B_)), fa(E_), fa(F_), ALU.add, eng="gpsimd")
        hb = [A_, B_]
        for ri in range(2):
            hv_ = fview(hb[ri].a(0, 128, 0, W8), 8)
            if kind == "P":
                p.copy(s5h_P[ri][:], hb[ri](hv_[:, :, Tc - 1]))
            else:
                p.copy(hinv[ri], hb[ri](hv_[:, :, :, Lc - 1]))
        yg = scrR.alloc(2 * Tc)
        ygv = v3(yg.a(), 2)
        yd = scr.alloc(2 * Tc)
        ydv = v3(yd.a(), 2)
        for h in range(2):
            ps = p.bank()
            for jj in range(4):
                j = 4 * h + jj
                p.mm(ps[:, 0:Tc], Cbd[0][:, j * 128:(j + 1) * 128], A_.f(0, 128, j * Tc, (j + 1) * Tc), start=(jj == 0), stop=False, r=True)
                p.mm(ps[:, 0:Tc], Cbd[1][:, j * 128:(j + 1) * 128], B_.f(0, 128, j * Tc, (j + 1) * Tc), start=False, stop=(jj == 3), r=True)
            p.stt(yd(ydv[:, h, :]), us5(usv[:, h, c0:c0 + Tc]), FC[:, 108 + h:109 + h], ps[:, 0:Tc], ALU.mult, ALU.add)
        p.act(R(yg.f()), yd.f(), AF.Gelu_apprx_tanh)
        gw = v3(gluw[:], 2)
        sg = scr.alloc(Tc)
        for h2 in range(2):
            ps = p.bank()
            for h in range(2):
                p.mm(ps[:, 0:Tc], gw[:, h, h2 * 128:(h2 + 1) * 128], yg(ygv[:, h, :]), start=(h == 0), stop=(h == 1), r=True)
            p.act(sg.f(), ps[:, 0:Tc], AF.Sigmoid, bias=FC[:, 110 + h2:111 + h2])
            p.tt(mixA(mixv[:, 4 + h2, c0:c0 + Tc]), yg(ygv[:, h2, :]), sg.f(), ALU.mult)
        if last:
            for ri, od in enumerate([o_s5re, o_s5im]):
                orow = scr.alloc(1024)
                if kind == "P":
                    ps = p.bank()
                    p.tr(ps[0:8, 0:128], s5h_P[ri][:], ident)
                    p.copy(orow.f(0, 8, 0, 128), ps[0:8, 0:128])
                    p.dma(od["P"][l].rearrange("a (j c) -> (a j) c", j=8), orow.f(0, 8, 0, 128), q="gpsimd")
                else:
                    for h2 in range(2):
                        ps = p.bank()
                        for jj in range(4):
                            j = 4 * h2 + jj
                            p.tr(ps[0:16, jj * 128:(jj + 1) * 128], hinv[ri].ap[:, j, :] if True else None, ident)
                        p.copy(orow.f(0, 16, h2 * 512, (h2 + 1) * 512), ps[0:16, :])
                    p.dma(od["S"][l], orow.f(0, 16), q="gpsimd")
        scr.release(m0)
        scrR.release(mR0)

    ptiles = [dict(kind="P", tag=t, TT=256, nseq=1, L=256, chunks=[(0, 128), (128, 128)], tok0=256 * t,
                   last=(t == npt - 1)) for t in range(npt)]
    stile = dict(kind="S", tag="S", TT=64, nseq=16, L=4, chunks=[(0, 64)], tok0=SP, last=True)
    emit_casts(0)
    for l in range(depth):
        layer_consts(l)
        p.memset(ssdT_P[:], 0.0)
        p.copy(R(rwH_P[:]), K("zeros", 0, 64, 0, 256), eng="gpsimd")
        p.memset(s5h_P[0][:], 0.0)
        p.memset(s5h_P[1][:], 0.0, eng="gpsimd")
        p.memset(ccar[:], 0.0)
        p.memset(scar[:], 0.0, eng="gpsimd")
        p.memset(gcar[:], 0.0)
        for T in ptiles:
            tile_fwd(l, T)
            if l + 1 < depth:
                emit_casts(l + 1, n=(30 + npt - 1) // npt)
        if l + 1 < depth:
            emit_casts(l + 1)
        s5_sample_qtab()
        tile_fwd(l, stile)
    assert wstate["next"] == len(plan), (wstate["next"], len(plan))
    p.emit()
    info = dict(nops=len(p.ops), nwaits=p.nwaits, ecount=p.ecount, scr_peak=scr.peak, scrR_peak=scrR.peak)
    return nc, info, list(dbg_outs.keys())


_CACHE = {}


def _run(inputs, npt=8, depth=2, dbg=None, ncores=8):
    key = (npt, depth, tuple(dbg) if dbg else None)
    nc, info, dbgs = build(npt=npt, depth=depth, dbg=dbg)
    SP = 256 * npt
    f32 = lambda a: np.ascontiguousarray(np.asarray(a, dtype=np.float32))
    shared = {n: f32(inputs[n]) for n in IN_NAMES}
    shared["cblob"] = _BLOB
    in_maps = []
    for c in range(ncores):
        b0, b1 = 16 * c, 16 * (c + 1)
        m = dict(shared)
        m["xp"] = f32(inputs["x_prompt"][c, :SP])
        m["xs"] = f32(inputs["x_sample"][b0:b1].reshape(64, D))
        m["cc"] = f32(np.concatenate([inputs["c_prompt"][c:c + 1], inputs["c_sample"][b0:b1]], axis=0))
        m["st_ssd"] = f32(inputs["state_ssd"][:, b0:b1].reshape(2, 16, 512, 128))
        m["st_conv"] = f32(inputs["state_ssd_conv"][:, b0:b1].reshape(2, 48, 1024))
        m["st_rwkv"] = f32(inputs["state_rwkv"][:, b0:b1].reshape(2, 16, 256, 64))
        m["st_shift"] = f32(inputs["state_rwkv_shift"][:, b0:b1])
        m["st_s5re"] = f32(inputs["state_s5_re"][:, b0:b1].reshape(2, 16, 1024))
        m["st_s5im"] = f32(inputs["state_s5_im"][:, b0:b1].reshape(2, 16, 1024))
        in_maps.append(m)
    res = run_bass_kernel_spmd(nc, in_maps, core_ids=list(range(ncores)))
    return res.results, info, dbgs


def kernel(**inputs):
    R, info, _ = _run(inputs)
    cat = lambda k, ax: np.concatenate([np.asarray(r[k]) for r in R], axis=ax)
    y_prompt = np.stack([np.asarray(r["yp"]) for r in R], axis=0)
    y_sample = cat("ys", 0).reshape(128, 4, 1024)
    p_ssd = cat("p_ssd", 1).reshape(2, 8, 8, 64, 128)
    p_conv = np.stack([np.asarray(r["p_conv"]) for r in R], axis=1)
    p_rwkv = cat("p_rwkv", 1).reshape(2, 8, 4, 64, 64)
    p_shift = cat("p_shift", 1)
    p_re = cat("p_s5re", 1).reshape(2, 8, 16, 64)
    p_im = cat("p_s5im", 1).reshape(2, 8, 16, 64)
    s_ssd = cat("s_ssd", 1).reshape(2, 128, 8, 64, 128)
    s_conv = cat("s_conv", 1).reshape(2, 128, 3, 1024)
    s_rwkv = cat("s_rwkv", 1).reshape(2, 128, 4, 64, 64)
    s_shift = cat("s_shift", 1)
    s_re = cat("s_s5re", 1).reshape(2, 128, 16, 64)
    s_im = cat("s_s5im", 1).reshape(2, 128, 16, 64)
    outs = (y_prompt, y_sample, p_ssd, p_conv, p_rwkv, p_shift, p_re, p_im, s_ssd, s_conv, s_rwkv, s_shift, s_re, s_im)
    return tuple(np.ascontiguousarray(o, dtype=np.float32) for o in outs)
```

```python
import numpy as np
from contextlib import ExitStack
import concourse.bass as bass
import concourse.mybir as mybir
from concourse.bass_utils import run_bass_kernel_spmd

F32 = mybir.dt.float32
BF16 = mybir.dt.bfloat16
F32R = mybir.dt.float32r
AF = mybir.ActivationFunctionType
ALU = mybir.AluOpType
AX = mybir.AxisListType
ENGS = ("sync", "gpsimd", "scalar", "vector", "tensor")
BLK = 64


class V:
    def __init__(self, ap, res):
        self.ap = ap
        self.res = res


def _norm(x):
    if isinstance(x, V):
        return x.ap, list(x.res)
    return x, [(x.name, None)]


def _ap(x):
    return x.ap if isinstance(x, V) else x


def R(x):
    if isinstance(x, V):
        return V(x.ap.bitcast(F32R), x.res)
    return V(x.bitcast(F32R), [(x.name, None)])


FOLD_WAITS = True
POOL_TO_DVE = True
ROUNDED_TENSORS = {"scrR", "cbr", "BBt0", "BBt1", "Cbd0", "Cbd1", "gluw", "rwH_P"}


class Prog:
    NDMASEM = 8

    def __init__(self, nc):
        self.nc = nc
        self.es = ExitStack()
        self.ops = []
        self.state = {}
        self.banks = []
        self.bi = 0

    def sb(self, name, shape, dtype=F32):
        return self.es.enter_context(self.nc.sbuf_tensor(name, list(shape), dtype))

    def ps(self, name, shape, dtype=F32):
        return self.es.enter_context(self.nc.psum_tensor(name, list(shape), dtype))

    def bank(self, hold=False):
        if not self.banks:
            self.banks = [self.ps("psb%d" % i, [128, 512]) for i in range(8)]
            self.held = set()
        while (self.bi % 8) in self.held:
            self.bi += 1
        k = self.bi % 8
        self.bi += 1
        if hold:
            self.held.add(k)
        return self.banks[k]

    def unhold(self, b):
        self.held.discard(self.banks.index(b))

    def op(self, eng, fn, reads=(), writes=(), dma=False):
        if eng == "gpsimd" and not dma and POOL_TO_DVE:
            eng = "vector"
        rr = []
        for x in reads:
            rr += _norm(x)[1]
        ww = []
        for x in writes:
            rs_ = _norm(x)[1]
            if rs_ and rs_[0][0] in ROUNDED_TENSORS:
                assert _ap(x).dtype == F32R, ("non-rounded write into fp32r arena", rs_[0])
            ww += rs_
        deps = set()

        def entries(name, key):
            d = self.state.setdefault(name, {})
            if key is None:
                return list(d.values())
            out = []
            if key in d:
                out.append(d[key])
            if None in d:
                out.append(d[None])
            return out

        for (n, k) in rr:
            for ent in entries(n, k):
                if ent[0] is not None:
                    deps.add((ent[0], "raw"))
                if n.startswith("psb"):
                    for r in ent[1]:
                        deps.add((r, "rar"))
        for (n, k) in ww:
            for ent in entries(n, k):
                if ent[0] is not None:
                    deps.add((ent[0], "waw"))
                for r in ent[1]:
                    deps.add((r, "war"))
        idx = len(self.ops)
        self.ops.append(dict(eng=eng, fn=fn, deps=deps, dma=dma))
        for (n, k) in rr:
            d = self.state.setdefault(n, {})
            if k is None:
                for ent in d.values():
                    ent[1].append(idx)
                if None not in d:
                    d[None] = [None, [idx]]
            else:
                d.setdefault(k, [None, []])[1].append(idx)
        for (n, k) in ww:
            d = self.state.setdefault(n, {})
            if k is None:
                d.clear()
                d[None] = [idx, []]
            else:
                if None in d:
                    pass
                d[k] = [idx, []]
        return idx

    def emit(self):
        nc = self.nc
        ops = self.ops
        es = self.es
        esem = {e: es.enter_context(nc.semaphore("s_" + e)) for e in ENGS}
        dsem = {e: [es.enter_context(nc.semaphore("d_%s%d" % (e, i))) for i in range(self.NDMASEM)]
                for e in ("sync", "gpsimd", "scalar")}
        eff = []
        for i, o in enumerate(ops):
            e = o["eng"]
            ds = set()
            for (d, kind) in o["deps"]:
                od = ops[d]
                if od["eng"] == e and not od["dma"] and not o["dma"]:
                    if e == "tensor" or kind == "rar":
                        continue
                ds.add(d)
            latest = {}
            keep = set()
            for d in ds:
                od = ops[d]
                if od["dma"]:
                    keep.add(d)
                elif latest.get(od["eng"], -1) < d:
                    latest[od["eng"]] = d
            keep.update(latest.values())
            eff.append(keep)
        signal = [False] * len(ops)
        for ds in eff:
            for d in ds:
                signal[d] = True
        ecount = {e: 0 for e in ENGS}
        dcount = {e: [0] * self.NDMASEM for e in dsem}
        drr = {e: 0 for e in dsem}
        for i, o in enumerate(ops):
            e = o["eng"]
            if o["dma"]:
                j = drr[e] % self.NDMASEM
                drr[e] += 1
                o["prev_dma_val"] = dcount[e][j]
                dcount[e][j] += 16
                o["sem"] = dsem[e][j]
                o["val"] = dcount[e][j]
                o["semkey"] = ("d", e, j)
            elif signal[i]:
                ecount[e] += 1
                o["sem"] = esem[e]
                o["val"] = ecount[e]
                o["semkey"] = ("e", e)
        semobj = {("e", e): esem[e] for e in ENGS}
        for e in dsem:
            for j in range(self.NDMASEM):
                semobj[("d", e, j)] = dsem[e][j]
        eclock = {e: {} for e in ENGS}
        opclock = {}
        waits = [None] * len(ops)
        self.nwaits = 0
        for i, o in enumerate(ops):
            e = o["eng"]
            ck = eclock[e]
            need = {}
            if o["dma"] and o["prev_dma_val"] > 0:
                need[o["semkey"]] = o["prev_dma_val"]
            for d in eff[i]:
                od = ops[d]
                sk = od["semkey"]
                if need.get(sk, 0) < od["val"]:
                    need[sk] = od["val"]
            todo = []
            for sk, v in need.items():
                if ck.get(sk, 0) < v:
                    todo.append((sk, v))
            for d in eff[i]:
                oc = opclock.get(d)
                if oc:
                    for sk, v in oc.items():
                        if ck.get(sk, 0) < v:
                            ck[sk] = v
            for sk, v in need.items():
                if ck.get(sk, 0) < v:
                    ck[sk] = v
            todo2 = []
            for (sk, v) in todo:
                implied = False
                for d in eff[i]:
                    od = ops[d]
                    if od["semkey"] == sk:
                        continue
                    oc = opclock.get(d)
                    if oc and oc.get(sk, 0) >= v and any(t[0] == od["semkey"] for t in todo):
                        implied = True
                        break
                if not implied:
                    todo2.append((sk, v))
            waits[i] = todo2
            self.nwaits += len(todo2)
            if o["dma"] or signal[i]:
                oc = dict(ck)
                oc[o["semkey"]] = max(oc.get(o["semkey"], 0), o["val"])
                opclock[i] = oc
        per = {e: [i for i, o in enumerate(ops) if o["eng"] == e] for e in ENGS}
        finals = [(dsem[e][j], dcount[e][j]) for e in dsem for j in range(self.NDMASEM) if dcount[e][j] > 0]
        self.ecount = ecount

        def run(e, eng):
            for i in per[e]:
                o = ops[i]
                todo = waits[i]
                fold = FOLD_WAITS and (not o["dma"]) and len(todo) > 0
                for (sk, v) in (todo[:-1] if fold else todo):
                    eng.wait_ge(semobj[sk], v)
                ins = o["fn"](eng)
                if fold:
                    sk, v = todo[-1]
                    ins._wait_ge(semobj[sk], v)
                if o["dma"]:
                    ins.then_inc(o["sem"], 16)
                elif signal[i]:
                    ins.then_inc(o["sem"], 1)
            if e == "sync":
                for (s_, v) in finals:
                    eng.wait_ge(s_, v)

        with nc.allow_non_contiguous_dma(reason="tiny constant/state gathers"), nc.Block() as block:
            @block.sync
            def _(eng):
                run("sync", eng)

            @block.gpsimd
            def _(eng):
                run("gpsimd", eng)

            @block.scalar
            def _(eng):
                run("scalar", eng)

            @block.vector
            def _(eng):
                run("vector", eng)

            @block.tensor
            def _(eng):
                run("tensor", eng)
        es.close()

    def dma(self, out, in_, q="sync"):
        o, i = _ap(out), _ap(in_)
        return self.op(q, lambda e: e.dma_start(out=o, in_=i), reads=[in_], writes=[out], dma=True)

    def mm(self, out, lhsT, rhs, start=True, stop=True, r=False):
        if r:
            lhsT, rhs = R(lhsT), R(rhs)
        o, a, b = _ap(out), _ap(lhsT), _ap(rhs)
        return self.op("tensor", lambda e: e.matmul(o, a, b, start=start, stop=stop),
                       reads=[lhsT, rhs], writes=[out])

    def tr(self, out, in_, ident):
        o, a, b = _ap(out), _ap(in_), _ap(ident)
        return self.op("tensor", lambda e: e.transpose(o, a, b), reads=[in_, ident], writes=[out])

    def act(self, out, in_, func, bias=None, scale=None):
        o, a = _ap(out), _ap(in_)
        kw = {}
        rd = [in_]
        if bias is not None:
            if isinstance(bias, (int, float)):
                kw["bias"] = float(bias)
            else:
                kw["bias"] = _ap(bias)
                rd.append(bias)
        if scale is not None:
            if isinstance(scale, (int, float)):
                kw["scale"] = float(scale)
            else:
                kw["scale"] = _ap(scale)
                rd.append(scale)
        return self.op("scalar", lambda e: e.activation(out=o, in_=a, func=func, **kw), reads=rd, writes=[out])

    def tt(self, out, in0, in1, op, eng="vector"):
        o, a, b = _ap(out), _ap(in0), _ap(in1)
        return self.op(eng, lambda e: e.tensor_tensor(out=o, in0=a, in1=b, op=op), reads=[in0, in1], writes=[out])

    def ts(self, out, in0, s1, op0, s2=None, op1=None, eng="vector"):
        o, a = _ap(out), _ap(in0)
        rd = [in0]
        x1 = s1
        if not isinstance(s1, (int, float)):
            rd.append(s1)
            x1 = _ap(s1)
        x2 = s2
        if s2 is not None and not isinstance(s2, (int, float)):
            rd.append(s2)
            x2 = _ap(s2)
        if op1 is None:
            fn = lambda e: e.tensor_scalar(out=o, in0=a, scalar1=x1, scalar2=None, op0=op0)
        else:
            fn = lambda e: e.tensor_scalar(out=o, in0=a, scalar1=x1, scalar2=x2, op0=op0, op1=op1)
        return self.op(eng, fn, reads=rd, writes=[out])

    def stt(self, out, in0, scalar, in1, op0, op1):
        o, a, b = _ap(out), _ap(in0), _ap(in1)
        rd = [in0, in1]
        s = scalar
        if not isinstance(scalar, (int, float)):
            rd.append(scalar)
            s = _ap(scalar)
        return self.op("vector", lambda e: e.scalar_tensor_tensor(out=o, in0=a, scalar=s, in1=b, op0=op0, op1=op1),
                       reads=rd, writes=[out])

    def copy(self, out, in_, eng="vector"):
        if eng == "scalar":
            return self.act(out, in_, AF.Copy)
        o, a = _ap(out), _ap(in_)
        return self.op(eng, lambda e: e.tensor_copy(out=o, in_=a), reads=[in_], writes=[out])

    def memset(self, out, val, eng="vector"):
        o = _ap(out)
        return self.op(eng, lambda e: e.memset(o, val), reads=[], writes=[out])

    def rsqrt(self, out, in_, scale=1.0, bias=0.0):
        self.act(out, in_, AF.Ln, bias=bias, scale=scale)
        self.act(out, out, AF.Exp, scale=-0.5)

    def recip(self, out, in_):
        o, a = _ap(out), _ap(in_)
        return self.op("vector", lambda e: e.reciprocal(out=o, in_=a), reads=[in_], writes=[out])

    def reduce(self, out, in_, op=ALU.add):
        o, a = _ap(out), _ap(in_)
        return self.op("vector", lambda e: e.tensor_reduce(out=o, in_=a, axis=AX.X, op=op), reads=[in_], writes=[out])

    def scan(self, out, d0, d1, initial, op0, op1):
        o, a, b = _ap(out), _ap(d0), _ap(d1)
        return self.op("vector", lambda e: e.tensor_tensor_scan(out=o, data0=a, data1=b, initial=initial, op0=op0, op1=op1),
                       reads=[d0, d1], writes=[out])


class Buf:
    def __init__(self, t, name, off, n):
        self.t = t
        self.off = off
        self.n = n
        self.res = [(name, k) for k in range(off // BLK, (off + n - 1) // BLK + 1)]

    def a(self, p0=0, p1=128, lo=0, hi=None):
        hi = self.n if hi is None else hi
        return self.t[p0:p1, self.off + lo:self.off + hi]

    def __call__(self, ap):
        return V(ap, self.res)

    def f(self, p0=0, p1=128, lo=0, hi=None):
        return V(self.a(p0, p1, lo, hi), self.res)


class Buf16(Buf):
    def __init__(self, t16, name, off32, n16):
        self.t16 = t16
        self.off = off32
        self.n = n16
        n32 = (n16 + 1) // 2
        self.res = [(name, k) for k in range(off32 // BLK, (off32 + n32 - 1) // BLK + 1)]

    def a(self, p0=0, p1=128, lo=0, hi=None):
        hi = self.n if hi is None else hi
        return self.t16[p0:p1, 2 * self.off + lo:2 * self.off + hi]


class Scr:
    def __init__(self, p, name, ncols):
        self.name = name
        self.t = p.sb(name, [128, ncols])
        self.t16 = self.t[:, :].bitcast(BF16)
        self.n = ncols
        self.top = 0
        self.peak = 0

    def alloc(self, ncols):
        off = self.top
        self.top += ((ncols + BLK - 1) // BLK) * BLK
        assert self.top <= self.n, ("scratch overflow", self.top, self.n)
        self.peak = max(self.peak, self.top)
        return Buf(self.t, self.name, off, ncols)

    def alloc16(self, n16):
        off = self.top
        n32 = (n16 + 1) // 2
        self.top += ((n32 + BLK - 1) // BLK) * BLK
        assert self.top <= self.n, ("scratch overflow", self.top, self.n)
        self.peak = max(self.peak, self.top)
        return Buf16(self.t16, self.name, off, n16)

    def mark(self):
        return self.top

    def release(self, m):
        self.top = m


def _const_blob():
    c = {}
    cols = []

    def add(name, arr):
        arr = np.asarray(arr, np.float32)
        full = np.zeros((128, arr.shape[1]), np.float32)
        full[:arr.shape[0]] = arr
        c[name] = (sum(a.shape[1] for a in cols), arr.shape[1])
        cols.append(full)

    i128 = np.arange(128)
    add("ident", np.eye(128))
    add("zeros", np.zeros((128, 512)))
    add("ones", np.ones((128, 128)))
    inclP = (i128[:, None] <= i128[None, :]).astype(np.float32)
    add("inclP", inclP)
    add("strictP", (i128[:, None] < i128[None, :]))
    add("strictPT", (i128[:, None] > i128[None, :]))
    add("mbP", np.where(inclP > 0, 0.0, -30000.0))
    add("sameP", np.ones((128, 128)))
    i64 = np.arange(64)
    same = (i64[:, None] // 4) == (i64[None, :] // 4)
    inclS = same & (i64[:, None] <= i64[None, :])
    add("inclS", inclS)
    add("strictS", same & (i64[:, None] < i64[None, :]))
    add("strictST", same & (i64[:, None] > i64[None, :]))
    add("mbS", np.where(inclS, 0.0, -30000.0))
    add("sameS", same)
    ind = (i64[:, None] // 4 == np.arange(16)[None, :]).astype(np.float32)
    add("indS", ind)
    add("indST", ind.T)
    add("indBC", np.broadcast_to(ind.T.reshape(1, 1024), (128, 1024)))
    add("restartS", np.broadcast_to((i64 % 4 != 0).astype(np.float32)[None, :], (128, 64)))
    add("onesrow", np.ones((128, 128)))
    part = np.arange(128)
    gl = part // 64
    msk = np.zeros((128, 8, 8), np.float32)
    for j in range(8):
        for g8 in range(8):
            msk[:, j, g8] = ((2 * j + gl) % 8 == g8)
    add("s5msk", msk.reshape(128, 64))
    mz = (part[:, None] // 16 == np.arange(8)[None, :]).astype(np.float32)
    add("s5mz", mz)
    blob = np.concatenate(cols, axis=1)
    return blob, c


_BLOB, _CO = _const_blob()
NCB = _BLOB.shape[1]

D = 1024
NORM_EPS = 1e-6
RWKV_LN_EPS = 64e-5
TWO_PI = 2.0 * np.pi

IN_NAMES = ["ada_w", "ada_b", "norm1_g", "norm2_g", "w_in", "ssd_conv_w", "ssd_conv_b", "ssd_dt_bias", "ssd_a_log",
            "ssd_d", "ssd_norm_g", "rwkv_mu", "rwkv_w0", "rwkv_w2", "rwkv_a0", "rwkv_a2", "rwkv_g2", "rwkv_k_k",
            "rwkv_k_a", "rwkv_r_k", "rwkv_ln_g", "rwkv_ln_b", "s5_a_re", "s5_a_im", "s5_log_dt", "s5_b_re", "s5_b_im",
            "s5_c_re", "s5_c_im", "s5_d", "s5_glu_w", "s5_glu_b", "w_out", "mlp_w1", "mlp_w2", "final_g"]
W_SHAPES = {
    "ada_w": (2, 1024, 6144), "ada_b": (2, 6144), "norm1_g": (2, 1024), "norm2_g": (2, 1024), "w_in": (2, 1024, 2824),
    "ssd_conv_w": (2, 4, 1024), "ssd_conv_b": (2, 1024), "ssd_dt_bias": (2, 8), "ssd_a_log": (2, 8), "ssd_d": (2, 8),
    "ssd_norm_g": (2, 512), "rwkv_mu": (2, 1024), "rwkv_w0": (2, 256), "rwkv_w2": (2, 64, 256), "rwkv_a0": (2, 256),
    "rwkv_a2": (2, 64, 256), "rwkv_g2": (2, 128, 256), "rwkv_k_k": (2, 256), "rwkv_k_a": (2, 256),
    "rwkv_r_k": (2, 4, 64), "rwkv_ln_g": (2, 256), "rwkv_ln_b": (2, 256), "s5_a_re": (2, 16, 64), "s5_a_im": (2, 16, 64),
    "s5_log_dt": (2, 16), "s5_b_re": (2, 16, 64, 16), "s5_b_im": (2, 16, 64, 16), "s5_c_re": (2, 16, 16, 64),
    "s5_c_im": (2, 16, 16, 64), "s5_d": (2, 256), "s5_glu_w": (2, 256, 256), "s5_glu_b": (2, 256),
    "w_out": (2, 1024, 1024), "mlp_w1": (2, 1024, 4096), "mlp_w2": (2, 4096, 1024), "final_g": (1024,),
}


def v3(ap, a):
    return ap.rearrange("p (a b) -> p a b", a=a)


def v4(ap, a, b):
    return ap.rearrange("p (a b c) -> p a b c", a=a, b=b)


def bc_last(ap, n):
    sh = list(ap.shape)
    return ap.unsqueeze(len(sh)).to_broadcast(sh + [n])


def bc_mid(ap, pos, n):
    sh = list(ap.shape)
    return ap.unsqueeze(pos).to_broadcast(sh[:pos] + [n] + sh[pos:])


def build(npt=8, depth=2, dbg=None):
    nc = bass.Bass("TRN2", target_bir_lowering=False)
    p = Prog(nc)
    SP = 256 * npt
    H = {}

    def din(name, shape):
        h = nc.dram_tensor(name, list(shape), F32, kind="ExternalInput")
        H[name] = h
        return h.ap()

    def dout(name, shape):
        h = nc.dram_tensor(name, list(shape), F32, kind="ExternalOutput")
        H[name] = h
        return h.ap()

    xp = din("xp", [SP, D])
    xs = din("xs", [64, D])
    cc = din("cc", [17, D])
    st_ssd = din("st_ssd", [2, 16, 512, 128])
    st_conv = din("st_conv", [2, 48, 1024])
    st_rwkv = din("st_rwkv", [2, 16, 256, 64])
    st_shift = din("st_shift", [2, 16, 1024])
    st_s5re = din("st_s5re", [2, 16, 1024])
    st_s5im = din("st_s5im", [2, 16, 1024])
    cblob = din("cblob", [128, NCB])
    Wt = {n: din(n, W_SHAPES[n]) for n in IN_NAMES}
    yp = dout("yp", [SP, D])
    ys = dout("ys", [64, D])
    o_ssd = {"P": dout("p_ssd", [2, 1, 512, 128]), "S": dout("s_ssd", [2, 16, 512, 128])}
    o_conv = {"P": dout("p_conv", [2, 3, 1024]), "S": dout("s_conv", [2, 48, 1024])}
    o_rwkv = {"P": dout("p_rwkv", [2, 1, 256, 64]), "S": dout("s_rwkv", [2, 16, 256, 64])}
    o_shift = {"P": dout("p_shift", [2, 1, 1024]), "S": dout("s_shift", [2, 16, 1024])}
    o_s5re = {"P": dout("p_s5re", [2, 1, 1024]), "S": dout("s_s5re", [2, 16, 1024])}
    o_s5im = {"P": dout("p_s5im", [2, 1, 1024]), "S": dout("s_s5im", [2, 16, 1024])}
    xmid = nc.dram_tensor("xmid", [128, 8, SP + 64], F32).ap()
    WB = {}
    for nm_ in ["ada_w", "w_in", "w_out", "mlp_w1", "mlp_w2"]:
        for l_ in range(depth):
            WB[(nm_, l_)] = nc.dram_tensor("wb_%s_%d" % (nm_, l_), list(W_SHAPES[nm_][1:]), BF16).ap()
    CAST_ROWS = {"ada_w": 128, "w_in": 256, "w_out": 512, "mlp_w1": 128, "mlp_w2": 512}
    cast_plan = {l_: [(nm_, r0) for nm_ in (["ada_w"] if l_ > 0 else []) + ["w_in", "w_out", "mlp_w1", "mlp_w2"]
                      for r0 in range(0, W_SHAPES[nm_][1], CAST_ROWS[nm_])] for l_ in range(depth)}

    def emit_casts(l_, n=None):
        k = 0
        while cast_plan[l_] and (n is None or k < n):
            nm_, r0 = cast_plan[l_].pop(0)
            r1 = r0 + CAST_ROWS[nm_]
            p.dma(WB[(nm_, l_)][r0:r1, :], Wt[nm_][l_, r0:r1, :], q="gpsimd")
            k += 1
    dbg_outs = {}

    cb = p.sb("cblob_sb", [128, NCB])
    p.dma(cb[:], cblob[:, :], q="gpsimd")

    def K(name, p0=0, p1=128, lo=0, hi=None):
        off, n = _CO[name]
        hi = n if hi is None else hi
        return cb[p0:p1, off + lo:off + hi]

    ident = K("ident")
    cbr = p.sb("cbr", [128, 448])
    for nm_, lo_, n_ in [("ones", 0, 128), ("inclP", 128, 128), ("inclS", 256, 64), ("sameS", 320, 64), ("ones", 384, 64)]:
        p.copy(R(cbr[:, lo_:lo_ + n_]), K(nm_, 0, 128, 0, n_))
    KR = {"ones": cbr[:, 0:128], "inclP": cbr[:, 128:256], "inclS": cbr[:, 256:320], "sameS": cbr[:, 320:384]}
    NWB = 5
    wbufs = [p.sb("wbuf%d" % i, [128, 8 * 256], BF16) for i in range(NWB)]
    wros = [p.sb("wro%d" % i, [64, 4 * 256], BF16) for i in range(NWB)]
    scr = Scr(p, "scr", 19200)
    scrR = Scr(p, "scrR", 9408)
    cT = p.sb("cT", [128, 8 * 17], BF16)
    cT32 = p.sb("cT32", [128, 8 * 17])
    modT = p.sb("modT", [128, 48 * 17])
    gsT = p.sb("gsT", [128, 2 * 8 * 17])
    FC = p.sb("FC", [128, 128])
    FCb = p.sb("FCb", [128, 16])
    F64 = p.sb("F64", [64, 64])
    TB8 = p.sb("TB8", [128, 3 * 8])
    Dcol = p.sb("Dcol", [128, 4])
    w2s = p.sb("w2s", [64, 256])
    a2s = p.sb("a2s", [64, 256])
    g2s = p.sb("g2s", [128, 256])
    gluw = p.sb("gluw", [128, 2 * 256])
    Ptab = [p.sb("Ptab%d" % i, [128, 8 * 128]) for i in range(2)]
    Qtab = [p.sb("Qtab%d" % i, [128, 1024]) for i in range(2)]
    BBt = [p.sb("BBt%d" % i, [128, 2 * 512]) for i in range(2)]
    Cbd = [p.sb("Cbd%d" % i, [128, 8 * 128]) for i in range(2)]
    s5sm = p.sb("s5sm", [128, 8 * 24])
    ssdT_P = p.sb("ssdT_P", [128, 512])
    rwH_P = p.sb("rwH_P", [64, 256])
    s5h_P = [p.sb("s5h_P%d" % i, [128, 8]) for i in range(2)]
    ccar = p.sb("ccar", [128, 24])
    scar = p.sb("scar", [64, 14])
    gcar = p.sb("gcar", [128, 1])
    fincol = p.sb("fincol", [128, 8])

    plan = []

    def plan_blocks():
        for l in range(depth):
            tiles = list(range(npt)) + ["S"]
            for t in tiles:
                for nm in ["z0", "z1", "x0", "x1", "x2", "x3", "dt", "r", "k", "v", "wag", "s5"]:
                    plan.append(("in", l, t, nm))
                for ob in range(4):
                    plan.append(("out", l, t, ob))
                for c1 in range(16):
                    plan.append(("w1", l, t, c1))
                for ob in range(4):
                    for kg in range(4):
                        plan.append(("w2", l, t, ob, kg))

    plan_blocks()
    INCOL = {"z0": (0, 256), "z1": (256, 256), "x0": (512, 256), "x1": (768, 256), "x2": (1024, 256),
             "x3": (1280, 256), "dt": (1536, 8), "r": (1544, 256), "k": (1800, 256), "v": (2056, 256),
             "wag": (2312, 256), "s5": (2568, 256)}
    wstate = {"issued": 0, "next": 0}

    def w_issue(i):
        d = plan[i]
        wb = wbufs[i % NWB]
        wv = v3(wb[:], 8)
        if d[0] == "in":
            c0, n = INCOL[d[3]]
            src = WB[("w_in", d[1])][:, c0:c0 + n].rearrange("(k p) n -> p k n", p=128)
            p.dma(wv[:, :, 0:n], src)
        elif d[0] == "out":
            l, ob = d[1], d[3]
            src = WB[("w_out", l)][0:512, ob * 256:(ob + 1) * 256].rearrange("(k p) n -> p k n", p=128)
            p.dma(wv[:, 0:4, 0:256], src)
            src = WB[("w_out", l)][768:1024, ob * 256:(ob + 1) * 256].rearrange("(k p) n -> p k n", p=128)
            p.dma(wv[:, 4:6, 0:256], src)
            src = WB[("w_out", l)][512:768, ob * 256:(ob + 1) * 256].rearrange("(k p) n -> p k n", p=64)
            p.dma(v3(wros[i % NWB][:], 4)[:, :, 0:256], src)
        elif d[0] == "w1":
            src = WB[("mlp_w1", d[1])][:, d[3] * 256:(d[3] + 1) * 256].rearrange("(k p) n -> p k n", p=128)
            p.dma(wv[:, :, 0:256], src)
        elif d[0] == "w2":
            l, ob, kg = d[1], d[3], d[4]
            src = WB[("mlp_w2", l)][kg * 1024:(kg + 1) * 1024, ob * 256:(ob + 1) * 256].rearrange("(k p) n -> p k n", p=128)
            p.dma(wv[:, :, 0:256], src)

    def wget(tag):
        i = wstate["next"]
        assert plan[i] == tag, (plan[i], tag)
        while wstate["issued"] < min(len(plan), i + NWB):
            w_issue(wstate["issued"])
            wstate["issued"] += 1
        wstate["next"] += 1
        return v3(wbufs[i % NWB][:], 8), v3(wros[i % NWB][:], 4)

    def dbgout(name, src, shape):
        if dbg is None or name not in dbg:
            return
        if name in dbg_outs:
            return
        o = dout("dbg_" + name, shape)
        dbg_outs[name] = o
        p.dma(o, src, q="gpsimd")

    def rows_to_cols(dst, buf, nrows, width):
        ps = p.bank()
        p.tr(ps[0:width, 0:nrows], buf.f(0, nrows, 0, width), ident[0:nrows, 0:nrows])
        p.copy(dst, ps[0:width, 0:nrows])

    m0 = scr.mark()
    cin = scr.alloc(1024)
    p.dma(cin.f(0, 17), cc[:, :], q="sync")
    ps = p.bank()
    for i in range(8):
        p.tr(ps[:, i * 17:(i + 1) * 17], cin(cin.a(0, 17, i * 128, (i + 1) * 128)), ident[0:17, 0:17])
    p.act(cT[:], ps[:, 0:136], AF.Silu)
    p.act(cT32[:], ps[:, 0:136], AF.Silu)
    scr.release(m0)
    cTv = v3(cT[:], 8)
    cTv32 = v3(cT32[:], 8)
    modv = v3(modT[:], 48)
    gsv = v4(gsT[:], 2, 8)

    def layer_consts(l):
        m0 = scr.mark()
        Rw = scr.alloc(128)
        Rb = scr.alloc(128)
        R64 = scr.alloc(64)
        q = "sync"
        p.dma(Rw.f(0, 48), Wt["ada_b"][l].rearrange("(r c) -> r c", c=128), q=q)
        p.dma(Rw.f(48, 56), Wt["norm1_g"][l].rearrange("(r c) -> r c", c=128), q=q)
        p.dma(Rw.f(56, 64), Wt["norm2_g"][l].rearrange("(r c) -> r c", c=128), q=q)
        p.dma(Rw.f(64, 96), Wt["ssd_conv_w"][l].rearrange("j (i c) -> (j i) c", c=128), q=q)
        p.dma(Rw.f(96, 104), Wt["ssd_conv_b"][l].rearrange("(r c) -> r c", c=128), q=q)
        p.dma(Rw.f(104, 108), Wt["ssd_norm_g"][l].rearrange("(r c) -> r c", c=128), q=q)
        p.dma(Rw.f(108, 110), Wt["s5_d"][l].rearrange("(r c) -> r c", c=128), q=q)
        p.dma(Rw.f(110, 112), Wt["s5_glu_b"][l].rearrange("(r c) -> r c", c=128), q=q)
        p.dma(Rw.f(112, 120), Wt["s5_a_re"][l].rearrange("(j a) c -> j (a c)", a=2), q=q)
        p.dma(Rw.f(120, 128), Wt["s5_a_im"][l].rearrange("(j a) c -> j (a c)", a=2), q=q)
        rows_to_cols(FC[:], Rw, 128, 128)
        p.dma(Rb.f(0, 1), Wt["rwkv_mu"][l, 896:1024].rearrange("(r c) -> r c", c=128), q=q)
        p.dma(Rb.f(1, 9), Wt["final_g"].rearrange("(r c) -> r c", c=128), q=q)
        rows_to_cols(FCb[:, 0:9], Rb, 9, 128)
        p.copy(fincol[:], FCb[:, 1:9])
        p.dma(R64.f(0, 14), Wt["rwkv_mu"][l, 0:896].rearrange("(r c) -> r c", c=64), q=q)
        for i, nm in enumerate(["rwkv_w0", "rwkv_a0", "rwkv_k_k", "rwkv_k_a", "rwkv_ln_g", "rwkv_ln_b"]):
            p.dma(R64.f(14 + 4 * i, 18 + 4 * i), Wt[nm][l].rearrange("(r c) -> r c", c=64), q=q)
        p.dma(R64.f(38, 42), Wt["rwkv_r_k"][l], q=q)
        rows_to_cols(F64[:, 0:42], R64, 42, 64)
        p.dma(TB8[:, 0:8], bass.AP(H["ssd_dt_bias"], l * 8, [[0, 128], [1, 8]]), q=q)
        p.dma(TB8[:, 8:16], bass.AP(H["ssd_a_log"], l * 8, [[0, 128], [1, 8]]), q=q)
        p.act(TB8[:, 8:16], TB8[:, 8:16], AF.Exp)
        p.ts(TB8[:, 8:16], TB8[:, 8:16], -1.0, ALU.mult)
        for pr in range(4):
            for hl in range(2):
                p.dma(Dcol[hl * 64:(hl + 1) * 64, pr:pr + 1],
                      bass.AP(H["ssd_d"], l * 8 + 2 * pr + hl, [[0, 64], [1, 1]]), q=q)
        p.dma(w2s[:], Wt["rwkv_w2"][l], q=q)
        p.dma(a2s[:], Wt["rwkv_a2"][l], q=q)
        p.dma(g2s[:], Wt["rwkv_g2"][l], q=q)
        gtmp_ = scr.alloc(512)
        p.dma(gtmp_(v3(gtmp_.a(), 2)), Wt["s5_glu_w"][l].rearrange("(k p) n -> p k n", p=128), q=q)
        for k_ in range(2):
            p.act(R(gluw[:, k_ * 256:(k_ + 1) * 256]), gtmp_.f(0, 128, k_ * 256, (k_ + 1) * 256), AF.Copy)
        ada16 = l > 0
        awb = [scr.alloc16(8 * 256), scr.alloc16(8 * 256)] if ada16 else [scr.alloc(8 * 256), scr.alloc(8 * 256)]
        cTa = cTv if ada16 else cTv32

        def ada_load(cbk):
            asrc = WB[("ada_w", l)] if ada16 else Wt["ada_w"][l]
            src = asrc[:, cbk * 256:(cbk + 1) * 256].rearrange("(k p) n -> p k n", p=128)
            p.dma(awb[cbk % 2](v3(awb[cbk % 2].a(), 8)), src)

        ada_load(0)
        for cbk in range(24):
            if cbk + 1 < 24:
                ada_load(cbk + 1)
            ab = awb[cbk % 2]
            wv = v3(ab.a(), 8)
            for m in range(2):
                ps = p.bank()
                for k in range(8):
                    p.mm(ps[:, 0:17], ab(wv[:, k, m * 128:(m + 1) * 128]), cTa[:, k, :], start=(k == 0), stop=(k == 7))
                j = cbk * 2 + m
                p.ts(modv[:, j, :], ps[:, 0:17], FC[:, j:j + 1], ALU.add)
        for which, (gq, scq) in enumerate([(48, 8), (56, 32)]):
            tmp = scr.alloc(8 * 17)
            p.ts(tmp.f(), modT[:, scq * 17:(scq + 8) * 17], 1.0, ALU.add)
            p.tt(gsv[:, which], tmp(v3(tmp.a(), 8)), bc_last(FC[:, gq:gq + 8], 17), ALU.mult)
        s5_consts(l)
        scr.release(m0)

    def s5_consts(l):
        m0 = scr.mark()
        S = v3(s5sm[:], 24)
        lre, lim = FC[:, 112:120], FC[:, 120:128]
        q = "sync"
        for glh in range(2):
            p.dma(s5sm[glh * 64:(glh + 1) * 64, 0:8], bass.AP(H["s5_log_dt"], l * 16 + glh, [[0, 64], [2, 8]]), q=q)
        sl = lambda i: s5sm[:, i * 8:(i + 1) * 8]
        p.act(sl(0), sl(0), AF.Exp)
        p.tt(sl(1), lre, sl(0), ALU.mult)
        p.act(sl(1), sl(1), AF.Exp)
        p.tt(sl(2), lim, sl(0), ALU.mult)

        def sin_of(dst, src, shift):
            t = scr.alloc(8)
            ti = scr.alloc(8)
            p.ts(t.f(), src, float(shift), ALU.add)
            p.ts(dst, t.f(), 1.0 / TWO_PI, ALU.mult)
            tii = ti(ti.a().bitcast(mybir.dt.int32))
            p.copy(tii, dst)
            p.copy(dst, tii)
            p.stt(dst, dst, -TWO_PI, t.f(), ALU.mult, ALU.add)
            p.ts(t.f(), dst, float(np.pi), ALU.is_gt)
            p.stt(dst, t.f(), -TWO_PI, dst, ALU.mult, ALU.add)
            p.ts(t.f(), dst, float(-np.pi), ALU.is_lt)
            p.stt(dst, t.f(), TWO_PI, dst, ALU.mult, ALU.add)
            p.act(dst, dst, AF.Sin)

        sin_of(sl(3), sl(2), 0.0)
        sin_of(sl(4), sl(2), np.pi / 2)
        p.tt(sl(5), sl(1), sl(4), ALU.mult)
        p.tt(sl(6), sl(1), sl(3), ALU.mult)
        p.tt(sl(7), lre, lre, ALU.mult)
        p.tt(sl(8), lim, lim, ALU.mult)
        p.tt(sl(7), sl(7), sl(8), ALU.add)
        p.recip(sl(7), sl(7))
        p.ts(sl(8), sl(5), -1.0, ALU.add)
        p.tt(sl(9), sl(8), lre, ALU.mult)
        p.tt(sl(10), sl(6), lim, ALU.mult)
        p.tt(sl(9), sl(9), sl(10), ALU.add)
        p.tt(sl(9), sl(9), sl(7), ALU.mult)
        p.tt(sl(10), sl(6), lre, ALU.mult)
        p.tt(sl(11), sl(8), lim, ALU.mult)
        p.tt(sl(10), sl(10), sl(11), ALU.subtract)
        p.tt(sl(10), sl(10), sl(7), ALU.mult)
        p.tt(sl(11), sl(1), sl(1), ALU.mult)
        p.recip(sl(11), sl(11))
        p.tt(sl(12), sl(5), sl(11), ALU.mult)
        p.tt(sl(13), sl(6), sl(11), ALU.mult)
        p.ts(sl(13), sl(13), -1.0, ALU.mult)
        bre = scr.alloc(128)
        bim = scr.alloc(128)
        p.dma(bre(v3(bre.a(), 8)), Wt["s5_b_re"][l].rearrange("(j a) q c -> (a q) j c", a=2), q=q)
        p.dma(bim(v3(bim.a(), 8)), Wt["s5_b_im"][l].rearrange("(j a) q c -> (a q) j c", a=2), q=q)
        t1 = scr.alloc(128)
        t2 = scr.alloc(128)
        bbr = scr.alloc(128)
        bbi = scr.alloc(128)
        qre3, qim3 = bc_last(sl(9), 16), bc_last(sl(10), 16)
        p.tt(t1(v3(t1.a(), 8)), bre(v3(bre.a(), 8)), qre3, ALU.mult)
        p.tt(t2(v3(t2.a(), 8)), bim(v3(bim.a(), 8)), qim3, ALU.mult)
        p.tt(bbr.f(), t1.f(), t2.f(), ALU.subtract)
        p.tt(t1(v3(t1.a(), 8)), bim(v3(bim.a(), 8)), qre3, ALU.mult)
        p.tt(t2(v3(t2.a(), 8)), bre(v3(bre.a(), 8)), qim3, ALU.mult)
        p.tt(bbi.f(), t1.f(), t2.f(), ALU.add)
        msk = v3(K("s5msk"), 8)
        for ri, bb in enumerate([bbr, bbi]):
            X = scr.alloc(1024)
            Xv = v4(X.a(), 8, 8)
            p.tt(X(Xv), bb(bc_mid(v3(bb.a(), 8), 2, 8)), bc_last(msk, 16), ALU.mult)
            for h in range(2):
                ps = p.bank()
                for jj in range(4):
                    j = 4 * h + jj
                    p.tr(ps[:, jj * 128:(jj + 1) * 128], X.f(0, 128, j * 128, (j + 1) * 128), ident)
                p.copy(R(BBt[ri][:, h * 512:(h + 1) * 512]), ps[:, :])
        mz = K("s5mz")
        for ri, nm in enumerate(["s5_c_re", "s5_c_im"]):
            for h in range(2):
                cn = scr.alloc(64)
                p.dma(cn.f(), Wt[nm][l, 8 * h:8 * h + 8].rearrange("g c q -> (g c) q"), q=q)
                Z = scr.alloc(512)
                p.tt(Z(v3(Z.a(), 8)), cn(bc_mid(cn.a(), 1, 8)), bc_last(mz, 64), ALU.mult)
                ps = p.bank()
                for jj in range(4):
                    p.tr(ps[:, jj * 128:(jj + 1) * 128], Z.f(0, 128, jj * 128, (jj + 1) * 128), ident)
                if ri == 0:
                    p.copy(R(Cbd[ri][:, h * 512:(h + 1) * 512]), ps[:, :])
                else:
                    p.ts(R(Cbd[ri][:, h * 512:(h + 1) * 512]), ps[:, :], -1.0, ALU.mult)
        Qf = [scr.alloc(1024), scr.alloc(1024)]
        tA = scr.alloc(512)
        tB = scr.alloc(512)

        def powers(dre, dim, bre_, bim_, wr):
            p.copy(wr[0](dre[:, :, 0:1]), bre_.unsqueeze(2))
            p.copy(wr[1](dim[:, :, 0:1]), bim_.unsqueeze(2))
            n = 1
            while n < 128:
                cr = dre[:, :, n - 1:n].to_broadcast([128, 8, n])
                ci = dim[:, :, n - 1:n].to_broadcast([128, 8, n])
                a_ = v3(tA.a(0, 128, 0, 8 * n), 8)
                b_ = v3(tB.a(0, 128, 0, 8 * n), 8)
                p.tt(tA(a_), wr[0](dre[:, :, 0:n]), wr[0](cr), ALU.mult)
                p.tt(tB(b_), wr[1](dim[:, :, 0:n]), wr[1](ci), ALU.mult, eng="gpsimd")
                p.tt(wr[0](dre[:, :, n:2 * n]), tA(a_), tB(b_), ALU.subtract)
                p.tt(tA(a_), wr[0](dre[:, :, 0:n]), wr[1](ci), ALU.mult)
                p.tt(tB(b_), wr[1](dim[:, :, 0:n]), wr[0](cr), ALU.mult, eng="gpsimd")
                p.tt(wr[1](dim[:, :, n:2 * n]), tA(a_), tB(b_), ALU.add)
                n *= 2

        ident_w = [lambda x: x, lambda x: x]
        powers(v3(Ptab[0][:], 8), v3(Ptab[1][:], 8), sl(5), sl(6), ident_w)
        powers(v3(Qf[0].a(), 8), v3(Qf[1].a(), 8), sl(12), sl(13), [Qf[0], Qf[1]])
        for ri in range(2):
            for h in range(2):
                ps = p.bank()
                for jj in range(4):
                    j = 4 * h + jj
                    p.tr(ps[:, jj * 128:(jj + 1) * 128], Qf[ri].f(0, 128, j * 128, (j + 1) * 128), ident)
                p.copy(Qtab[ri][:, h * 512:(h + 1) * 512], ps[:, :], eng="scalar")
        scr.release(m0)

    def s5_sample_qtab():
        m0 = scr.mark()
        sl = lambda i: s5sm[:, i * 8:(i + 1) * 8]
        q4 = [scr.alloc(32), scr.alloc(32)]
        tA = scr.alloc(32)
        tB = scr.alloc(32)
        qv = [v3(q4[0].a(), 8), v3(q4[1].a(), 8)]
        p.copy(q4[0](qv[0][:, :, 0:1]), sl(12).unsqueeze(2))
        p.copy(q4[1](qv[1][:, :, 0:1]), sl(13).unsqueeze(2))
        n = 1
        while n < 4:
            cr = qv[0][:, :, n - 1:n].to_broadcast([128, 8, n])
            ci = qv[1][:, :, n - 1:n].to_broadcast([128, 8, n])
            a_ = v3(tA.a(0, 128, 0, 8 * n), 8)
            b_ = v3(tB.a(0, 128, 0, 8 * n), 8)
            p.tt(tA(a_), q4[0](qv[0][:, :, 0:n]), q4[0](cr), ALU.mult)
            p.tt(tB(b_), q4[1](qv[1][:, :, 0:n]), q4[1](ci), ALU.mult)
            p.tt(q4[0](qv[0][:, :, n:2 * n]), tA(a_), tB(b_), ALU.subtract)
            p.tt(tA(a_), q4[0](qv[0][:, :, 0:n]), q4[1](ci), ALU.mult)
            p.tt(tB(b_), q4[1](qv[1][:, :, 0:n]), q4[0](cr), ALU.mult)
            p.tt(q4[1](qv[1][:, :, n:2 * n]), tA(a_), tB(b_), ALU.add)
            n *= 2
        for ri in range(2):
            Qs = scr.alloc(512)
            p.copy(Qs(v4(Qs.a(), 8, 16)), q4[ri](bc_mid(qv[ri], 2, 16)))
            for h in range(2):
                ps = p.bank()
                for jj in range(4):
                    j = 4 * h + jj
                    p.tr(ps[0:64, jj * 128:(jj + 1) * 128], Qs.f(0, 128, j * 64, (j + 1) * 64), ident)
                p.copy(Qtab[ri][0:64, h * 512:(h + 1) * 512], ps[0:64, :])
        scr.release(m0)

    def tile_fwd(l, T):
        kind, TT, nseq, L, chunks = T["kind"], T["TT"], T["nseq"], T["L"], T["chunks"]
        tag = T["tag"]
        last_tile = T["last"]
        sq0 = 0 if kind == "P" else 1
        tok0 = T["tok0"]
        sfx = "S" if kind == "S" else "P"
        incl, strict, strictT, mb, same = (K("incl" + sfx), K("strict" + sfx), K("strict" + sfx + "T"),
                                           K("mb" + sfx), K("same" + sfx) if kind == "S" else K("ones"))
        ones = K("ones")
        m_tile = scr.mark()
        xT = scr.alloc(8 * TT)
        xv = v3(xT.a(), 8)
        mixA = scr.alloc16(6 * TT)
        mixv = v3(mixA.a(), 6)
        mixR = scr.alloc16(4 * TT)
        mixRv = v3(mixR.a(0, 64), 4)
        E1 = 1 + L
        E3 = 3 + L
        zs = scr.alloc(4 * TT)
        zsv = v3(zs.a(), 4)
        xext = scr.alloc(8 * nseq * E3)
        xev = v4(xext.a(), 8, nseq)
        rkvx = scr.alloc(12 * nseq * E1)
        rkv = v4(rkvx.a(0, 64), 12, nseq)
        wax = scr.alloc(2 * nseq * E1)
        wav = v4(wax.a(0, 64), 2, nseq)
        glx = scr.alloc(nseq * E1)
        glv = v3(glx.a(), nseq)
        mR_tile = scrR.mark()
        us5 = scrR.alloc(2 * TT)
        usv = v3(us5.a(), 2)
        dtraw = scr.alloc(8 * len(chunks))
        m_h = scr.mark()

        if l == 0:
            for (c0, Tc) in chunks:
                mm_ = scr.mark()
                xin = scr.alloc(1024)
                src = (xp if kind == "P" else xs)[tok0 + c0 - (0 if kind == "P" else SP):tok0 + c0 - (0 if kind == "P" else SP) + Tc, :]
                p.dma(xin.f(0, Tc), src)
                for h2 in range(2):
                    ps = p.bank()
                    for i in range(4):
                        p.tr(ps[:, i * Tc:(i + 1) * Tc], xin.f(0, Tc, (4 * h2 + i) * 128, (4 * h2 + i + 1) * 128), ident[0:Tc, 0:Tc])
                    p.copy(xT(xv[:, 4 * h2:4 * h2 + 4, c0:c0 + Tc]), v3(ps[:, 0:4 * Tc], 4), eng=("vector" if h2 == 0 else "scalar"))
                scr.release(mm_)
        else:
            p.dma(xT(xv), xmid[:, :, tok0:tok0 + TT])

        def rmsnorm_mod(dst, which):
            mm_ = scr.mark()
            mr_ = scrR.mark()
            sq = scrR.alloc(8 * TT)
            p.act(R(sq.f()), xT.f(), AF.Square)
            ps = p.bank()
            for i in range(8):
                p.mm(ps[:, 0:TT], KR["ones"], sq.f(0, 128, i * TT, (i + 1) * TT), start=(i == 0), stop=(i == 7), r=True)
            scrR.release(mr_)
            rstd = scr.alloc(TT)
            p.rsqrt(rstd.f(), ps[:, 0:TT], scale=1.0 / 1024, bias=NORM_EPS)
            t = scr.alloc(8 * TT)
            p.tt(t(v3(t.a(), 8)), xT(xv), rstd(bc_mid(rstd.a(), 1, 8)), ALU.mult)
            gs = gsv[:, which, :, sq0:sq0 + nseq]
            shq = 0 if which == 0 else 24
            sh = modv[:, shq:shq + 8, sq0:sq0 + nseq]
            t4 = v4(t.a(), 8, nseq)
            p.tt(t(t4), t(t4), bc_last(gs, L), ALU.mult, eng="gpsimd")
            p.tt(dst(v4(dst.a(), 8, nseq)), t(t4), bc_last(sh, L), ALU.add)
            scr.release(mm_)

        hT = scr.alloc16(8 * TT)
        hv = v3(hT.a(), 8)
        rmsnorm_mod(hT, 0)

        if kind == "P":
            p.copy(xext(xev[:, :, 0, 0:3]), v3(ccar[:], 8))
            p.copy(rkvx(rkv[:, :, 0, 0:1]), scar[:, 0:12].unsqueeze(2))
            p.copy(wax(wav[:, :, 0, 0:1]), scar[:, 12:14].unsqueeze(2))
            p.copy(glx(glv[:, 0, 0:1]), gcar[:, 0:1])
        else:
            mm_ = scr.mark()
            cst = scr.alloc(1024)
            p.dma(cst.f(0, 48), st_conv[l], q="gpsimd")
            for i in range(8):
                ps = p.bank()
                p.tr(ps[:, 0:48], cst.f(0, 48, i * 128, (i + 1) * 128), ident[0:48, 0:48])
                p.copy(xext(xev[:, i, :, 0:3]), v3(ps[:, 0:48], 16))
            sst = scr.alloc(1024)
            p.dma(sst.f(0, 16), st_shift[l], q="gpsimd")
            ps = p.bank()
            for idx in range(14):
                p.tr(ps[0:64, idx * 16:(idx + 1) * 16], sst.f(0, 16, idx * 64, (idx + 1) * 64), ident[0:16, 0:16])
            p.copy(rkvx(rkv[:, :, :, 0]), v3(ps[0:64, 0:192], 12))
            p.copy(wax(wav[:, :, :, 0]), v3(ps[0:64, 192:224], 2))
            ps = p.bank()
            p.tr(ps[:, 0:16], sst.f(0, 16, 896, 1024), ident[0:16, 0:16])
            p.copy(glx(glv[:, :, 0]), ps[:, 0:16])
            scr.release(mm_)

        def proj_tile(wv, lo, M, dst, func=AF.Copy, parts=128, rnd=False):
            ps = p.bank()
            for k in range(8):
                p.mm(ps[0:M, 0:TT], wv[:, k, lo:lo + M], hT(hv[:, k, :]), start=(k == 0), stop=(k == 7))
            p.act(R(dst) if rnd else dst, v3(ps[0:M, 0:TT], nseq) if dst_is3(dst) else ps[0:M, 0:TT], func)

        def dst_is3(dst):
            return len(_ap(dst).shape) == 3

        for bi, nm in enumerate(["z0", "z1"]):
            wv, _ = wget(("in", l, tag, nm))
            for m in range(2):
                proj_tile(wv, m * 128, 128, zs(zsv[:, 2 * bi + m, :]), AF.Silu)
        for bi in range(4):
            wv, _ = wget(("in", l, tag, "x%d" % bi))
            for m in range(2):
                proj_tile(wv, m * 128, 128, xext(xev[:, 2 * bi + m, :, 3:]))
        wv, _ = wget(("in", l, tag, "dt"))
        for ci, (c0, Tc) in enumerate(chunks):
            ps = p.bank()
            for k in range(8):
                p.mm(ps[0:Tc, 0:8], hT(hv[:, k, c0:c0 + Tc]), wv[:, k, 0:8], start=(k == 0), stop=(k == 7))
            p.tt(dtraw.f(0, Tc, ci * 8, ci * 8 + 8), ps[0:Tc, 0:8], TB8[0:Tc, 0:8], ALU.add)
        for wi, nm in enumerate(["r", "k", "v"]):
            wv, _ = wget(("in", l, tag, nm))
            for h in range(4):
                proj_tile(wv, h * 64, 64, rkvx(rkv[:, 4 * wi + h, :, 1:]))
        wv, _ = wget(("in", l, tag, "wag"))
        proj_tile(wv, 0, 64, wax(wav[:, 0, :, 1:]))
        proj_tile(wv, 64, 64, wax(wav[:, 1, :, 1:]))
        proj_tile(wv, 128, 128, glx(glv[:, :, 1:]))
        wv, _ = wget(("in", l, tag, "s5"))
        for m in range(2):
            proj_tile(wv, m * 128, 128, us5(usv[:, m, :]), rnd=True)
        scr.release(m_h)

        if kind == "P":
            p.copy(v3(ccar[:], 8), xext(xev[:, :, 0, L:L + 3]))
            p.copy(scar[:, 0:12].unsqueeze(2), rkvx(rkv[:, :, 0, L:L + 1]))
            p.copy(scar[:, 12:14].unsqueeze(2), wax(wav[:, :, 0, L:L + 1]))
            p.copy(gcar[:, 0:1], glx(glv[:, 0, L:L + 1]))
        if last_tile:
            mm_ = scr.mark()
            n3 = nseq * 3
            ctmp = scr.alloc(8 * n3)
            p.copy(ctmp(v4(ctmp.a(), 8, nseq)), xext(xev[:, :, :, L:L + 3]))
            crow = scr.alloc(1024)
            for h2 in range(2):
                ps = p.bank()
                for i in range(4):
                    p.tr(ps[0:n3, i * 128:(i + 1) * 128], ctmp.f(0, 128, (4 * h2 + i) * n3, (4 * h2 + i + 1) * n3), ident)
                p.copy(crow.f(0, n3, h2 * 512, (h2 + 1) * 512), ps[0:n3, :])
            p.dma(o_conv[kind][l], crow.f(0, n3), q="gpsimd")
            stmp = scr.alloc(14 * nseq)
            p.copy(stmp(v3(stmp.a(0, 64, 0, 12 * nseq), 12)), rkvx(rkv[:, :, :, L]))
            p.copy(stmp(v3(stmp.a(0, 64, 12 * nseq, 14 * nseq), 2)), wax(wav[:, :, :, L]))
            gtmp = scr.alloc(nseq)
            p.copy(gtmp.f(), glx(glv[:, :, L]))
            srow = scr.alloc(1024)
            for h2 in range(2):
                ps = p.bank()
                n_idx = 8 if h2 == 0 else 6
                for ii in range(n_idx):
                    idx = 8 * h2 + ii
                    p.tr(ps[0:nseq, ii * 64:(ii + 1) * 64], stmp.f(0, 64, idx * nseq, (idx + 1) * nseq), ident[0:64, 0:64])
                if h2 == 1:
                    p.tr(ps[0:nseq, 384:512], gtmp.f(), ident)
                p.copy(srow.f(0, nseq, h2 * 512, (h2 + 1) * 512), ps[0:nseq, :])
            p.dma(o_shift[kind][l], srow.f(0, nseq), q="gpsimd")
            scr.release(mm_)

        for ci, (c0, Tc) in enumerate(chunks):
            nsc = nseq
            ssd_chunk(l, T, ci, c0, Tc, dict(xev=xev, xext=xext, zs=zs, zsv=zsv, dtraw=dtraw, mixA=mixA, mixv=mixv,
                                             incl=incl, mb=mb, same=same, ones=ones))
            rwkv_chunk(l, T, ci, c0, Tc, dict(rkvx=rkvx, rkv=rkv, wax=wax, wav=wav, glx=glx, glv=glv, mixR=mixR,
                                              mixRv=mixRv, incl=incl, strict=strict, strictT=strictT, ones=ones))
            s5_chunk(l, T, ci, c0, Tc, dict(us5=us5, usv=usv, mixA=mixA, mixv=mixv, incl=incl))

        def resid(ps, i, gq):
            g = modv[:, gq + i, sq0:sq0 + nseq]
            if nseq == 1:
                p.stt(xT(xv[:, i, :]), ps[:, 0:TT], g, xT(xv[:, i, :]), ALU.mult, ALU.add)
            else:
                mm_ = scr.mark()
                t = scr.alloc(TT)
                p.tt(t(v3(t.a(), nseq)), v3(ps[:, 0:TT], nseq), bc_last(g, L), ALU.mult)
                p.tt(xT(xv[:, i, :]), xT(xv[:, i, :]), t.f(), ALU.add, eng="gpsimd")
                scr.release(mm_)

        for ob in range(4):
            wv, wr = wget(("out", l, tag, ob))
            for m in range(2):
                ps = p.bank()
                for k in range(4):
                    p.mm(ps[:, 0:TT], wv[:, k, m * 128:(m + 1) * 128], mixA(mixv[:, k, :]), start=(k == 0), stop=False)
                for h in range(4):
                    p.mm(ps[:, 0:TT], wr[:, h, m * 128:(m + 1) * 128], mixR(mixRv[:, h, :]), start=False, stop=False)
                for k in range(2):
                    p.mm(ps[:, 0:TT], wv[:, 4 + k, m * 128:(m + 1) * 128], mixA(mixv[:, 4 + k, :]), start=False, stop=(k == 1))
                resid(ps, 2 * ob + m, 16)
        dbgout("x1_%d_%s" % (l, tag), xT(xv), [128, 8, TT])

        m_mlp = scr.mark()
        h2T = scr.alloc16(8 * TT)
        h2v = v3(h2T.a(), 8)
        rmsnorm_mod(h2T, 1)
        hid = scr.alloc16(32 * TT)
        hidv = v3(hid.a(), 32)
        rl = [scr.alloc(TT), scr.alloc(TT)]
        for c1 in range(16):
            wv, _ = wget(("w1", l, tag, c1))
            for m in range(2):
                ps = p.bank()
                for k in range(8):
                    p.mm(ps[:, 0:TT], wv[:, k, m * 128:(m + 1) * 128], h2T(h2v[:, k, :]), start=(k == 0), stop=(k == 7))
                j = 2 * c1 + m
                p.act(rl[m].f(), ps[:, 0:TT], AF.Relu)
                p.tt(hid(hidv[:, j, :]), rl[m].f(), rl[m].f(), ALU.mult, eng=("vector" if m == 0 else "gpsimd"))
        for ob in range(4):
            pss = [p.bank(), p.bank()]
            for kg in range(4):
                wv, _ = wget(("w2", l, tag, ob, kg))
                for m in range(2):
                    for k in range(8):
                        p.mm(pss[m][:, 0:TT], wv[:, k, m * 128:(m + 1) * 128], hid(hidv[:, kg * 8 + k, :]),
                             start=(kg == 0 and k == 0), stop=(kg == 3 and k == 7))
            for m in range(2):
                resid(pss[m], 2 * ob + m, 40)
        scr.release(m_mlp)
        dbgout("x2_%d_%s" % (l, tag), xT(xv), [128, 8, TT])

        if l < depth - 1:
            p.dma(xmid[:, :, tok0:tok0 + TT], xT(xv), q="gpsimd")
        else:
            mm_ = scr.mark()
            mr_ = scrR.mark()
            sq = scrR.alloc(8 * TT)
            p.act(R(sq.f()), xT.f(), AF.Square)
            ps = p.bank()
            for i in range(8):
                p.mm(ps[:, 0:TT], KR["ones"], sq.f(0, 128, i * TT, (i + 1) * TT), start=(i == 0), stop=(i == 7), r=True)
            scrR.release(mr_)
            rstd = scr.alloc(TT)
            p.rsqrt(rstd.f(), ps[:, 0:TT], scale=1.0 / 1024, bias=NORM_EPS)
            t = scr.alloc(8 * TT)
            tv = v3(t.a(), 8)
            p.tt(t(tv), xT(xv), rstd(bc_mid(rstd.a(), 1, 8)), ALU.mult)
            p.tt(t(tv), t(tv), bc_last(fincol[:, 0:8], TT), ALU.mult, eng="gpsimd")
            orows = [scr.alloc(1024), scr.alloc(1024)]
            for cix, (c0, Tc) in enumerate(chunks):
                orow = orows[cix % 2]
                for h2 in range(2):
                    ps = p.bank()
                    for i in range(4):
                        p.tr(ps[0:Tc, i * 128:(i + 1) * 128], t(tv[:, 4 * h2 + i, c0:c0 + Tc]), ident)
                    p.copy(orow.f(0, Tc, h2 * 512, (h2 + 1) * 512), ps[0:Tc, :], eng=("vector" if h2 == 0 else "scalar"))
                if kind == "P":
                    p.dma(yp[tok0 + c0:tok0 + c0 + Tc, :], orow.f(0, Tc), q="gpsimd")
                else:
                    p.dma(ys[c0:c0 + Tc, :], orow.f(0, Tc), q="gpsimd")
            scr.release(mm_)
        scr.release(m_tile)
        scrR.release(mR_tile)

    def ssd_chunk(l, T, ci, c0, Tc, C):
        kind, L = T["kind"], T["L"]
        nb = 1 if kind == "P" else 16
        Lc = Tc // nb
        last = T["last"] and ci == len(T["chunks"]) - 1
        xev, xext, zs, zsv, dtraw, mixA, mixv = C["xev"], C["xext"], C["zs"], C["zsv"], C["dtraw"], C["mixA"], C["mixv"]
        incl, mb, same, ones = C["incl"], C["mb"], C["same"], C["ones"]
        m0 = scr.mark()
        mR0 = scrR.mark()
        W8 = 8 * Tc
        xbc = scrR.alloc(W8)
        xb3 = v3(xbc.a(), 8)

        def fview(buf):
            return v3(buf.a(), 8) if kind == "P" else v4(buf.a(), 8, 16)

        def tap(j):
            return xev[:, :, 0, c0 + j:c0 + j + Tc] if kind == "P" else xev[:, :, :, j:j + L]

        def bcw(ap2):
            if kind == "P":
                return bc_last(ap2, Tc)
            return ap2.unsqueeze(2).unsqueeze(3).to_broadcast([128, 8, 16, 4])

        m1 = scr.mark()
        acc = scr.alloc(W8)
        tmp = scr.alloc(W8)
        p.tt(acc(fview(acc)), xext(tap(0)), bcw(FC[:, 64:72]), ALU.mult)
        for j in range(1, 4):
            p.tt(tmp(fview(tmp)), xext(tap(j)), bcw(FC[:, 64 + 8 * j:72 + 8 * j]), ALU.mult)
            p.tt(acc.f(), acc.f(), tmp.f(), ALU.add, eng="gpsimd")
        p.tt(acc(fview(acc)), acc(fview(acc)), bcw(FC[:, 96:104]), ALU.add, eng="gpsimd")
        p.act(R(xbc.f()), acc.f(), AF.Silu)
        scr.release(m1)
        dbgout("xbc_%d_%s_%d" % (l, T["tag"], ci), xbc(xb3), [128, 8, Tc])
        dt = scr.alloc(8)
        aa = scrR.alloc(8)
        acs = scr.alloc(8)
        dte = scr.alloc(8)
        p.act(dt.f(0, Tc), dtraw.f(0, Tc, ci * 8, ci * 8 + 8), AF.Exp)
        p.act(dt.f(0, Tc), dt.f(0, Tc), AF.Ln, bias=1.0)
        p.tt(R(aa.f(0, Tc)), dt.f(0, Tc), TB8[0:Tc, 8:16], ALU.mult)
        xdt = scr.alloc(512)
        xd2 = scrR.alloc(512)
        xdt16 = scr.alloc16(512)
        bmtok = scrR.alloc(256)
        inclr = KR["incl" + ("P" if kind == "P" else "S")]
        samer = KR["ones"] if kind == "P" else KR["sameS"]
        ps = p.bank()
        for i in range(4):
            p.tr(ps[0:Tc, i * 128:(i + 1) * 128], xbc(xb3[:, i, :]), ident)
        p.tt(xdt(v3(xdt.a(0, Tc), 8)), v3(ps[0:Tc, :], 8), dt(bc_last(dt.a(0, Tc), 64)), ALU.mult)
        p.act(xdt16.f(0, Tc), xdt.f(0, Tc), AF.Copy)
        ps = p.bank()
        for g in range(2):
            p.tr(ps[0:Tc, g * 128:(g + 1) * 128], xbc(xb3[:, 4 + g, :]), ident)
        p.act(R(bmtok.f(0, Tc)), ps[0:Tc, 0:256], AF.Copy)
        ps = p.bank()
        p.mm(ps[0:Tc, 0:8], inclr[0:Tc, 0:Tc], aa.f(0, Tc), r=True)
        p.copy(acs.f(0, Tc), ps[0:Tc, 0:8])
        ps = p.bank()
        p.mm(ps[0:Tc, 0:8], samer[0:Tc, 0:Tc], aa.f(0, Tc), r=True)
        p.tt(dte.f(0, Tc), ps[0:Tc, 0:8], acs.f(0, Tc), ALU.subtract)
        p.act(dte.f(0, Tc), dte.f(0, Tc), AF.Exp)
        p.tt(R(xd2(v3(xd2.a(0, Tc), 8))), xdt(v3(xdt.a(0, Tc), 8)), dte(bc_last(dte.a(0, Tc), 64)), ALU.mult, eng="gpsimd")
        R1 = scrR.alloc(W8)
        E2 = scr.alloc(W8)
        Lm = scr.alloc(W8)
        p.tt(R(R1(v3(R1.a(0, Tc), 8))), aa(bc_last(aa.a(0, Tc), Tc)), bc_mid(incl[0:Tc, 0:Tc], 1, 8), ALU.mult)
        nhalf = (W8 + 511) // 512
        hph = 512 // Tc
        for hf in range(nhalf):
            ps = p.bank()
            p.mm(ps[:, :], KR["ones"][0:Tc, 0:128], R1.f(0, Tc, hf * 512, (hf + 1) * 512), r=True)
            p.act(E2.f(0, 128, hf * 512, (hf + 1) * 512), ps[:, :], AF.Exp)
            p.tt(Lm(v3(Lm.a(0, Tc, hf * 512, (hf + 1) * 512), hph)), v3(ps[0:Tc, :], hph),
                 acs(bc_last(acs.a(0, Tc, hf * hph, (hf + 1) * hph), Tc)), ALU.subtract)
        p.tt(Lm(v3(Lm.a(0, Tc), 8)), Lm(v3(Lm.a(0, Tc), 8)), bc_mid(mb[0:Tc, 0:Tc], 1, 8), ALU.add, eng="gpsimd")
        p.act(Lm.f(0, Tc), Lm.f(0, Tc), AF.Exp)
        pscb = p.bank()
        for g in range(2):
            p.mm(pscb[0:Tc, g * Tc:(g + 1) * Tc], xbc(xb3[:, 4 + g, :]), xbc(xb3[:, 6 + g, :]), r=True)
        Mm = scr.alloc16(W8)
        p.tt(Mm(v4(Mm.a(0, Tc), 2, 4)), Lm(v4(Lm.a(0, Tc), 2, 4)), bc_mid(v3(pscb[0:Tc, 0:2 * Tc], 2), 2, 4), ALU.mult)
        cmh = scr.alloc16(W8)
        p.tt(cmh(v4(cmh.a(), 2, 4)), E2(v4(E2.a(), 2, 4)), xbc(bc_mid(xb3[:, 6:8, :], 2, 4)), ALU.mult, eng="gpsimd")
        E23 = v3(E2.a(), 8)
        psY = [p.bank(hold=True) for _ in range(4)]

        def yreg(h):
            return psY[h // 2][(h % 2) * 64:(h % 2 + 1) * 64, 0:Tc]

        def state_update(stT, b):
            mm_ = scr.mark()
            mmr_ = scrR.mark()
            if nb > 1:
                bmb = scrR.alloc(256)
                p.ts(R(bmb.f(0, Tc)), bmtok.f(0, Tc), K("indS", 0, Tc, b, b + 1), ALU.mult)
            else:
                bmb = bmtok
            pS = p.bank()
            for g in range(2):
                p.mm(pS[:, g * 256:(g + 1) * 256], bmb.f(0, Tc, g * 128, (g + 1) * 128), xd2.f(0, Tc, g * 256, (g + 1) * 256), r=True)
            dec = E23[:, :, b * Lc + Lc - 1]
            tmp = scr.alloc(512)
            sv = V(v3(stT.ap, 8), stT.res)
            p.tt(tmp(v3(tmp.a(), 8)), sv, E2(bc_last(dec, 64)), ALU.mult, eng="gpsimd")
            p.tt(stT, tmp.f(), pS[:, :], ALU.add)
            scr.release(mm_)
            scrR.release(mmr_)

        if kind == "P":
            stP = V(ssdT_P[:], _norm(ssdT_P[:])[1])
            st16 = scr.alloc16(512)
            p.act(st16.f(), ssdT_P[:], AF.Copy)
            for h in range(8):
                p.mm(yreg(h), xdt16.f(0, Tc, h * 64, (h + 1) * 64), Mm.f(0, Tc, h * Tc, (h + 1) * Tc), start=True, stop=False)
                p.mm(yreg(h), st16.f(0, 128, h * 64, (h + 1) * 64), cmh.f(0, 128, h * Tc, (h + 1) * Tc), start=False, stop=True)
            state_update(stP, 0)
            if last:
                mm_ = scr.mark()
                orow = scr.alloc(512)
                ps = p.bank()
                for i in range(4):
                    p.tr(ps[:, i * 128:(i + 1) * 128], ssdT_P[:, i * 128:(i + 1) * 128], ident)
                p.copy(orow.f(), ps[:, :])
                p.dma(o_ssd["P"][l, 0].rearrange("(i q) n -> q i n", q=128), orow(v3(orow.a(), 4)), q="gpsimd")
                scr.release(mm_)
        else:
            for h in range(8):
                p.mm(yreg(h), xdt16.f(0, Tc, h * 64, (h + 1) * 64), Mm.f(0, Tc, h * Tc, (h + 1) * Tc), start=True, stop=False)
            sbufs = [(scr.alloc(2048), scr.alloc(2048), scr.alloc16(2048)) for _ in range(2)]

            def sload(bg_):
                nat_ = sbufs[bg_ % 2][0]
                p.dma(nat_(v4(nat_.a(), 4, 4)), st_ssd[l, 4 * bg_:4 * bg_ + 4].rearrange("b (i q) n -> q b i n", q=128), q="sync")

            sload(0)
            for bg in range(4):
                mm_ = scr.mark()
                nat, stT, st16 = sbufs[bg % 2]
                if bg + 1 < 4:
                    sload(bg + 1)
                for bl in range(4):
                    ps = p.bank()
                    for i in range(4):
                        p.tr(ps[:, i * 128:(i + 1) * 128], nat.f(0, 128, (bl * 4 + i) * 128, (bl * 4 + i + 1) * 128), ident)
                    p.copy(stT.f(0, 128, bl * 512, (bl + 1) * 512), ps[:, :], eng=("scalar" if bl % 2 else "vector"))
                p.act(st16.f(), stT.f(), AF.Copy)
                for bl in range(4):
                    b = 4 * bg + bl
                    for h in range(8):
                        p.mm(psY[h // 2][(h % 2) * 64:(h % 2 + 1) * 64, b * Lc:(b + 1) * Lc],
                             st16.f(0, 128, bl * 512 + h * 64, bl * 512 + (h + 1) * 64),
                             cmh.f(0, 128, h * Tc + b * Lc, h * Tc + (b + 1) * Lc), start=False, stop=(b == 15))
                for bl in range(4):
                    state_update(stT.f(0, 128, bl * 512, (bl + 1) * 512), 4 * bg + bl)
                for bl in range(4):
                    ps = p.bank()
                    for i in range(4):
                        p.tr(ps[:, i * 128:(i + 1) * 128], stT.f(0, 128, bl * 512 + i * 128, bl * 512 + (i + 1) * 128), ident)
                    p.copy(nat.f(0, 128, bl * 512, (bl + 1) * 512), ps[:, :], eng=("scalar" if bl % 2 else "vector"))
                p.dma(o_ssd["S"][l, 4 * bg:4 * bg + 4].rearrange("b (i q) n -> q b i n", q=128), nat(v4(nat.a(), 4, 4)), q="gpsimd")
                scr.release(mm_)
        y3 = scr.alloc(4 * Tc)
        sq = scrR.alloc(4 * Tc)
        y33 = v3(y3.a(), 4)
        for pr in range(4):
            p.stt(y3(y33[:, pr, :]), xbc(xb3[:, pr, :]), Dcol[:, pr:pr + 1], psY[pr][:, 0:Tc], ALU.mult, ALU.add)
            p.unhold(psY[pr])
        p.tt(y3(y33), y3(y33), zs(zsv[:, :, c0:c0 + Tc]), ALU.mult, eng="gpsimd")
        p.act(R(sq.f()), y3.f(), AF.Square)
        ps = p.bank()
        for pr in range(4):
            p.mm(ps[:, 0:Tc], KR["ones"], sq.f(0, 128, pr * Tc, (pr + 1) * Tc), start=(pr == 0), stop=(pr == 3), r=True)
        rstd = scr.alloc(Tc)
        p.rsqrt(rstd.f(), ps[:, 0:Tc], scale=1.0 / 512, bias=NORM_EPS)
        for pr in range(4):
            p.stt(mixA(mixv[:, pr, c0:c0 + Tc]), y3(y33[:, pr, :]), FC[:, 104 + pr:105 + pr], rstd.f(), ALU.mult, ALU.mult)
        scr.release(m0)
        scrR.release(mR0)

    def rwkv_chunk(l, T, ci, c0, Tc, C):
        kind, L = T["kind"], T["L"]
        nb = 1 if kind == "P" else 16
        Lc = Tc // nb
        last = T["last"] and ci == len(T["chunks"]) - 1
        rkvx, rkv, wax, wav, glx, glv, mixR, mixRv = (C["rkvx"], C["rkv"], C["wax"], C["wav"], C["glx"], C["glv"],
                                                      C["mixR"], C["mixRv"])
        incl, strict, strictT, ones = C["incl"], C["strict"], C["strictT"], C["ones"]
        m0 = scr.mark()
        mR0 = scrR.mark()
        W4 = 4 * Tc

        def fv(buf):
            return v3(buf.a(0, 64), 4) if kind == "P" else v4(buf.a(0, 64), 4, 16)

        def f2(buf):
            return buf.f(0, 64)

        def cur(i0, n):
            return rkv[:, i0:i0 + n, 0, 1 + c0:1 + c0 + Tc] if kind == "P" else rkv[:, i0:i0 + n, :, 1:1 + L]

        def prv(i0, n):
            return rkv[:, i0:i0 + n, 0, c0:c0 + Tc] if kind == "P" else rkv[:, i0:i0 + n, :, 0:L]

        def bc4(ap2):
            if kind == "P":
                return bc_last(ap2, Tc)
            return ap2.unsqueeze(2).unsqueeze(3).to_broadcast([64, 4, 16, 4])

        gT = scr.alloc(W4)
        bonus = scr.alloc(W4)
        vT = scr.alloc(W4)
        epos = scr.alloc(W4)
        at, rt, bt, kt = [scrR.alloc(W4) for _ in range(4)]
        kb, bb = [scr.alloc(W4) for _ in range(2)]
        m1 = scr.mark()
        mR1 = scrR.mark()
        sqr = scrR.alloc(W4)
        rkr = scrR.alloc(W4)
        rT = scr.alloc(W4)
        kT = scr.alloc(W4)
        lw = scr.alloc(W4)
        aG = scr.alloc(W4)
        t1 = scr.alloc(W4)
        t2 = scr.alloc(W4)
        kkn = scr.alloc(W4)
        for wi, dst in enumerate([rT, kT, vT]):
            p.tt(t1(fv(t1)), rkvx(prv(4 * wi, 4)), rkvx(cur(4 * wi, 4)), ALU.subtract)
            p.tt(t1(fv(t1)), t1(fv(t1)), bc4(F64[:, 4 * wi:4 * wi + 4]), ALU.mult, eng="gpsimd")
            p.tt(dst(fv(dst)), t1(fv(t1)), rkvx(cur(4 * wi, 4)), ALU.add)

        def mix1(xv_, i, mucol, dst, parts):
            if kind == "P":
                c_, p_ = xv_[:, i, 0, 1 + c0:1 + c0 + Tc], xv_[:, i, 0, c0:c0 + Tc]
                d_ = dst.a(0, parts, 0, Tc)
            else:
                c_, p_ = xv_[:, i, :, 1:1 + L], xv_[:, i, :, 0:L]
                d_ = v3(dst.a(0, parts, 0, Tc), 16)
            return c_, p_, d_

        wlm = scr.alloc(Tc)
        alm = scr.alloc(Tc)
        glm = scr.alloc(Tc)
        for i, (dst, mc) in enumerate([(wlm, F64[:, 12:13]), (alm, F64[:, 13:14])]):
            c_, p_, d_ = mix1(wav, i, mc, dst, 64)
            p.tt(dst(d_), wax(p_), wax(c_), ALU.subtract)
            p.stt(dst(d_), dst(d_), mc, wax(c_), ALU.mult, ALU.add)
        if kind == "P":
            c_, p_ = glv[:, 0, 1 + c0:1 + c0 + Tc], glv[:, 0, c0:c0 + Tc]
            d_ = glm.a(0, 128, 0, Tc)
        else:
            c_, p_ = glv[:, :, 1:1 + L], glv[:, :, 0:L]
            d_ = v3(glm.a(0, 128, 0, Tc), 16)
        p.tt(glm(d_), glx(p_), glx(c_), ALU.subtract)
        p.stt(glm(d_), glm(d_), FCb[:, 0:1], glx(c_), ALU.mult, ALU.add)
        p.act(wlm.f(0, 64), wlm.f(0, 64), AF.Tanh)
        ps = p.bank()
        for h in range(4):
            p.mm(ps[0:64, h * Tc:(h + 1) * Tc], w2s[:, h * 64:(h + 1) * 64], wlm.f(0, 64))
        for h in range(4):
            p.act(lw.f(0, 64, h * Tc, (h + 1) * Tc), ps[0:64, h * Tc:(h + 1) * Tc], AF.Sigmoid, bias=F64[:, 14 + h:15 + h])
        p.ts(f2(lw), f2(lw), -float(np.exp(-0.5)), ALU.mult)
        ps = p.bank()
        for h in range(4):
            p.mm(ps[0:64, h * Tc:(h + 1) * Tc], a2s[:, h * 64:(h + 1) * 64], alm.f(0, 64))
        for h in range(4):
            p.act(aG.f(0, 64, h * Tc, (h + 1) * Tc), ps[0:64, h * Tc:(h + 1) * Tc], AF.Sigmoid, bias=F64[:, 18 + h:19 + h])
        p.act(glm.f(), glm.f(), AF.Sigmoid)
        ps = p.bank()
        for h in range(4):
            p.mm(ps[0:64, h * Tc:(h + 1) * Tc], g2s[:, h * 64:(h + 1) * 64], glm.f())
        p.copy(f2(gT), ps[0:64, 0:W4], eng="scalar")
        p.tt(t1(fv(t1)), kT(fv(kT)), bc4(F64[:, 22:26]), ALU.mult)
        p.tt(R(f2(sqr)), f2(t1), f2(t1), ALU.mult, eng="gpsimd")
        ps = p.bank()
        p.mm(ps[0:64, 0:W4], KR["ones"][0:64, 0:64], f2(sqr), r=True)
        p.ts(f2(t2), ps[0:64, 0:W4], 1e-18, ALU.max)
        p.rsqrt(f2(t2), f2(t2))
        p.tt(f2(kkn), f2(t1), f2(t2), ALU.mult)
        p.ts(f2(t1), f2(aG), -1.0, ALU.add)
        p.tt(t1(fv(t1)), t1(fv(t1)), bc4(F64[:, 26:30]), ALU.mult, eng="gpsimd")
        p.stt(f2(kT), f2(t1), 1.0, f2(kT), ALU.add, ALU.mult)
        p.tt(f2(t1), f2(rT), f2(kT), ALU.mult)
        p.tt(R(rkr(fv(rkr))), t1(fv(t1)), bc4(F64[:, 38:42]), ALU.mult, eng="gpsimd")
        ps = p.bank()
        p.mm(ps[0:64, 0:W4], KR["ones"][0:64, 0:64], f2(rkr), r=True)
        p.tt(f2(bonus), ps[0:64, 0:W4], f2(vT), ALU.mult)
        cs = t2
        for h in range(4):
            d0 = ones[0:64, 0:Tc] if kind == "P" else K("restartS", 0, 64)
            p.scan(cs.f(0, 64, h * Tc, (h + 1) * Tc), d0, lw.f(0, 64, h * Tc, (h + 1) * Tc), 0.0, ALU.mult, ALU.add)
        p.act(f2(epos), f2(cs), AF.Exp)
        eneg = scr.alloc(W4)
        eprev = scr.alloc(W4)
        eend = scr.alloc(W4)
        p.act(f2(eneg), f2(cs), AF.Exp, scale=-1.0)
        p.tt(f2(eprev), f2(cs), f2(lw), ALU.subtract)
        p.act(f2(eprev), f2(eprev), AF.Exp)
        if kind == "P":
            cs3 = v3(cs.a(0, 64), 4)
            p.tt(eend(v3(eend.a(0, 64), 4)), cs(bc_last(cs3[:, :, Tc - 1], Tc)), cs(cs3), ALU.subtract)
        else:
            cs4 = v4(cs.a(0, 64), 4, 16)
            p.tt(eend(v4(eend.a(0, 64), 4, 16)), cs(bc_last(cs4[:, :, :, Lc - 1], Lc)), cs(cs4), ALU.subtract)
        p.act(f2(eend), f2(eend), AF.Exp)
        p.tt(f2(t1), f2(kkn), f2(aG), ALU.mult, eng="gpsimd")
        p.stt(R(f2(at)), f2(kkn), -1.0, f2(eprev), ALU.mult, ALU.mult)
        p.tt(R(f2(rt)), f2(rT), f2(epos), ALU.mult, eng="gpsimd")
        p.tt(R(f2(bt)), f2(t1), f2(eneg), ALU.mult)
        p.tt(R(f2(kt)), f2(kT), f2(eneg), ALU.mult, eng="gpsimd")
        p.tt(f2(kb), f2(kT), f2(eend), ALU.mult)
        p.tt(f2(bb), f2(t1), f2(eend), ALU.mult, eng="gpsimd")
        dbgout("rw_at_%d_%s_%d" % (l, T["tag"], ci), f2(at), [64, W4])
        dbgout("rw_kt_%d_%s_%d" % (l, T["tag"], ci), f2(kt), [64, W4])
        scr.release(m1)
        scrR.release(mR1)
        Vtok = scrR.alloc(256)
        Kbtok = scrR.alloc(256)
        Bbtok = scrR.alloc(256)
        for src_, dst_ in [(vT, Vtok), (kb, Kbtok), (bb, Bbtok)]:
            ps = p.bank()
            for h in range(4):
                p.tr(ps[0:Tc, h * 64:(h + 1) * 64], src_.f(0, 64, h * Tc, (h + 1) * Tc), ident[0:64, 0:64])
            p.act(R(dst_.f(0, Tc)), ps[0:Tc, 0:256], AF.Copy)

        def amat(lh, rh, mask, dst, rnd=True):
            ps = p.bank()
            for h in range(4):
                p.mm(ps[0:Tc, h * Tc:(h + 1) * Tc], lh.f(0, 64, h * Tc, (h + 1) * Tc), rh.f(0, 64, h * Tc, (h + 1) * Tc), r=True)
            dv = dst(v3(dst.a(0, Tc, 0, W4), 4))
            p.tt(R(dv) if rnd else dv, v3(ps[0:Tc, 0:W4], 4), bc_mid(mask[0:Tc, 0:Tc], 1, 4), ALU.mult)

        A0, B0 = [scr.alloc(W4) for _ in range(2)]
        AakT, ArkT, ArbT = [scrR.alloc(W4) for _ in range(3)]
        amat(bt, at, strict, A0, rnd=False)
        amat(at, bt, strictT, B0, rnd=False)
        amat(kt, at, strict, AakT)
        amat(kt, rt, incl, ArkT)
        amat(bt, rt, incl, ArbT)
        Zs = [scr.alloc(W4), scr.alloc(W4)]
        As = [A0, scr.alloc(W4)]
        Bs = [B0, scr.alloc(W4)]
        p.tt(Zs[0](v3(Zs[0].a(0, Tc, 0, W4), 4)), A0(v3(A0.a(0, Tc, 0, W4), 4)), bc_mid(ident[0:Tc, 0:Tc], 1, 4), ALU.add, eng="gpsimd")
        nst = 7 if kind == "P" else 2
        zi = 0
        for i in range(1, nst):
            Ao, Bo, An, Bn = As[(i - 1) % 2], Bs[(i - 1) % 2], As[i % 2], Bs[i % 2]
            psB = p.bank()
            for h in range(4):
                p.mm(psB[0:Tc, h * Tc:(h + 1) * Tc], Ao.f(0, Tc, h * Tc, (h + 1) * Tc), Bo.f(0, Tc, h * Tc, (h + 1) * Tc))
            if i < nst - 1:
                psA = p.bank()
                for h in range(4):
                    p.mm(psA[0:Tc, h * Tc:(h + 1) * Tc], Bo.f(0, Tc, h * Tc, (h + 1) * Tc), Ao.f(0, Tc, h * Tc, (h + 1) * Tc))
            p.copy(Bn.f(0, Tc, 0, W4), psB[0:Tc, 0:W4])
            if i < nst - 1:
                p.act(An.f(0, Tc, 0, W4), psA[0:Tc, 0:W4], AF.Copy)
            psZ = p.bank()
            for h in range(4):
                p.mm(psZ[0:Tc, h * Tc:(h + 1) * Tc], Bn.f(0, Tc, h * Tc, (h + 1) * Tc), Zs[zi].f(0, Tc, h * Tc, (h + 1) * Tc))
            p.tt(Zs[1 - zi].f(0, Tc, 0, W4), Zs[zi].f(0, Tc, 0, W4), psZ[0:Tc, 0:W4], ALU.add)
            zi = 1 - zi
        Z = Zs[zi]
        if kind == "P":
            Hs = None

            def Hap(b, h):
                return rwH_P[:, h * 64:(h + 1) * 64]
            atm, rtm = "at", "rt"

            def lq(which, b, h):
                return (at if which == "at" else rt).f(0, 64, h * Tc, (h + 1) * Tc)
        else:
            Hs = scrR.alloc(16 * 256)
            nat = scr.alloc(16 * 256)
            p.dma(nat(v3(nat.a(0, 64), 64)), st_rwkv[l].rearrange("b (h v) k -> v (b h) k", h=4), q="gpsimd")
            for g8 in range(8):
                ps = p.bank()
                for j in range(8):
                    bh = g8 * 8 + j
                    p.tr(ps[0:64, j * 64:(j + 1) * 64], nat.f(0, 64, bh * 64, (bh + 1) * 64), ident[0:64, 0:64])
                if g8 % 2:
                    p.act(R(Hs.f(0, 64, g8 * 512, (g8 + 1) * 512)), ps[0:64, :], AF.Copy)
                else:
                    p.copy(R(Hs.f(0, 64, g8 * 512, (g8 + 1) * 512)), ps[0:64, :])

            def Hap(b, h):
                return Hs.f(0, 64, (b * 4 + h) * 64, (b * 4 + h + 1) * 64)
            mbuf = [scrR.alloc(16 * Tc), scrR.alloc(16 * Tc)]
            atm, rtm = "at", "rt"
            indv3 = v3(K("indBC", 0, 64), 16)

            def lq(which, b, h):
                return mbuf[h % 2].f(0, 64, b * Tc, (b + 1) * Tc)

        def prep_q(which, h):
            if kind == "P":
                return
            src_ = at if which == "at" else rt
            p.tt(R(mbuf[h % 2](v3(mbuf[h % 2].a(0, 64), 16))), src_(bc_mid(src_.a(0, 64, h * Tc, (h + 1) * Tc), 1, 16)), indv3, ALU.mult)

        psY = p.bank()
        for h in range(4):
            prep_q("at", h)
            for b in range(nb):
                p.mm(psY[0:Tc, h * 64:(h + 1) * 64], lq(atm, b, h), Hap(b, h), start=(b == 0), stop=False, r=True)
            p.mm(psY[0:Tc, h * 64:(h + 1) * 64], AakT.f(0, Tc, h * Tc, (h + 1) * Tc), Vtok.f(0, Tc, h * 64, (h + 1) * 64),
                 start=False, stop=True, r=True)
        Ysb = scrR.alloc(256)
        Usb = scrR.alloc(256)
        p.act(R(Ysb.f(0, Tc)), psY[0:Tc, 0:256], AF.Copy)
        psU = p.bank()
        for h in range(4):
            p.mm(psU[0:Tc, h * 64:(h + 1) * 64], Z.f(0, Tc, h * Tc, (h + 1) * Tc), Ysb.f(0, Tc, h * 64, (h + 1) * 64))
        p.copy(R(Usb.f(0, Tc)), psU[0:Tc, 0:256])
        psO = p.bank()
        for h in range(4):
            prep_q("rt", h)
            for b in range(nb):
                p.mm(psO[0:Tc, h * 64:(h + 1) * 64], lq(rtm, b, h), Hap(b, h), start=(b == 0), stop=False, r=True)
            p.mm(psO[0:Tc, h * 64:(h + 1) * 64], ArbT.f(0, Tc, h * Tc, (h + 1) * Tc), Usb.f(0, Tc, h * 64, (h + 1) * 64),
                 start=False, stop=False, r=True)
            p.mm(psO[0:Tc, h * 64:(h + 1) * 64], ArkT.f(0, Tc, h * Tc, (h + 1) * Tc), Vtok.f(0, Tc, h * 64, (h + 1) * 64),
                 start=False, stop=True, r=True)
        mu = scr.alloc(4)
        var = scr.alloc(4)
        dd = scr.alloc(256)
        d2 = scr.alloc(256)
        dd3 = v3(dd.a(0, Tc), 4)
        p.reduce(mu.f(0, Tc), v3(psO[0:Tc, 0:256], 4))
        p.ts(mu.f(0, Tc), mu.f(0, Tc), 1.0 / 64, ALU.mult)
        p.tt(dd(dd3), v3(psO[0:Tc, 0:256], 4), mu(bc_last(mu.a(0, Tc), 64)), ALU.subtract)
        p.tt(d2.f(0, Tc), dd.f(0, Tc), dd.f(0, Tc), ALU.mult, eng="gpsimd")
        p.reduce(var.f(0, Tc), d2(v3(d2.a(0, Tc), 4)))
        p.rsqrt(var.f(0, Tc), var.f(0, Tc), scale=1.0 / 64, bias=RWKV_LN_EPS)
        p.tt(dd(dd3), dd(dd3), var(bc_last(var.a(0, Tc), 64)), ALU.mult)
        psT = p.bank()
        for h in range(4):
            p.tr(psT[0:64, h * Tc:(h + 1) * Tc], dd.f(0, Tc, h * 64, (h + 1) * 64), ident[0:Tc, 0:Tc])
        onT = scr.alloc(W4)
        for h in range(4):
            p.ts(onT.f(0, 64, h * Tc, (h + 1) * Tc), psT[0:64, h * Tc:(h + 1) * Tc], F64[:, 30 + h:31 + h], ALU.mult,
                 F64[:, 34 + h:35 + h], ALU.add)
        p.tt(f2(onT), f2(onT), f2(bonus), ALU.add, eng="gpsimd")
        p.tt(mixR(mixRv[:, :, c0:c0 + Tc]), onT(v3(onT.a(0, 64), 4)), gT(v3(gT.a(0, 64), 4)), ALU.mult)
        if kind == "P":
            psH = p.bank()
            for h in range(4):
                p.mm(psH[0:64, h * 64:(h + 1) * 64], Kbtok.f(0, Tc, h * 64, (h + 1) * 64), Vtok.f(0, Tc, h * 64, (h + 1) * 64),
                     start=True, stop=False, r=True)
                p.mm(psH[0:64, h * 64:(h + 1) * 64], Bbtok.f(0, Tc, h * 64, (h + 1) * 64), Usb.f(0, Tc, h * 64, (h + 1) * 64),
                     start=False, stop=True, r=True)
            tmp = scr.alloc(256)
            ep3 = v3(epos.a(0, 64), 4)
            p.tt(tmp(v3(tmp.a(0, 64), 4)), v3(rwH_P[:], 4), epos(bc_last(ep3[:, :, Tc - 1], 64)), ALU.mult, eng="gpsimd")
            p.tt(R(rwH_P[:]), tmp.f(0, 64), psH[0:64, 0:256], ALU.add)
            if last:
                ps = p.bank()
                for h in range(4):
                    p.tr(ps[0:64, h * 64:(h + 1) * 64], rwH_P[:, h * 64:(h + 1) * 64], ident[0:64, 0:64])
                orow = scr.alloc(256)
                p.copy(orow.f(0, 64), ps[0:64, 0:256])
                p.dma(o_rwkv["P"][l, 0].rearrange("(h v) k -> v h k", h=4), orow(v3(orow.a(0, 64), 4)), q="gpsimd")
        else:
            ep4 = v4(epos.a(0, 64), 4, 16)
            Hs4 = v4(Hs.a(0, 64), 16, 4)
            for h in range(4):
                mm_ = scr.mark()
                mmr_ = scrR.mark()
                Km, Bm = mbuf[0], mbuf[1]
                indr = bc_last(K("indS", 0, Tc), 64)
                p.tt(R(Km(v3(Km.a(0, Tc), 16))), Kbtok(bc_mid(Kbtok.a(0, Tc, h * 64, (h + 1) * 64), 1, 16)), indr, ALU.mult)
                p.tt(R(Bm(v3(Bm.a(0, Tc), 16))), Bbtok(bc_mid(Bbtok.a(0, Tc, h * 64, (h + 1) * 64), 1, 16)), indr, ALU.mult, eng="gpsimd")
                tmp = scr.alloc(1024)
                p.tt(tmp(v3(tmp.a(0, 64), 16)), Hs(Hs4[:, :, h, :]), epos(bc_last(ep4[:, h, :, Lc - 1], 64)), ALU.mult, eng="gpsimd")
                for half in range(2):
                    psH = p.bank()
                    for j in range(8):
                        b = half * 8 + j
                        p.mm(psH[0:64, j * 64:(j + 1) * 64], Km.f(0, Tc, b * 64, (b + 1) * 64), Vtok.f(0, Tc, h * 64, (h + 1) * 64),
                             start=True, stop=False, r=True)
                        p.mm(psH[0:64, j * 64:(j + 1) * 64], Bm.f(0, Tc, b * 64, (b + 1) * 64), Usb.f(0, Tc, h * 64, (h + 1) * 64),
                             start=False, stop=True, r=True)
                    p.tt(R(Hs(Hs4[:, half * 8:(half + 1) * 8, h, :])), tmp(v3(tmp.a(0, 64, half * 512, (half + 1) * 512), 8)),
                         v3(psH[0:64, :], 8), ALU.add)
                scr.release(mm_)
                scrR.release(mmr_)
            for g8 in range(8):
                ps = p.bank()
                for j in range(8):
                    bh = g8 * 8 + j
                    p.tr(ps[0:64, j * 64:(j + 1) * 64], Hs.f(0, 64, bh * 64, (bh + 1) * 64), ident[0:64, 0:64])
                p.copy(nat.f(0, 64, g8 * 512, (g8 + 1) * 512), ps[0:64, :], eng=("scalar" if g8 % 2 else "vector"))
            p.dma(o_rwkv["S"][l].rearrange("b (h v) k -> v (b h) k", h=4), nat(v3(nat.a(0, 64), 64)), q="gpsimd")
        scr.release(m0)
        scrR.release(mR0)

    s5hS = [None, None]

    def s5_chunk(l, T, ci, c0, Tc, C):
        kind, L = T["kind"], T["L"]
        nb = 1 if kind == "P" else 16
        Lc = Tc // nb
        last = T["last"] and ci == len(T["chunks"]) - 1
        us5, usv, mixA, mixv, incl = C["us5"], C["usv"], C["mixA"], C["mixv"], C["incl"]
        inclr = KR["incl" + ("P" if kind == "P" else "S")]
        m0 = scr.mark()
        mR0 = scrR.mark()
        W8 = 8 * Tc
        A_, B_ = [scrR.alloc(1024) for _ in range(2)]
        C_, D_, E_, F_ = [scr.alloc(1024) for _ in range(4)]
        if kind == "S":
            hin = [scr.alloc(128), scr.alloc(128)]
            for ri, st in enumerate([st_s5re, st_s5im]):
                row = scr.alloc(1024)
                p.dma(row.f(0, 16), st[l], q="gpsimd")
                ps = p.bank()
                for j in range(8):
                    p.tr(ps[:, j * 16:(j + 1) * 16], row.f(0, 16, j * 128, (j + 1) * 128), ident[0:16, 0:16])
                p.copy(hin[ri].f(), ps[:, 0:128])
            hinv = [hin[ri](v3(hin[ri].a(), 8)) for ri in range(2)]
        else:
            hinv = [s5h_P[0][:], s5h_P[1][:]]
        psr = [p.bank(), p.bank()]
        psi = [p.bank(), p.bank()]
        for h in range(2):
            p.mm(psr[h][0:Tc, :], us5(usv[:, h, c0:c0 + Tc]), BBt[0][:, h * 512:(h + 1) * 512], r=True)
            p.mm(psi[h][0:Tc, :], us5(usv[:, h, c0:c0 + Tc]), BBt[1][:, h * 512:(h + 1) * 512], r=True)
        for h in range(2):
            cs_ = slice(h * 512, (h + 1) * 512)
            qre, qim = Qtab[0][0:Tc, cs_], Qtab[1][0:Tc, cs_]
            a_, b_ = A_.f(0, Tc, h * 512, (h + 1) * 512), B_.f(0, Tc, h * 512, (h + 1) * 512)
            c_, d_ = C_.f(0, Tc, h * 512, (h + 1) * 512), D_.f(0, Tc, h * 512, (h + 1) * 512)
            e_, f_ = E_.f(0, Tc, h * 512, (h + 1) * 512), F_.f(0, Tc, h * 512, (h + 1) * 512)
            p.tt(c_, qre, psr[h][0:Tc, :], ALU.mult)
            p.tt(d_, qim, psi[h][0:Tc, :], ALU.mult)
            p.tt(R(a_), c_, d_, ALU.subtract, eng="gpsimd")
            p.tt(e_, qre, psi[h][0:Tc, :], ALU.mult)
            p.tt(f_, qim, psr[h][0:Tc, :], ALU.mult)
            p.tt(R(b_), e_, f_, ALU.add, eng="gpsimd")
        nbank = (W8 + 511) // 512
        jpb = 512 // Tc
        psG = [[p.bank() for _ in range(nbank)] for _ in range(2)]
        for ri, Wb in enumerate([A_, B_]):
            for j in range(8):
                col = j * Tc
                p.mm(psG[ri][col // 512][:, col % 512:col % 512 + Tc], Wb.f(0, Tc, j * 128, (j + 1) * 128), inclr[0:Tc, 0:Tc], r=True)

        def fview(ap2, nj):
            return v3(ap2, nj) if kind == "P" else v4(ap2, nj, 16)

        for ri, Gb in enumerate([C_, D_]):
            for bk in range(nbank):
                j0 = bk * jpb
                hslice = hinv[ri].ap[:, j0:j0 + jpb] if isinstance(hinv[ri], V) else hinv[ri][:, j0:j0 + jpb]
                hres = hinv[ri].res if isinstance(hinv[ri], V) else _norm(hinv[ri])[1]
                p.tt(Gb(fview(Gb.a(0, 128, bk * 512, bk * 512 + jpb * Tc), jpb)), fview(psG[ri][bk][:, 0:jpb * Tc], jpb),
                     V(bc_last(hslice, Lc), hres), ALU.add)
        P3 = [v3(Ptab[ri][:], 8) for ri in range(2)]
        if kind == "P":
            pw = [P3[ri][:, :, 0:Tc] for ri in range(2)]
        else:
            pw = [bc_mid(P3[ri][:, :, 0:Lc], 2, 16) for ri in range(2)]
        fa = lambda buf: buf(fview(buf.a(0, 128, 0, W8), 8))
        p.tt(fa(E_), fa(C_), pw[0], ALU.mult)
        p.tt(fa(F_), fa(D_), pw[1], ALU.mult, eng="gpsimd")
        p.tt(R(fa(A_)), fa(E_), fa(F_), ALU.subtract)
        p.tt(fa(E_), fa(C_), pw[1], ALU.mult, eng="gpsimd")
        p.tt(fa(F_), fa(D_), pw[0], ALU.mult)
        p.tt(R(fa(B_)), fa(E_), fa(F_), ALU.add, eng="gpsimd")
        hb = [A_, B_]
        for ri in range(2):
            hv_ = fview(hb[ri].a(0, 128, 0, W8), 8)
            if kind == "P":
                p.copy(s5h_P[ri][:], hb[ri](hv_[:, :, Tc - 1]))
            else:
                p.copy(hinv[ri], hb[ri](hv_[:, :, :, Lc - 1]))
        yg = scrR.alloc(2 * Tc)
        ygv = v3(yg.a(), 2)
        yd = scr.alloc(2 * Tc)
        ydv = v3(yd.a(), 2)
        for h in range(2):
            ps = p.bank()
            for jj in range(4):
                j = 4 * h + jj
                p.mm(ps[:, 0:Tc], Cbd[0][:, j * 128:(j + 1) * 128], A_.f(0, 128, j * Tc, (j + 1) * Tc), start=(jj == 0), stop=False, r=True)
                p.mm(ps[:, 0:Tc], Cbd[1][:, j * 128:(j + 1) * 128], B_.f(0, 128, j * Tc, (j + 1) * Tc), start=False, stop=(jj == 3), r=True)
            p.stt(yd(ydv[:, h, :]), us5(usv[:, h, c0:c0 + Tc]), FC[:, 108 + h:109 + h], ps[:, 0:Tc], ALU.mult, ALU.add)
        p.act(R(yg.f()), yd.f(), AF.Gelu_apprx_tanh)
        gw = v3(gluw[:], 2)
        sg = scr.alloc(Tc)
        for h2 in range(2):
            ps = p.bank()
            for h in range(2):
                p.mm(ps[:, 0:Tc], gw[:, h, h2 * 128:(h2 + 1) * 128], yg(ygv[:, h, :]), start=(h == 0), stop=(h == 1), r=True)
            p.act(sg.f(), ps[:, 0:Tc], AF.Sigmoid, bias=FC[:, 110 + h2:111 + h2])
            p.tt(mixA(mixv[:, 4 + h2, c0:c0 + Tc]), yg(ygv[:, h2, :]), sg.f(), ALU.mult)
        if last:
            for ri, od in enumerate([o_s5re, o_s5im]):
                orow = scr.alloc(1024)
                if kind == "P":
                    ps = p.bank()
                    p.tr(ps[0:8, 0:128], s5h_P[ri][:], ident)
                    p.copy(orow.f(0, 8, 0, 128), ps[0:8, 0:128])
                    p.dma(od["P"][l].rearrange("a (j c) -> (a j) c", j=8), orow.f(0, 8, 0, 128), q="gpsimd")
                else:
                    for h2 in range(2):
                        ps = p.bank()
                        for jj in range(4):
                            j = 4 * h2 + jj
                            p.tr(ps[0:16, jj * 128:(jj + 1) * 128], hinv[ri].ap[:, j, :] if True else None, ident)
                        p.copy(orow.f(0, 16, h2 * 512, (h2 + 1) * 512), ps[0:16, :])
                    p.dma(od["S"][l], orow.f(0, 16), q="gpsimd")
        scr.release(m0)
        scrR.release(mR0)

    ptiles = [dict(kind="P", tag=t, TT=256, nseq=1, L=256, chunks=[(0, 128), (128, 128)], tok0=256 * t,
                   last=(t == npt - 1)) for t in range(npt)]
    stile = dict(kind="S", tag="S", TT=64, nseq=16, L=4, chunks=[(0, 64)], tok0=SP, last=True)
    emit_casts(0)
    for l in range(depth):
        layer_consts(l)
        p.memset(ssdT_P[:], 0.0)
        p.copy(R(rwH_P[:]), K("zeros", 0, 64, 0, 256), eng="gpsimd")
        p.memset(s5h_P[0][:], 0.0)
        p.memset(s5h_P[1][:], 0.0, eng="gpsimd")
        p.memset(ccar[:], 0.0)
        p.memset(scar[:], 0.0, eng="gpsimd")
        p.memset(gcar[:], 0.0)
        for T in ptiles:
            tile_fwd(l, T)
            if l + 1 < depth:
                emit_casts(l + 1, n=(30 + npt - 1) // npt)
        if l + 1 < depth:
            emit_casts(l + 1)
        s5_sample_qtab()
        tile_fwd(l, stile)
    assert wstate["next"] == len(plan), (wstate["next"], len(plan))
    p.emit()
    info = dict(nops=len(p.ops), nwaits=p.nwaits, ecount=p.ecount, scr_peak=scr.peak, scrR_peak=scrR.peak)
    return nc, info, list(dbg_outs.keys())


_CACHE = {}


def _run(inputs, npt=8, depth=2, dbg=None, ncores=8):
    key = (npt, depth, tuple(dbg) if dbg else None)
    nc, info, dbgs = build(npt=npt, depth=depth, dbg=dbg)
    SP = 256 * npt
    f32 = lambda a: np.ascontiguousarray(np.asarray(a, dtype=np.float32))
    shared = {n: f32(inputs[n]) for n in IN_NAMES}
    shared["cblob"] = _BLOB
    in_maps = []
    for c in range(ncores):
        b0, b1 = 16 * c, 16 * (c + 1)
        m = dict(shared)
        m["xp"] = f32(inputs["x_prompt"][c, :SP])
        m["xs"] = f32(inputs["x_sample"][b0:b1].reshape(64, D))
        m["cc"] = f32(np.concatenate([inputs["c_prompt"][c:c + 1], inputs["c_sample"][b0:b1]], axis=0))
        m["st_ssd"] = f32(inputs["state_ssd"][:, b0:b1].reshape(2, 16, 512, 128))
        m["st_conv"] = f32(inputs["state_ssd_conv"][:, b0:b1].reshape(2, 48, 1024))
        m["st_rwkv"] = f32(inputs["state_rwkv"][:, b0:b1].reshape(2, 16, 256, 64))
        m["st_shift"] = f32(inputs["state_rwkv_shift"][:, b0:b1])
        m["st_s5re"] = f32(inputs["state_s5_re"][:, b0:b1].reshape(2, 16, 1024))
        m["st_s5im"] = f32(inputs["state_s5_im"][:, b0:b1].reshape(2, 16, 1024))
        in_maps.append(m)
    res = run_bass_kernel_spmd(nc, in_maps, core_ids=list(range(ncores)))
    return res.results, info, dbgs


def kernel(**inputs):
    R, info, _ = _run(inputs)
    cat = lambda k, ax: np.concatenate([np.asarray(r[k]) for r in R], axis=ax)
    y_prompt = np.stack([np.asarray(r["yp"]) for r in R], axis=0)
    y_sample = cat("ys", 0).reshape(128, 4, 1024)
    p_ssd = cat("p_ssd", 1).reshape(2, 8, 8, 64, 128)
    p_conv = np.stack([np.asarray(r["p_conv"]) for r in R], axis=1)
    p_rwkv = cat("p_rwkv", 1).reshape(2, 8, 4, 64, 64)
    p_shift = cat("p_shift", 1)
    p_re = cat("p_s5re", 1).reshape(2, 8, 16, 64)
    p_im = cat("p_s5im", 1).reshape(2, 8, 16, 64)
    s_ssd = cat("s_ssd", 1).reshape(2, 128, 8, 64, 128)
    s_conv = cat("s_conv", 1).reshape(2, 128, 3, 1024)
    s_rwkv = cat("s_rwkv", 1).reshape(2, 128, 4, 64, 64)
    s_shift = cat("s_shift", 1)
    s_re = cat("s_s5re", 1).reshape(2, 128, 16, 64)
    s_im = cat("s_s5im", 1).reshape(2, 128, 16, 64)
    outs = (y_prompt, y_sample, p_ssd, p_conv, p_rwkv, p_shift, p_re, p_im, s_ssd, s_conv, s_rwkv, s_shift, s_re, s_im)
    return tuple(np.ascontiguousarray(o, dtype=np.float32) for o in outs)
```
